# Optimizing a Trainium2 kernel written in Bass

```python
import math
import jax
import jax.numpy as jnp
from jax import lax
import numpy as np

D_MODEL = 2048
BATCH = 8
SEQ = 4096
DEPTH = 2

GRID_W = 64
CTX_LEN = 256
N_MIXERS = 2
N_CONV_LAYERS = (DEPTH + 1) // 2
N_ATTN_LAYERS = DEPTH // 2
N_SUB = 3
FFN_DIM = 5632
CONV_WIDTH = 31
DA_HEADS = 16
DA_HEAD_DIM = D_MODEL // DA_HEADS // 2
AXIS_DIM = DA_HEAD_DIM // 2
ROPE_THETA = 10000.0
Q_BLOCK = 128
NORM_EPS = 1e-6
SUBLN_EPS = 1e-5
LN_EPS = 1e-5

kernel_name = 'hybrid_conformer_diffattn_block'


def _rms_norm(x, g, eps=NORM_EPS):
    xf = x.astype(jnp.float32)
    y = xf * lax.rsqrt(jnp.mean(xf * xf, axis=-1, keepdims=True) + eps)
    return (y * g.astype(jnp.float32)).astype(x.dtype)


def _layer_norm(x, g, b, eps=LN_EPS):
    xf = x.astype(jnp.float32)
    mu = jnp.mean(xf, axis=-1, keepdims=True)
    var = jnp.mean(jnp.square(xf - mu), axis=-1, keepdims=True)
    y = (xf - mu) * lax.rsqrt(var + eps)
    return (y * g.astype(jnp.float32) + b.astype(jnp.float32)).astype(x.dtype)


def _pre(x, g, mod, j):
    return _rms_norm(x, g) * (1 + mod[3 * j + 1]) + mod[3 * j]


def _post(x, y, g, mod, j, weight):
    return x + weight * mod[3 * j + 2] * _rms_norm(y, g)


def _swiglu(h, w_in, w_out):
    a, u = jnp.split(h @ w_in, 2, axis=-1)
    return (jax.nn.silu(a) * u) @ w_out


def _half_ffn(x, mod, j, g_pre, g_post, w_in, w_out):
    return _post(x, _swiglu(_pre(x, g_pre, mod, j), w_in, w_out), g_post, mod, j, 0.5)


def _conformer_conv(h, w_pw1, b_pw1, w_dw, b_dw, ln_g, ln_b, w_pw2, b_pw2):
    a, g = jnp.split(h @ w_pw1 + b_pw1, 2, axis=-1)
    u = a * jax.nn.sigmoid(g)
    pad = CONV_WIDTH // 2
    u = lax.conv_general_dilated(u, w_dw[:, None, :].astype(u.dtype), window_strides=(1,),
                                 padding=[(pad, pad)], dimension_numbers=('NWC', 'WIO', 'NWC'),
                                 feature_group_count=u.shape[-1]) + b_dw
    u = jax.nn.silu(_layer_norm(u, ln_g, ln_b))
    return u @ w_pw2 + b_pw2


def _axial_rope_tables(n_tokens, dtype):
    t = jnp.arange(n_tokens, dtype=jnp.int32)
    inv_freq = ROPE_THETA ** (-jnp.arange(0, AXIS_DIM, 2, dtype=jnp.float32) / AXIS_DIM)
    tabs = []
    for pos in (t // GRID_W, t % GRID_W):
        ang = (pos.astype(jnp.float32)[:, None] * inv_freq)[None, :, None, None, :]
        tabs += [jnp.cos(ang).astype(dtype), jnp.sin(ang).astype(dtype)]
    return tabs


def _rotate_half(x, cos, sin):
    x1, x2 = jnp.split(x, 2, axis=-1)
    return jnp.concatenate([x1 * cos - x2 * sin, x1 * sin + x2 * cos], axis=-1)


def _axial_rope(x, tabs):
    cos_r, sin_r, cos_c, sin_c = tabs
    return jnp.concatenate([_rotate_half(x[..., :AXIS_DIM], cos_r, sin_r),
                            _rotate_half(x[..., AXIS_DIM:], cos_c, sin_c)], axis=-1)


def _diff_weights(s, lam):
    p = jax.nn.softmax(s.astype(jnp.float32), axis=-1)
    return p[:, :, 0] - lam * p[:, :, 1]


def _diff_attn_latent(q, k_all, v_all, lam):
    b, n, h, _, dh = q.shape
    qb = jnp.moveaxis(q.reshape(b, n // Q_BLOCK, Q_BLOCK, h, 2, dh), 1, 0)

    def block(q_blk):
        s = jnp.einsum('bqhcd,bkhcd->bhcqk', q_blk, k_all)
        a = _diff_weights(s, lam).astype(v_all.dtype)
        return jnp.einsum('bhqk,bkhe->bqhe', a, v_all)

    o = lax.map(block, qb)
    return jnp.moveaxis(o, 0, 1).reshape(b, n, h, v_all.shape[-1])


def _diff_head_out(o, subln_g, lam_init, w_o):
    b, n = o.shape[:2]
    o = _rms_norm(o, subln_g, SUBLN_EPS) * (1.0 - lam_init)
    return o.reshape(b, n, -1) @ w_o


def _diff_attention(hl, hc, w_qkv, lq1, lk1, lq2, lk2, subln_g, w_o, lam_init, rope, with_ctx):
    b, n, d = hl.shape
    m = hc.shape[1]
    h, dh = DA_HEADS, DA_HEAD_DIM
    scale = dh ** -0.5
    ql, kl, vl = jnp.split(hl @ w_qkv, 3, axis=-1)
    ql = _axial_rope(ql.reshape(b, n, h, 2, dh), rope) * scale
    kl = _axial_rope(kl.reshape(b, n, h, 2, dh), rope)
    vl = vl.reshape(b, n, h, 2 * dh)
    if with_ctx:
        qc, kc, vc = jnp.split(hc @ w_qkv, 3, axis=-1)
    else:
        kc, vc = jnp.split(hc @ w_qkv[:, d:], 2, axis=-1)
    kc = kc.reshape(b, m, h, 2, dh)
    vc = vc.reshape(b, m, h, 2 * dh)
    lam = (jnp.exp(jnp.sum(lq1.astype(jnp.float32) * lk1.astype(jnp.float32)))
           - jnp.exp(jnp.sum(lq2.astype(jnp.float32) * lk2.astype(jnp.float32))) + lam_init)
    k_all = jnp.concatenate([kl, kc], axis=1)
    v_all = jnp.concatenate([vl, vc], axis=1)
    yl = _diff_head_out(_diff_attn_latent(ql, k_all, v_all, lam), subln_g, lam_init, w_o)
    if not with_ctx:
        return yl, None
    qc = qc.reshape(b, m, h, 2, dh) * scale
    s = jnp.einsum('bqhcd,bkhcd->bhcqk', qc, kc)
    oc = jnp.einsum('bhqk,bkhe->bqhe', _diff_weights(s, lam).astype(vc.dtype), vc)
    return yl, _diff_head_out(oc, subln_g, lam_init, w_o)


def setup_inputs(seed: int = 0) -> dict:
    key = jax.random.key(seed)
    ks = iter(jax.random.split(key, 32))

    def nrm(shape, scale):
        return scale * jax.random.normal(next(ks), shape, dtype=jnp.float32)

    d, f, h2 = D_MODEL, FFN_DIM, 2 * DA_HEAD_DIM
    nc, na = N_CONV_LAYERS, N_ATTN_LAYERS
    return {
        'x': nrm((BATCH, SEQ, d), 1.0),
        'c': nrm((BATCH, d), 1.0),
        'ctx': nrm((BATCH, CTX_LEN, d), 1.0),
        'c_ctx': nrm((d,), 1.0),
        'ada_w': nrm((DEPTH, d, 3 * N_SUB * d), 0.5 * d ** -0.5),
        'ada_b': nrm((DEPTH, 3 * N_SUB * d), 0.01),
        'norm_pre': 1.0 + nrm((DEPTH, N_SUB, d), 0.05),
        'norm_post': 1.0 + nrm((DEPTH, N_SUB, d), 0.05),
        'ffn1_w_in': nrm((DEPTH, d, 2 * f), d ** -0.5),
        'ffn1_w_out': nrm((DEPTH, f, d), f ** -0.5),
        'ffn2_w_in': nrm((DEPTH, d, 2 * f), d ** -0.5),
        'ffn2_w_out': nrm((DEPTH, f, d), f ** -0.5),
        'conv_w_pw1': nrm((nc, d, 2 * d), d ** -0.5),
        'conv_b_pw1': nrm((nc, 2 * d), 0.01),
        'conv_w_dw': nrm((nc, CONV_WIDTH, d), CONV_WIDTH ** -0.5),
        'conv_b_dw': nrm((nc, d), 0.01),
        'conv_ln_g': 1.0 + nrm((nc, d), 0.05),
        'conv_ln_b': nrm((nc, d), 0.01),
        'conv_w_pw2': nrm((nc, d, d), d ** -0.5),
        'conv_b_pw2': nrm((nc, d), 0.01),
        'attn_w_qkv': nrm((na, d, 3 * d), d ** -0.5),
        'attn_lambda_q1': nrm((na, DA_HEAD_DIM), 0.1),
        'attn_lambda_k1': nrm((na, DA_HEAD_DIM), 0.1),
        'attn_lambda_q2': nrm((na, DA_HEAD_DIM), 0.1),
        'attn_lambda_k2': nrm((na, DA_HEAD_DIM), 0.1),
        'attn_subln_g': 1.0 + nrm((na, h2), 0.05),
        'attn_w_o': nrm((na, d, d), d ** -0.5),
    }


def reference(x, c, ctx, c_ctx, ada_w, ada_b, norm_pre, norm_post, ffn1_w_in, ffn1_w_out, ffn2_w_in, ffn2_w_out,
              conv_w_pw1, conv_b_pw1, conv_w_dw, conv_b_dw, conv_ln_g, conv_ln_b, conv_w_pw2, conv_b_pw2,
              attn_w_qkv, attn_lambda_q1, attn_lambda_k1, attn_lambda_q2, attn_lambda_k2, attn_subln_g, attn_w_o):
    b, n, d = x.shape
    rope = _axial_rope_tables(n, x.dtype)
    xc = ctx
    for i in range(DEPTH):
        last = i == DEPTH - 1
        is_conv = i % N_MIXERS == 0
        li = i // N_MIXERS
        mod_l = (jax.nn.silu(c) @ ada_w[i] + ada_b[i]).reshape(b, 3 * N_SUB, d).transpose(1, 0, 2)[:, :, None, :]
        mod_c = (jax.nn.silu(c_ctx) @ ada_w[i] + ada_b[i]).reshape(3 * N_SUB, 1, 1, d)
        g_pre, g_post = norm_pre[i], norm_post[i]
        ctx_live = not (last and is_conv)

        x = _half_ffn(x, mod_l, 0, g_pre[0], g_post[0], ffn1_w_in[i], ffn1_w_out[i])
        if ctx_live:
            xc = _half_ffn(xc, mod_c, 0, g_pre[0], g_post[0], ffn1_w_in[i], ffn1_w_out[i])

        hl = _pre(x, g_pre[1], mod_l, 1)
        if is_conv:
            conv_p = (conv_w_pw1[li], conv_b_pw1[li], conv_w_dw[li], conv_b_dw[li],
                      conv_ln_g[li], conv_ln_b[li], conv_w_pw2[li], conv_b_pw2[li])
            x = _post(x, _conformer_conv(hl, *conv_p), g_post[1], mod_l, 1, 1.0)
            if not last:
                yc = _conformer_conv(_pre(xc, g_pre[1], mod_c, 1), *conv_p)
                xc = _post(xc, yc, g_post[1], mod_c, 1, 1.0)
        else:
            lam_init = 0.8 - 0.6 * math.exp(-0.3 * i)
            yl, yc = _diff_attention(hl, _pre(xc, g_pre[1], mod_c, 1), attn_w_qkv[li],
                                     attn_lambda_q1[li], attn_lambda_k1[li], attn_lambda_q2[li], attn_lambda_k2[li],
                                     attn_subln_g[li], attn_w_o[li], lam_init, rope, not last)
            x = _post(x, yl, g_post[1], mod_l, 1, 1.0)
            if not last:
                xc = _post(xc, yc, g_post[1], mod_c, 1, 1.0)

        x = _half_ffn(x, mod_l, 2, g_pre[2], g_post[2], ffn2_w_in[i], ffn2_w_out[i])
        if not last:
            xc = _half_ffn(xc, mod_c, 2, g_pre[2], g_post[2], ffn2_w_in[i], ffn2_w_out[i])
    return x
```

```python
import math
from contextlib import ExitStack

import numpy as np
import concourse.bass as bass
import concourse.mybir as mybir
from concourse.bass_utils import run_bass_kernel_spmd

F32 = mybir.dt.float32
BF16 = mybir.dt.bfloat16
AF = mybir.ActivationFunctionType
ALU = mybir.AluOpType

NCORES = 8
D = 2048
NL = 4096
NCTX = 256
NT = NL + NCTX
FF = 5632
KC = D // 128
FC = FF // 128
CW = 31
PAD = 15
NHEAD = 16
NORM_EPS = 1e-6
SUBLN_EPS = 1e-5
LN_EPS = 1e-5

V_CC = 0
V_ADAB = V_CC + 32
V_NPRE = V_ADAB + 288
V_NPOST = V_NPRE + 96
V_BPW1 = V_NPOST + 96
V_WDW = V_BPW1 + 32
V_BDW = V_WDW + 496
V_LNG = V_BDW + 16
V_LNB = V_LNG + 16
V_BPW2 = V_LNB + 16
V_LAM = V_BPW2 + 16
V_SUBLN = V_LAM + 256
NV = V_SUBLN + 1


class Res:
    __slots__ = ("name", "w", "rs", "excl")

    def __init__(self, name="", excl=False):
        self.name = name
        self.excl = excl
        self.w = None
        self.rs = {}


def mkres(n, name=""):
    return [Res(f"{name}{i}") for i in range(n)]


class Eng:
    def __init__(self, name, sem):
        self.name = name
        self.sem = sem
        self.cnt = 0
        self.known = {}
        self.ops = []
        self.ring = []
        self.ring_i = 0


class Prog:
    def __init__(self, nc, stack):
        self.nc = nc
        self.sems = {}
        self.E = {}
        for name in ("pe", "act", "dve", "pool", "sp"):
            s = stack.enter_context(nc.semaphore(f"prog_{name}"))
            self.sems[name] = s
            self.E[name] = Eng(name, s)
        for q, n in (("sp", 24), ("pool", 24), ("act", 8)):
            for i in range(n):
                key = f"dma_{q}_{i}"
                s = stack.enter_context(nc.semaphore(key))
                self.sems[key] = s
                self.E[q].ring.append([key, 0])

    def _need(self, eng, toks):
        for key, val in toks:
            if key == eng.name and eng.name == "pe":
                continue
            if eng.known.get(key, 0) < val:
                eng.known[key] = val
                eng.ops.append(("wait", key, val))

    def _deps(self, eng, reads, writes, same_eng_war=False):
        toks = []
        for r in reads:
            if r.w is not None:
                toks.append(r.w)
        for r in writes:
            if r.w is not None and r.w[0] != eng.name:
                toks.append(r.w)
            for k, v in r.rs.items():
                if k != eng.name:
                    toks.append((k, v))
        self._need(eng, toks)

    def _commit(self, tok, reads, writes):
        for r in reads:
            if r.rs.get(tok[0], 0) < tok[1]:
                r.rs[tok[0]] = tok[1]
        for r in writes:
            r.w = tok
            r.rs = {}

    def op(self, eng, fn, reads=(), writes=()):
        e = self.E[eng]
        if any(r.excl for r in reads):
            writes = list(writes) + [r for r in reads if r.excl]
        self._deps(e, reads, writes)
        e.cnt += 1
        tok = (eng, e.cnt)
        e.ops.append(("op", fn, eng, 1))
        self._commit(tok, reads, writes)

    def dma(self, q, fn, reads=(), writes=()):
        e = self.E[q]
        toks = []
        for r in reads:
            if r.w is not None:
                toks.append(r.w)
        for r in writes:
            if r.w is not None:
                toks.append(r.w)
            toks.extend(r.rs.items())
        slot = e.ring[e.ring_i]
        e.ring_i = (e.ring_i + 1) % len(e.ring)
        if slot[1] > 0:
            toks.append((slot[0], slot[1]))
        self._need(e, toks)
        slot[1] += 16
        tok = (slot[0], slot[1])
        e.ops.append(("op", fn, slot[0], 16))
        self._commit(tok, reads, writes)

    def drain(self, q="sp"):
        e = self.E[q]
        toks = []
        for name, o in self.E.items():
            if o.cnt > 0 and name != q:
                toks.append((name, o.cnt))
            for key, val in o.ring:
                if val > 0:
                    toks.append((key, val))
        self._need(e, toks)

    def flush(self, name="blk"):
        nc = self.nc
        sems = self.sems
        with nc.Block() as block:
            def replay(handle, ops):
                for o in ops:
                    if o[0] == "wait":
                        handle.wait_ge(sems[o[1]], o[2])
                    else:
                        o[1](handle).then_inc(sems[o[2]], o[3])

            @block.tensor
            def _(h):
                replay(h, self.E["pe"].ops)

            @block.scalar
            def _(h):
                replay(h, self.E["act"].ops)

            @block.vector
            def _(h):
                replay(h, self.E["dve"].ops)

            @block.gpsimd
            def _(h):
                replay(h, self.E["pool"].ops)

            @block.sync
            def _(h):
                replay(h, self.E["sp"].ops)
        for e in self.E.values():
            e.ops = []


class Builder:
    def __init__(self, phases=None, dbg=None):
        self.phases = phases
        self.dbg = dbg
        self.nc = bass.Bass("TRN2", target_bir_lowering=False)
        self.stack = ExitStack()
        self.in_names = []

    def din(self, name, shape, dt=F32):
        self.in_names.append(name)
        return self.nc.dram_tensor(name, list(shape), dt, kind="ExternalInput").ap()

    def dint(self, name, shape, dt=F32):
        return self.nc.dram_tensor(name, list(shape), dt, kind="Internal").ap()

    def sb(self, st, name, shape, dt=F32):
        self._uid = getattr(self, "_uid", 0) + 1
        return st.enter_context(self.nc.sbuf_tensor(f"sb{self._uid}_{name}", list(shape), dt))

    def build(self):
        nc = self.nc
        st = self.stack
        with st:
            self.P = Prog(nc, st)
            self._declare()
            self._globals(st)
            self._run_phases()
        return nc

    def _declare(self):
        self.x_in = self.din("x", [NL, D])
        self.ctx_in = self.din("ctx", [NCTX, D])
        self.vecs_in = self.din("vecs", [128, NV])
        self.ident_in = self.din("ident", [128, 128])
        self.adaw = self.din("adaw", [2 * 144 * 128, 2048])
        self.win = {}
        self.wout = {}
        for i in range(2):
            for w in (1, 2):
                self.win[(i, w)] = self.din(f"win{i}{w}", [FC * 128, 2 * KC * 128])
                self.wout[(i, w)] = self.din(f"wout{i}{w}", [KC * 128, FC * 128])
        self.pw1 = self.din("pw1", [KC * 128, 2 * KC * 128])
        self.pw2 = self.din("pw2", [KC * 128, KC * 128])
        self.wq = self.din("wq", [KC * 128, KC * 128])
        self.wk = self.din("wk", [KC * 128, KC * 128])
        self.perm_in = self.din("perm", [128, 128])
        self.wv = self.din("wv", [4 * 128, KC * 512])
        self.wo = self.din("wo", [KC * 128, KC * 128])
        self.rope = self.din("rope", [4 * 128, NL])
        self.out = self.nc.dram_tensor("out", [NL, D], F32, kind="ExternalOutput").ap()
        self.XA = self.dint("XA", [D, NT])
        self.XB = self.dint("XB", [D, NT])
        self.QT = self.dint("QT", [D, NL], BF16)
        self.KT = self.dint("KT", [D, NT], BF16)
        self.VV = self.dint("VV", [NT, D], BF16)
        self.ON = self.dint("ON", [D, NL], BF16)
        self.rXA = Res("XA")
        self.rXB = Res("XB")
        self.rQT = Res("QT")
        self.rKT = Res("KT")
        self.rVV = Res("VV")
        self.rON = Res("ON")
        if self.dbg is not None:
            self.dbg_out = self.nc.dram_tensor("dbg", [D, NT], F32, kind="ExternalOutput").ap()
            self.dbgv_out = self.nc.dram_tensor("dbgv", [128, 576], F32, kind="ExternalOutput").ap()

    def _globals(self, st):
        nc, P = self.nc, self.P
        self.ps = [st.enter_context(nc.psum_tensor(f"ps{i}", [128, 512], F32)) for i in range(8)]
        self.rps = [Res(f"ps{i}", excl=True) for i in range(8)]
        self.vecs = self.sb(st, "vecs", [128, NV])
        self.rvecs = Res("vecs")
        self.ident = self.sb(st, "ident", [128, 128])
        self.ones = self.sb(st, "ones", [128, 128])
        self.rconst = Res("const")
        self.modv = self.sb(st, "modv", [128, 2 * 2 * 3 * 3 * 16])
        self.rmodv = Res("modv")
        self.scb = self.sb(st, "scb", [128, 32], BF16)
        self.rscb = Res("scb")
        self.lam = self.sb(st, "lamv", [128, 8])
        self.rlam = Res("lam")
        vecs, ident, ones = self.vecs, self.ident, self.ones
        P.dma("sp", lambda e: e.dma_start(out=vecs[:], in_=self.vecs_in[:, :]), writes=[self.rvecs])
        P.dma("sp", lambda e: e.dma_start(out=ident[:], in_=self.ident_in[:, :]), writes=[self.rconst])
        P.op("dve", lambda e: e.memset(ones[:], 1.0), writes=[self.rconst])
        self.onesb = self.sb(st, "onesb", [128, 128], BF16)
        P.op("dve", lambda e: e.memset(self.onesb[:], 1.0), writes=[self.rconst])
        self.sqb = [self.sb(st, f"sqb{k}", [128, 512], BF16) for k in range(2)]
        self.rsqb = mkres(2, "sqb")
        self.setup_eps(st)

    def mv(self, i, s, j, kind, c):
        off = ((((i * 2 + s) * 3 + j) * 3 + kind) * 16) + c
        return self.modv[:, off:off + 1]

    def vcol(self, off, n=1):
        return self.vecs[:, off:off + n]

    def _run_phases(self):
        P = self.P
        ph = self.phases
        def want(name):
            return ph is None or name in ph
        if want("tin"):
            self.phase_transpose_in(self.XA, self.rXA)
            P.drain(); P.flush()
        self.mod1_in_conv = want("conv") and want("mod")
        if want("mod"):
            self.phase_mod((0,) if self.mod1_in_conv else (0, 1))
            P.drain(); P.flush()
        full = ph is None
        self.bg_conv = []
        self.bg_attnb = []
        bg_f20 = []
        if full:
            p = self.ffn_prep(0, 2); p["pre"] = True; self.bg_conv = p["jobs"]
            p = self.ffn_prep(1, 1); p["pre"] = True; bg_f20 = p["jobs"]
            p = self.ffn_prep(1, 2); p["pre"] = True; self.bg_attnb = p["jobs"]
        if want("ffn1_0"):
            self.phase_ffn(0, 1, 0, self.XA, self.rXA, self.XB, self.rXB, True)
            P.drain(); P.flush()
        if want("conv"):
            self.phase_conv(self.XB, self.rXB, self.XA, self.rXA)
            P.drain(); P.flush()
        if want("ffn2_0"):
            self.phase_ffn(0, 2, 2, self.XA, self.rXA, self.XB, self.rXB, True, bg=bg_f20)
            P.drain(); P.flush()
        if want("ffn1_1"):
            self.phase_ffn(1, 1, 0, self.XB, self.rXB, self.XA, self.rXA, True)
            P.drain(); P.flush()
        if want("attn"):
            self.phase_attn(self.XA, self.rXA, self.XB, self.rXB)
            P.drain(); P.flush()
        if want("ffn2_1"):
            self.phase_ffn(1, 2, 2, self.XB, self.rXB, self.XA, self.rXA, False)
            P.drain(); P.flush()
        if self.dbg is not None:
            src, rsrc = (self.XA, self.rXA) if self.dbg == "A" else (self.XB, self.rXB)
            rdbg = Res("dbg")
            P.dma("sp", lambda e: e.dma_start(out=self.dbgv_out[:, :], in_=self.modv[:]), reads=[self.rmodv], writes=[Res()])
            for c in range(KC):
                P.dma("sp", lambda e, c=c: e.dma_start(out=self.dbg_out[c * 128:(c + 1) * 128, :], in_=src[c * 128:(c + 1) * 128, :]), reads=[rsrc], writes=[rdbg])
        if want("tout"):
            self.phase_transpose_out(self.XA, self.rXA)
        P.drain(); P.flush()

    def phase_transpose_in(self, XO, rXO):
        nc, P = self.nc, self.P
        with ExitStack() as st:
            xin = [self.sb(st, f"ti_x{i}", [128, D]) for i in range(2)]
            rxin = mkres(2)
            stg = [self.sb(st, f"ti_s{i}", [128, KC, 512]) for i in range(2)]
            rstg = mkres(2)
            ident = self.ident
            XOv = XO.rearrange("(c p) t -> p c t", p=128)
            groups = [(self.x_in, g * 512, 4, g * 512) for g in range(8)] + [(self.ctx_in, 0, 2, NL)]
            blk = 0
            for gi, (src, r0, nb, c0) in enumerate(groups):
                sg, rsg = stg[gi % 2], rstg[gi % 2]
                for tb in range(nb):
                    xb, rxb = xin[blk % 2], rxin[blk % 2]
                    blk += 1
                    rr = r0 + tb * 128
                    P.dma("sp", lambda e, xb=xb, src=src, rr=rr: e.dma_start(out=xb[:], in_=src[rr:rr + 128, :]), writes=[rxb])
                    for b4 in range(4):
                        pst, rpst = self.ps[b4 + 4 * (tb % 2)], self.rps[b4 + 4 * (tb % 2)]
                        for cc in range(4):
                            c = b4 * 4 + cc
                            P.op("pe", lambda e, pst=pst, cc=cc, xb=xb, c=c: e.transpose(
                                out=pst[:, cc * 128:(cc + 1) * 128], in_=xb[:, c * 128:(c + 1) * 128], identity=ident[:]),
                                reads=[rxb, self.rconst], writes=[rpst])
                        dst = sg[:, b4 * 4:(b4 + 1) * 4, tb * 128:(tb + 1) * 128]
                        srcp = pst[:].rearrange("p (c t) -> p c t", c=4)
                        if b4 % 2 == 0:
                            P.op("act", lambda e, dst=dst, srcp=srcp: e.activation(out=dst, in_=srcp, func=AF.Copy), reads=[rpst], writes=[rsg])
                        else:
                            P.op("dve", lambda e, dst=dst, srcp=srcp: e.tensor_copy(out=dst, in_=srcp), reads=[rpst], writes=[rsg])
                n = nb * 128
                P.dma("sp", lambda e, sg=sg, c0=c0, n=n: e.dma_start(out=XOv[:, :, c0:c0 + n], in_=sg[:, :, 0:n]), reads=[rsg], writes=[rXO])

    def phase_transpose_out(self, XI, rXI):
        nc, P = self.nc, self.P
        with ExitStack() as st:
            xt = [self.sb(st, f"to_x{i}", [128, KC, 512]) for i in range(2)]
            rxt = mkres(2)
            ot = [self.sb(st, f"to_o{i}", [128, D]) for i in range(2)]
            rot = mkres(2)
            rout = Res("out")
            ident = self.ident
            XIv = XI.rearrange("(c p) t -> p c t", p=128)
            blk = 0
            for g in range(8):
                xg, rxg = xt[g % 2], rxt[g % 2]
                P.dma("sp", lambda e, xg=xg, g=g: e.dma_start(out=xg[:], in_=XIv[:, :, g * 512:(g + 1) * 512]), reads=[rXI], writes=[rxg])
                for tb in range(4):
                    o, ro = ot[blk % 2], rot[blk % 2]
                    blk += 1
                    for b4 in range(4):
                        pst, rpst = self.ps[b4 + 4 * (tb % 2)], self.rps[b4 + 4 * (tb % 2)]
                        for cc in range(4):
                            c = b4 * 4 + cc
                            P.op("pe", lambda e, pst=pst, cc=cc, xg=xg, c=c, tb=tb: e.transpose(
                                out=pst[:, cc * 128:(cc + 1) * 128], in_=xg[:, c, tb * 128:(tb + 1) * 128], identity=ident[:]),
                                reads=[rxg, self.rconst], writes=[rpst])
                        dst = o[:, b4 * 512:(b4 + 1) * 512]
                        if b4 % 2 == 0:
                            P.op("act", lambda e, dst=dst, pst=pst: e.activation(out=dst, in_=pst[:], func=AF.Copy), reads=[rpst], writes=[ro])
                        else:
                            P.op("dve", lambda e, dst=dst, pst=pst: e.tensor_copy(out=dst, in_=pst[:]), reads=[rpst], writes=[ro])
                    r0 = g * 512 + tb * 128
                    P.dma("sp", lambda e, o=o, r0=r0: e.dma_start(out=self.out[r0:r0 + 128, :], in_=o[:]), reads=[ro], writes=[rout])

    def mod_pieces(self, layer, wb, rwb, raw, rraw, banks):
        P = self.P
        OCB = 2
        vecs = self.vecs
        scb = self.scb
        adv = self.adaw.rearrange("(g o p) k -> g p o k", p=128, o=OCB)
        ng = 144 // OCB
        pieces = []
        for g in range(ng):
            def piece(g=g):
                w, rw = wb[g % len(wb)], rwb[g % len(wb)]
                gg = layer * ng + g
                for o in range(OCB):
                    P.dma("pool", lambda e, o=o: e.dma_start(out=w[:, o, :], in_=adv[gg][:, o, :], max_dma_last_dim=2048), writes=[rw[o]])
                pst, rpst = self.ps[banks[g % 2]], self.rps[banks[g % 2]]
                for o in range(OCB):
                    for kc in range(KC):
                        P.op("pe", lambda e, o=o, kc=kc: e.matmul(
                            out=pst[:, o * 2:o * 2 + 2], lhsT=w[:, o, kc * 128:(kc + 1) * 128], rhs=scb[:, kc * 2:kc * 2 + 2],
                            start=(kc == 0), stop=(kc == KC - 1)), reads=[rw[o], self.rscb], writes=[rpst])
                for o in range(OCB):
                    oc = g * OCB + o
                    ioc = layer * 144 + oc
                    P.op("dve", lambda e, o=o, oc=oc, ioc=ioc: e.tensor_scalar(
                        out=raw[:, oc * 2:oc * 2 + 2], in0=pst[:, o * 2:o * 2 + 2], scalar1=vecs[:, V_ADAB + ioc:V_ADAB + ioc + 1],
                        scalar2=None, op0=ALU.add), reads=[rpst, self.rvecs], writes=[rraw])
            pieces.append(piece)
        return pieces

    def mod_derive(self, i, raw, rraw):
        P = self.P
        vecs = self.vecs
        modv = self.modv
        for s in range(2):
            for j in range(3):
                wgt = 0.5 if j != 1 else 1.0

                def rawv(r):
                    b0 = (((3 * j + r) * 16) * 2) + s
                    return raw[:, b0:b0 + 31:2]
                offA = (((i * 2 + s) * 3 + j) * 3 + 0) * 16
                offB = offA + 16
                offG = offA + 32
                npre = vecs[:, V_NPRE + (i * 3 + j) * 16:V_NPRE + (i * 3 + j) * 16 + 16]
                npost = vecs[:, V_NPOST + (i * 3 + j) * 16:V_NPOST + (i * 3 + j) * 16 + 16]
                P.op("dve", lambda e, o=offA, a=rawv(1), b=npre: e.scalar_tensor_tensor(
                    out=modv[:, o:o + 16], in0=a, scalar=1.0, in1=b, op0=ALU.add, op1=ALU.mult),
                    reads=[rraw, self.rvecs], writes=[self.rmodv])
                P.op("dve", lambda e, o=offB, a=rawv(0): e.tensor_copy(out=modv[:, o:o + 16], in_=a),
                     reads=[rraw], writes=[self.rmodv])
                P.op("dve", lambda e, o=offG, a=rawv(2), b=npost, wgt=wgt: e.scalar_tensor_tensor(
                    out=modv[:, o:o + 16], in0=a, scalar=wgt, in1=b, op0=ALU.mult, op1=ALU.mult),
                    reads=[rraw, self.rvecs], writes=[self.rmodv])

    def phase_mod(self, layers=(0,)):
        nc, P = self.nc, self.P
        with ExitStack() as st:
            sc = self.sb(st, "mod_sc", [128, 32])
            rsc = Res("sc")
            vecs = self.vecs
            P.op("act", lambda e: e.activation(out=sc[:], in_=vecs[:, V_CC:V_CC + 32], func=AF.Silu), reads=[self.rvecs], writes=[rsc])
            P.op("dve", lambda e: e.tensor_copy(out=self.scb[:], in_=sc[:]), reads=[rsc], writes=[self.rscb])
            wb = [self.sb(st, f"mod_w{k}", [128, 2, 2048], BF16) for k in range(6)]
            rwb = [mkres(2) for _ in range(6)]
            for layer in layers:
                raw = self.sb(st, f"mod_raw{layer}", [128, 144 * 2])
                rraw = Res("raw")
                for piece in self.mod_pieces(layer, wb, rwb, raw, rraw, (0, 1)):
                    piece()
                self.mod_derive(layer, raw, rraw)

    def rstd_from_sumsq(self, pst, rpst, n, out, rout, tmp, rtmp, eps):
        P = self.P
        P.op("act", lambda e: e.activation(out=tmp[:, 0:n], in_=pst[:, 0:n], func=AF.Sqrt, bias=self.epsv(eps), scale=1.0 / D),
             reads=[rpst, self.rconst], writes=[rtmp])
        P.op("dve", lambda e: e.reciprocal(out=out[:, 0:n], in_=tmp[:, 0:n]), reads=[rtmp], writes=[rout])

    def _stat_mm(self, item, pst, rpst, n, last=KC - 1):
        tq, rtq, dc = item
        ones = self.onesb
        self.P.op("pe", lambda e: e.matmul(out=pst[:, 0:n], lhsT=ones[:], rhs=tq[:, 0:n], start=(dc == 0), stop=(dc == last)),
                  reads=[rtq, self.rconst], writes=[rpst])

    def epsv(self, eps):
        return self.epst[:, self.eps_idx[eps]:self.eps_idx[eps] + 1]

    def setup_eps(self, st):
        self.epst = self.sb(st, "epst", [128, 4])
        self.eps_idx = {}
        for k, v in enumerate(sorted({NORM_EPS, SUBLN_EPS, LN_EPS})):
            self.eps_idx[v] = k
            self.P.op("dve", lambda e, k=k, v=v: e.memset(self.epst[:, k:k + 1], v), writes=[self.rconst])

    def load_w_cast(self, dst, rdst, src_ap):
        self.P.dma("pool", lambda e: e.dma_start(out=dst, in_=src_ap, max_dma_last_dim=2048), writes=[rdst])

    def ffn_prep(self, i, w):
        if not hasattr(self, "_ffn_prep"):
            self._ffn_prep = {}
        if (i, w) in self._ffn_prep:
            return self._ffn_prep[(i, w)]
        P = self.P
        NQ, QF = 4, FC // 4
        win, wout = self.win[(i, w)], self.wout[(i, w)]
        winv = win.rearrange("(f p) k -> f p k", p=128)
        woutv = wout.rearrange("(d p) k -> d p k", p=128)
        WinS = self.dint(f"wins{i}{w}", [FC * 2 * 128, KC * 128], BF16).rearrange("(f h p) k -> f h p k", h=2, p=128)
        WoutS = self.dint(f"wouts{i}{w}", [KC * NQ * 128, QF * 128], BF16).rearrange("(d q p) k -> d q p k", q=NQ, p=128)
        rWinS = [[Res("wins") for _ in range(2)] for _ in range(FC)]
        rWoutS = [[Res("wouts") for _ in range(NQ)] for _ in range(KC)]
        jobs = []
        for fc in range(FC):
            for half in range(2):
                jobs.append(lambda fc=fc, half=half: P.dma("pool", lambda e: e.dma_start(
                    out=WinS[fc, half], in_=winv[fc][:, half * KC * 128:(half + 1) * KC * 128], max_dma_last_dim=2048), writes=[rWinS[fc][half]]))
        for dc in range(KC):
            for q in range(NQ):
                jobs.append(lambda dc=dc, q=q: P.dma("pool", lambda e: e.dma_start(
                    out=WoutS[dc, q], in_=woutv[dc][:, q * QF * 128:(q + 1) * QF * 128], max_dma_last_dim=2048), writes=[rWoutS[dc][q]]))
        d = {"WinS": WinS, "WoutS": WoutS, "rWinS": rWinS, "rWoutS": rWoutS, "jobs": jobs, "pre": False}
        self._ffn_prep[(i, w)] = d
        return d

    def run_jobs(self, jobs, n):
        for _ in range(n):
            if jobs:
                jobs.pop(0)()

    def phase_ffn(self, i, w, j, XI, rXI, XO, rXO, with_ctx, bg=None):
        nc, P = self.nc, self.P
        prep = self.ffn_prep(i, w)
        pre = prep["pre"]
        bg = bg if bg is not None else []
        win, wout = self.win[(i, w)], self.wout[(i, w)]
        winv = win.rearrange("(f p) k -> f p k", p=128)
        woutv = wout.rearrange("(d p) k -> d p k", p=128)
        XIv = XI.rearrange("(c p) t -> p c t", p=128)
        XOv = XO.rearrange("(c p) t -> p c t", p=128)
        tiles = [(g * 512, 512, 0) for g in range(8)] + ([(NL, 256, 1)] if with_ctx else [])
        NQ = 4
        QF = FC // NQ
        with ExitStack() as st:
            xts = [self.sb(st, f"f_x{k}", [128, KC, 512]) for k in range(2)]
            rxs = [mkres(KC, "x") for _ in range(2)]
            h = self.sb(st, "f_h", [128, KC, 512], BF16)
            rh = mkres(KC, "h")
            act = self.sb(st, "f_act", [128, FC, 512], BF16)
            ract = mkres(FC, "act")
            y = self.sb(st, "f_y", [128, KC, 512])
            ry = mkres(KC, "y")
            wi = [self.sb(st, f"f_wi{k}", [128, KC * 128], BF16) for k in range(4)]
            rwi = mkres(4, "wi")
            wo = [self.sb(st, f"f_wo{k}", [128, QF * 128], BF16) for k in range(4)]
            rwo = mkres(4, "wo")
            tmp = [self.sb(st, f"f_t{k}", [128, 512]) for k in range(4)]
            rtmp = mkres(4, "tmp")
            rstd = self.sb(st, "f_rstd", [128, 512])
            rrstd = Res("rstd")
            rstd2 = self.sb(st, "f_rstd2", [128, 512])
            rrstd2 = Res("rstd2")
            ones = self.ones
            cnt = {"wi": 0, "wo": 0}
            WinS, WoutS, rWinS, rWoutS = prep["WinS"], prep["WoutS"], prep["rWinS"], prep["rWoutS"]

            def load_x(k):
                t0, n, s = tiles[k]
                xt, rx = xts[k % 2], rxs[k % 2]
                P.dma("pool", lambda e: e.dma_start(out=xt[:, :, 0:n], in_=XIv[:, :, t0:t0 + n]), reads=[rXI], writes=rx)

            def prenorm(k):
                t0, n, s = tiles[k]
                xt, rx = xts[k % 2], rxs[k % 2]
                pst, rpst = self.ps[6], self.rps[6]
                for c in range(KC):
                    tq, rtq = self.sqb[c % 2], self.rsqb[c % 2]
                    P.op("act", lambda e, tq=tq, c=c: e.activation(out=tq[:, 0:n], in_=xt[:, c, 0:n], func=AF.Square), reads=[rx[c]], writes=[rtq])
                    P.op("pe", lambda e, tq=tq, c=c: e.matmul(out=pst[:, 0:n], lhsT=self.onesb[:], rhs=tq[:, 0:n], start=(c == 0), stop=(c == KC - 1)),
                         reads=[rtq, self.rconst], writes=[rpst])
                self.rstd_from_sumsq(pst, rpst, n, rstd, rrstd, tmp[2], rtmp[2], NORM_EPS)
                for c in range(KC):
                    tq, rtq = tmp[2 + c % 2], rtmp[2 + c % 2]
                    P.op("dve", lambda e, tq=tq, c=c: e.tensor_tensor(out=tq[:, 0:n], in0=xt[:, c, 0:n], in1=rstd[:, 0:n], op=ALU.mult),
                         reads=[rx[c], rrstd], writes=[rtq])
                    P.op("act", lambda e, tq=tq, c=c: e.activation(out=h[:, c, 0:n], in_=tq[:, 0:n], func=AF.Identity,
                                                                   scale=self.mv(i, s, j, 0, c), bias=self.mv(i, s, j, 1, c)),
                         reads=[rtq, self.rmodv], writes=[rh[c]])

            def instage(k):
                t0, n, s = tiles[k]
                for fc in range(FC):
                    pa, rpa = self.ps[fc % 2], self.rps[fc % 2]
                    pu, rpu = self.ps[2 + fc % 2], self.rps[2 + fc % 2]
                    for half, (pp, rpp) in enumerate(((pa, rpa), (pu, rpu))):
                        wt, rwt = wi[cnt["wi"] % 4], rwi[cnt["wi"] % 4]
                        cnt["wi"] += 1
                        if k == 0 and not pre:
                            self.load_w_cast(wt[:], rwt, winv[fc][:, half * KC * 128:(half + 1) * KC * 128])
                            if len(tiles) > 1:
                                P.dma("sp", lambda e, wt=wt, fc=fc, half=half: e.dma_start(out=WinS[fc, half], in_=wt[:]), reads=[rwt], writes=[rWinS[fc][half]])
                        else:
                            P.dma("sp", lambda e, wt=wt, fc=fc, half=half: e.dma_start(out=wt[:], in_=WinS[fc, half]), reads=[rWinS[fc][half]], writes=[rwt])
                        for kc in range(KC):
                            P.op("pe", lambda e, pp=pp, wt=wt, kc=kc: e.matmul(out=pp[:, 0:n], lhsT=wt[:, kc * 128:(kc + 1) * 128], rhs=h[:, kc, 0:n],
                                                                                start=(kc == 0), stop=(kc == KC - 1)), reads=[rwt, rh[kc]], writes=[rpp])
                    tq, rtq = tmp[fc % 2], rtmp[fc % 2]
                    P.op("act", lambda e, tq=tq, pa=pa: e.activation(out=tq[:, 0:n], in_=pa[:, 0:n], func=AF.Silu), reads=[rpa], writes=[rtq])
                    P.op("dve", lambda e, tq=tq, pu=pu, fc=fc: e.tensor_tensor(out=act[:, fc, 0:n], in0=pu[:, 0:n], in1=tq[:, 0:n], op=ALU.mult),
                         reads=[rpu, rtq], writes=[ract[fc]])

            def outstage(k):
                t0, n, s = tiles[k]
                pst2, rpst2 = self.ps[7], self.rps[7]
                pend = []
                for dc in range(KC):
                    py, rpy = self.ps[4 + dc % 2], self.rps[4 + dc % 2]
                    for q in range(NQ):
                        wt, rwt = wo[cnt["wo"] % 4], rwo[cnt["wo"] % 4]
                        cnt["wo"] += 1
                        if k == 0 and not pre:
                            self.load_w_cast(wt[:], rwt, woutv[dc][:, q * QF * 128:(q + 1) * QF * 128])
                            if len(tiles) > 1:
                                P.dma("sp", lambda e, wt=wt, dc=dc, q=q: e.dma_start(out=WoutS[dc, q], in_=wt[:]), reads=[rwt], writes=[rWoutS[dc][q]])
                        else:
                            P.dma("sp", lambda e, wt=wt, dc=dc, q=q: e.dma_start(out=wt[:], in_=WoutS[dc, q]), reads=[rWoutS[dc][q]], writes=[rwt])
                        for f in range(QF):
                            fc = q * QF + f
                            P.op("pe", lambda e, py=py, wt=wt, f=f, fc=fc: e.matmul(out=py[:, 0:n], lhsT=wt[:, f * 128:(f + 1) * 128], rhs=act[:, fc, 0:n],
                                                                                     start=(fc == 0), stop=(fc == FC - 1)), reads=[rwt, ract[fc]], writes=[rpy])
                    tq, rtq = self.sqb[dc % 2], self.rsqb[dc % 2]
                    P.op("dve", lambda e, py=py, dc=dc: e.tensor_copy(out=y[:, dc, 0:n], in_=py[:, 0:n]), reads=[rpy], writes=[ry[dc]])
                    P.op("act", lambda e, tq=tq, dc=dc: e.activation(out=tq[:, 0:n], in_=y[:, dc, 0:n], func=AF.Square), reads=[ry[dc]], writes=[rtq])
                    pend.append((tq, rtq, dc))
                    if len(pend) > 1:
                        self._stat_mm(pend.pop(0), pst2, rpst2, n)
                self._stat_mm(pend.pop(0), pst2, rpst2, n)
                self.rstd_from_sumsq(pst2, rpst2, n, rstd2, rrstd2, tmp[0], rtmp[0], NORM_EPS)

            def post(k):
                t0, n, s = tiles[k]
                xt, rx = xts[k % 2], rxs[k % 2]
                for c in range(KC):
                    tq, rtq = tmp[c % 2], rtmp[c % 2]
                    P.op("dve", lambda e, tq=tq, c=c: e.tensor_tensor(out=tq[:, 0:n], in0=y[:, c, 0:n], in1=rstd2[:, 0:n], op=ALU.mult),
                         reads=[ry[c], rrstd2], writes=[rtq])
                    P.op("dve", lambda e, tq=tq, c=c: e.scalar_tensor_tensor(out=y[:, c, 0:n], in0=tq[:, 0:n], scalar=self.mv(i, s, j, 2, c), in1=xt[:, c, 0:n],
                                                                              op0=ALU.mult, op1=ALU.add),
                         reads=[rtq, rx[c], self.rmodv], writes=[ry[c]])
                P.dma("pool", lambda e: e.dma_start(out=XOv[:, :, t0:t0 + n], in_=y[:, :, 0:n]), reads=ry, writes=[rXO])

            NTI = len(tiles)
            load_x(0)
            if NTI > 1:
                load_x(1)
            prenorm(0)
            bper = (len(bg) + NTI - 1) // NTI
            for k in range(NTI):
                instage(k)
                self.run_jobs(bg, bper)
                if k + 1 < NTI:
                    prenorm(k + 1)
                outstage(k)
                post(k)
                if k + 2 < NTI:
                    load_x(k + 2)
            self.run_jobs(bg, len(bg))

    def phase_conv(self, XI, rXI, XO, rXO):
        nc, P = self.nc, self.P
        i, j = 0, 1
        E = 512 + 2 * PAD
        pw1v = self.pw1.rearrange("(o p) k -> o p k", p=128)
        pw2v = self.pw2.rearrange("(o p) k -> o p k", p=128)
        XIv = XI.rearrange("(c p) t -> p c t", p=128)
        XOv = XO.rearrange("(c p) t -> p c t", p=128)
        tiles = [(g * 512, 512, 0, 0, NL) for g in range(8)] + [(NL, 256, 1, NL, NT)]
        vecs = self.vecs
        ones = self.ones
        with ExitStack() as st:
            xe = self.sb(st, "c_x", [128, KC, E])
            rx = mkres(KC, "x")
            h = self.sb(st, "c_h", [128, KC, E], BF16)
            rh = mkres(KC, "h")
            u = self.sb(st, "c_u", [128, KC, E], BF16)
            ru = mkres(KC, "u")
            dg = [self.sb(st, f"c_dg{k}", [128, CW, 128], BF16) for k in range(2)]
            rdg = [mkres(CW, "dg") for _ in range(2)]
            v = self.sb(st, "c_v", [128, KC, 512])
            rv = mkres(KC, "v")
            zt = self.sb(st, "c_z", [128, KC, 512], BF16)
            rz = mkres(KC, "z")
            w1 = [self.sb(st, f"c_w1{k}", [128, 2 * KC * 128], BF16) for k in range(2)]
            rw1 = mkres(2, "w1")
            w2 = [self.sb(st, f"c_w2{k}", [128, KC * 128], BF16) for k in range(2)]
            rw2 = mkres(2, "w2")
            tmp = [self.sb(st, f"c_t{k}", [128, E]) for k in range(4)]
            rtmp = mkres(4, "tmp")
            rstd_e = self.sb(st, "c_rstde", [128, E])
            rrstd_e = Res("rstde")
            mean = self.sb(st, "c_mean", [128, 512])
            rmean = Res("mean")
            rstd = self.sb(st, "c_rstd", [128, 512])
            rrstd = Res("rstd")
            rstd2 = self.sb(st, "c_rstd2", [128, 512])
            rrstd2 = Res("rstd2")
            nw1 = 0
            nw2 = 0
            mwb = [self.sb(st, f"c_mw{k}", [128, 2, 2048], BF16) for k in range(2)]
            rmwb = [mkres(2, "mw") for _ in range(2)]
            mraw = self.sb(st, "c_mraw", [128, 144 * 2])
            rmraw = Res("mraw")
            mpieces = self.mod_pieces(1, mwb, rmwb, mraw, rmraw, (0, 1)) if self.mod1_in_conv else []
            mper = (len(mpieces) + len(tiles) - 1) // len(tiles)
            W1S = self.dint("pw1s", [KC * 128, 2 * KC * 128], BF16).rearrange("(o p) k -> o p k", p=128)
            W2S = self.dint("pw2s", [KC * 128, KC * 128], BF16).rearrange("(o p) k -> o p k", p=128)
            rW1S = mkres(KC, "w1s")
            rW2S = mkres(KC, "w2s")
            for tidx, (t0, n, s, s0, s1) in enumerate(tiles):
                lo = max(t0 - PAD, s0)
                hi = min(t0 + n + PAD, s1)
                ne = hi - lo
                eo = lo - (t0 - PAD)
                pieces = [(eo, eo + min(ne, 512))]
                if ne > 512:
                    pieces.append((eo + 512, eo + ne))
                P.dma("pool", lambda e, lo=lo, hi=hi, eo=eo, ne=ne: e.dma_start(out=xe[:, :, eo:eo + ne], in_=XIv[:, :, lo:hi]), reads=[rXI], writes=rx)
                for pi, (a, b) in enumerate(pieces):
                    pst, rpst = self.ps[6 + pi], self.rps[6 + pi]
                    m = b - a
                    for c in range(KC):
                        tq, rtq = self.sqb[c % 2], self.rsqb[c % 2]
                        P.op("act", lambda e, tq=tq, c=c, a=a, b=b, m=m: e.activation(out=tq[:, 0:m], in_=xe[:, c, a:b], func=AF.Square), reads=[rx[c]], writes=[rtq])
                        P.op("pe", lambda e, tq=tq, c=c, m=m, pst=pst: e.matmul(out=pst[:, 0:m], lhsT=self.onesb[:], rhs=tq[:, 0:m], start=(c == 0), stop=(c == KC - 1)),
                             reads=[rtq, self.rconst], writes=[rpst])
                    tq, rtq = tmp[2], rtmp[2]
                    P.op("act", lambda e, tq=tq, pst=pst, m=m: e.activation(out=tq[:, 0:m], in_=pst[:, 0:m], func=AF.Sqrt, bias=self.epsv(NORM_EPS), scale=1.0 / D),
                         reads=[rpst, self.rconst], writes=[rtq])
                    P.op("dve", lambda e, tq=tq, a=a, b=b, m=m: e.reciprocal(out=rstd_e[:, a:b], in_=tq[:, 0:m]), reads=[rtq], writes=[rrstd_e])
                for c in range(KC):
                    tq = tmp[2 + c % 2]
                    rtq = rtmp[2 + c % 2]
                    P.op("dve", lambda e, c=c, eo=eo, ne=ne, tq=tq: e.tensor_tensor(out=tq[:, 0:ne], in0=xe[:, c, eo:eo + ne], in1=rstd_e[:, eo:eo + ne], op=ALU.mult),
                         reads=[rx[c], rrstd_e], writes=[rtq])
                    P.op("act", lambda e, c=c, eo=eo, ne=ne, s=s, tq=tq: e.activation(out=h[:, c, eo:eo + ne], in_=tq[:, 0:ne], func=AF.Identity,
                                                                                      scale=self.mv(i, s, j, 0, c), bias=self.mv(i, s, j, 1, c)),
                         reads=[rtq, self.rmodv], writes=[rh[c]])
                if eo > 0:
                    P.op("dve", lambda e, eo=eo: e.memset(u[:, :, 0:eo], 0.0), reads=rh, writes=ru)
                if eo + ne < n + 2 * PAD:
                    P.op("dve", lambda e, eo=eo, ne=ne, n=n: e.memset(u[:, :, eo + ne:n + 2 * PAD], 0.0), reads=rh, writes=ru)
                k2 = 0
                for oc in range(KC):
                    wt, rwt = w1[nw1 % 2], rw1[nw1 % 2]
                    nw1 += 1
                    if tidx == 0:
                        self.load_w_cast(wt[:], rwt, pw1v[oc])
                        P.dma("sp", lambda e, wt=wt, oc=oc: e.dma_start(out=W1S[oc], in_=wt[:]), reads=[rwt], writes=[rW1S[oc]])
                    else:
                        P.dma("sp", lambda e, wt=wt, oc=oc: e.dma_start(out=wt[:], in_=W1S[oc]), reads=[rW1S[oc]], writes=[rwt])
                    for (a, b) in pieces:
                        m = b - a
                        pa, rpa = self.ps[k2 % 2], self.rps[k2 % 2]
                        pg, rpg = self.ps[2 + k2 % 2], self.rps[2 + k2 % 2]
                        tq, rtq = tmp[k2 % 2], rtmp[k2 % 2]
                        k2 += 1
                        for kc in range(KC):
                            P.op("pe", lambda e, pa=pa, wt=wt, kc=kc, a=a, b=b, m=m: e.matmul(out=pa[:, 0:m], lhsT=wt[:, kc * 128:(kc + 1) * 128], rhs=h[:, kc, a:b],
                                                                                              start=(kc == 0), stop=(kc == KC - 1)), reads=[rwt, rh[kc]], writes=[rpa])
                        for kc in range(KC):
                            P.op("pe", lambda e, pg=pg, wt=wt, kc=kc, a=a, b=b, m=m: e.matmul(out=pg[:, 0:m], lhsT=wt[:, (KC + kc) * 128:(KC + kc + 1) * 128], rhs=h[:, kc, a:b],
                                                                                              start=(kc == 0), stop=(kc == KC - 1)), reads=[rwt, rh[kc]], writes=[rpg])
                        P.op("act", lambda e, tq=tq, pg=pg, m=m, oc=oc: e.activation(out=tq[:, 0:m], in_=pg[:, 0:m], func=AF.Sigmoid,
                                                                                      bias=vecs[:, V_BPW1 + 16 + oc:V_BPW1 + 16 + oc + 1]),
                             reads=[rpg, self.rvecs], writes=[rtq])
                        P.op("dve", lambda e, tq=tq, pa=pa, m=m, oc=oc, a=a, b=b: e.scalar_tensor_tensor(
                            out=u[:, oc, a:b], in0=pa[:, 0:m], scalar=vecs[:, V_BPW1 + oc:V_BPW1 + oc + 1], in1=tq[:, 0:m], op0=ALU.add, op1=ALU.mult),
                            reads=[rpa, rtq, self.rvecs], writes=[ru[oc]])
                for _ in range(mper):
                    if mpieces:
                        mpieces.pop(0)()
                self.run_jobs(self.bg_conv, (152 + len(tiles) - 1) // len(tiles))
                for c in range(KC):
                    dgt, rdgt = dg[c % 2], rdg[c % 2]
                    for k in range(CW):
                        wk = vecs[:, V_WDW + c * CW + k:V_WDW + c * CW + k + 1]
                        if k % 2 == 0:
                            P.op("act", lambda e, dgt=dgt, k=k, wk=wk: e.activation(out=dgt[:, k, :], in_=self.ident[:], func=AF.Identity, scale=wk),
                                 reads=[self.rconst, self.rvecs], writes=[rdgt[k]])
                        else:
                            P.op("dve", lambda e, dgt=dgt, k=k, wk=wk: e.tensor_scalar(out=dgt[:, k, :], in0=self.ident[:], scalar1=wk, scalar2=None, op0=ALU.mult),
                                 reads=[self.rconst, self.rvecs], writes=[rdgt[k]])
                    pv, rpv = self.ps[4 + c % 2], self.rps[4 + c % 2]
                    for k in range(CW):
                        P.op("pe", lambda e, pv=pv, dgt=dgt, k=k, c=c, n=n: e.matmul(out=pv[:, 0:n], lhsT=dgt[:, k, :], rhs=u[:, c, k:k + n], start=(k == 0), stop=(k == CW - 1)),
                             reads=[rdgt[k], ru[c]], writes=[rpv])
                    P.op("dve", lambda e, pv=pv, c=c, n=n: e.tensor_scalar(out=v[:, c, 0:n], in0=pv[:, 0:n], scalar1=vecs[:, V_BDW + c:V_BDW + c + 1], scalar2=None, op0=ALU.add),
                         reads=[rpv, self.rvecs], writes=[rv[c]])
                ps1, rps1 = self.ps[6], self.rps[6]
                ps2, rps2 = self.ps[7], self.rps[7]
                for c in range(KC):
                    tq, rtq = self.sqb[c % 2], self.rsqb[c % 2]
                    P.op("pe", lambda e, c=c, n=n: e.matmul(out=ps1[:, 0:n], lhsT=ones[:], rhs=v[:, c, 0:n], start=(c == 0), stop=(c == KC - 1)),
                         reads=[rv[c], self.rconst], writes=[rps1])
                    P.op("act", lambda e, tq=tq, c=c, n=n: e.activation(out=tq[:, 0:n], in_=v[:, c, 0:n], func=AF.Square), reads=[rv[c]], writes=[rtq])
                    P.op("pe", lambda e, tq=tq, c=c, n=n: e.matmul(out=ps2[:, 0:n], lhsT=self.onesb[:], rhs=tq[:, 0:n], start=(c == 0), stop=(c == KC - 1)),
                         reads=[rtq, self.rconst], writes=[rps2])
                P.op("dve", lambda e, n=n: e.tensor_scalar(out=mean[:, 0:n], in0=ps1[:, 0:n], scalar1=1.0 / D, scalar2=None, op0=ALU.mult), reads=[rps1], writes=[rmean])
                P.op("act", lambda e, n=n: e.activation(out=tmp[2][:, 0:n], in_=mean[:, 0:n], func=AF.Square), reads=[rmean], writes=[rtmp[2]])
                P.op("dve", lambda e, n=n: e.scalar_tensor_tensor(out=tmp[3][:, 0:n], in0=ps2[:, 0:n], scalar=1.0 / D, in1=tmp[2][:, 0:n], op0=ALU.mult, op1=ALU.subtract),
                     reads=[rps2, rtmp[2]], writes=[rtmp[3]])
                P.op("act", lambda e, n=n: e.activation(out=tmp[2][:, 0:n], in_=tmp[3][:, 0:n], func=AF.Sqrt, bias=self.epsv(LN_EPS), scale=1.0),
                     reads=[rtmp[3], self.rconst], writes=[rtmp[2]])
                P.op("dve", lambda e, n=n: e.reciprocal(out=rstd[:, 0:n], in_=tmp[2][:, 0:n]), reads=[rtmp[2]], writes=[rrstd])
                for c in range(KC):
                    tq, rtq = tmp[c % 2], rtmp[c % 2]
                    P.op("dve", lambda e, tq=tq, c=c, n=n: e.tensor_tensor(out=tq[:, 0:n], in0=v[:, c, 0:n], in1=mean[:, 0:n], op=ALU.subtract), reads=[rv[c], rmean], writes=[rtq])
                    P.op("dve", lambda e, c=c, n=n, tq=tq: e.tensor_tensor(out=v[:, c, 0:n], in0=tq[:, 0:n], in1=rstd[:, 0:n], op=ALU.mult), reads=[rtq, rrstd], writes=[rv[c]])
                    P.op("act", lambda e, c=c, n=n: e.activation(out=zt[:, c, 0:n], in_=v[:, c, 0:n], func=AF.Silu,
                                                                 scale=vecs[:, V_LNG + c:V_LNG + c + 1], bias=vecs[:, V_LNB + c:V_LNB + c + 1]),
                         reads=[rv[c], self.rvecs], writes=[rz[c]])
                y, ry = v, rv
                pst2, rpst2 = self.ps[6], self.rps[6]
                pend = []
                for dc in range(KC):
                    wt, rwt = w2[nw2 % 2], rw2[nw2 % 2]
                    nw2 += 1
                    if tidx == 0:
                        self.load_w_cast(wt[:], rwt, pw2v[dc])
                        P.dma("sp", lambda e, wt=wt, dc=dc: e.dma_start(out=W2S[dc], in_=wt[:]), reads=[rwt], writes=[rW2S[dc]])
                    else:
                        P.dma("sp", lambda e, wt=wt, dc=dc: e.dma_start(out=wt[:], in_=W2S[dc]), reads=[rW2S[dc]], writes=[rwt])
                    py, rpy = self.ps[4 + dc % 2], self.rps[4 + dc % 2]
                    for kc in range(KC):
                        P.op("pe", lambda e, py=py, wt=wt, kc=kc, n=n: e.matmul(out=py[:, 0:n], lhsT=wt[:, kc * 128:(kc + 1) * 128], rhs=zt[:, kc, 0:n],
                                                                                 start=(kc == 0), stop=(kc == KC - 1)), reads=[rwt, rz[kc]], writes=[rpy])
                    tq, rtq = self.sqb[dc % 2], self.rsqb[dc % 2]
                    P.op("dve", lambda e, py=py, dc=dc, n=n: e.tensor_scalar(out=y[:, dc, 0:n], in0=py[:, 0:n], scalar1=vecs[:, V_BPW2 + dc:V_BPW2 + dc + 1],
                                                                              scalar2=None, op0=ALU.add), reads=[rpy, self.rvecs], writes=[ry[dc]])
                    P.op("act", lambda e, tq=tq, dc=dc, n=n: e.activation(out=tq[:, 0:n], in_=y[:, dc, 0:n], func=AF.Square), reads=[ry[dc]], writes=[rtq])
                    pend.append((tq, rtq, dc))
                    if len(pend) > 1:
                        self._stat_mm(pend.pop(0), pst2, rpst2, n)
                self._stat_mm(pend.pop(0), pst2, rpst2, n)
                self.rstd_from_sumsq(pst2, rpst2, n, rstd2, rrstd2, tmp[0], rtmp[0], NORM_EPS)
                for c in range(KC):
                    tq, rtq = tmp[c % 2], rtmp[c % 2]
                    P.op("dve", lambda e, tq=tq, c=c, n=n: e.tensor_tensor(out=tq[:, 0:n], in0=y[:, c, 0:n], in1=rstd2[:, 0:n], op=ALU.mult),
                         reads=[ry[c], rrstd2], writes=[rtq])
                    P.op("dve", lambda e, tq=tq, c=c, n=n, s=s: e.scalar_tensor_tensor(out=y[:, c, 0:n], in0=tq[:, 0:n], scalar=self.mv(i, s, j, 2, c), in1=xe[:, c, PAD:PAD + n],
                                                                                        op0=ALU.mult, op1=ALU.add),
                         reads=[rtq, rx[c], self.rmodv], writes=[ry[c]])
                P.dma("sp", lambda e, t0=t0, n=n: e.dma_start(out=XOv[:, :, t0:t0 + n], in_=y[:, :, 0:n]), reads=ry, writes=[rXO])
            self.run_jobs(self.bg_conv, len(self.bg_conv))
            if self.mod1_in_conv:
                while mpieces:
                    mpieces.pop(0)()
                self.mod_derive(1, mraw, rmraw)

    def phase_attn(self, XI, rXI, XO, rXO):
        self.attn_a(XI, rXI)
        self.P.drain(); self.P.flush()
        self.attn_b()
        self.P.drain(); self.P.flush()
        self.attn_c(XI, rXI, XO, rXO)

    def attn_a(self, XI, rXI):
        nc, P = self.nc, self.P
        i, j = 1, 1
        XIv = XI.rearrange("(c p) t -> p c t", p=128)
        ropev = self.rope.rearrange("(k p) t -> p k t", p=128)
        wv_v = self.wv.rearrange("(b p) k -> b p k", p=128)
        ones = self.ones
        vecs = self.vecs
        lam_init = 0.8 - 0.6 * math.exp(-0.3 * 1)
        lam = self.lam
        with ExitStack() as st0:
            lt = self.sb(st0, "a_lt", [128, 128])
            rlt = Res("lt")
            for q in range(2):
                P.op("dve", lambda e, q=q: e.tensor_tensor(out=lt[:, q * 64:(q + 1) * 64], in0=vecs[:, V_LAM + q * 128:V_LAM + q * 128 + 64],
                                                            in1=vecs[:, V_LAM + q * 128 + 64:V_LAM + q * 128 + 128], op=ALU.mult),
                     reads=[self.rvecs], writes=[rlt])
                P.op("dve", lambda e, q=q: e.tensor_reduce(out=lam[:, q:q + 1], in_=lt[:, q * 64:(q + 1) * 64], axis=mybir.AxisListType.X, op=ALU.add),
                     reads=[rlt], writes=[self.rlam])
            P.op("act", lambda e: e.activation(out=lam[:, 2:4], in_=lam[:, 0:2], func=AF.Exp), reads=[self.rlam], writes=[self.rlam])
            P.op("dve", lambda e: e.tensor_tensor(out=lam[:, 4:5], in0=lam[:, 3:4], in1=lam[:, 2:3], op=ALU.subtract), reads=[self.rlam], writes=[self.rlam])
            P.op("dve", lambda e: e.tensor_scalar(out=lam[:, 5:6], in0=lam[:, 4:5], scalar1=-lam_init, scalar2=None, op0=ALU.add), reads=[self.rlam], writes=[self.rlam])
            P.op("dve", lambda e: e.tensor_scalar(out=lam[:, 6:7], in0=vecs[:, V_SUBLN:V_SUBLN + 1], scalar1=1.0 - lam_init, scalar2=None, op0=ALU.mult),
                 reads=[self.rvecs], writes=[self.rlam])
            P.drain(); P.flush()
        groups = [[(g * 512, 512, 0) for g in range(4)], [(g * 512, 512, 0) for g in range(4, 8)] + [(NL, 256, 1)]]
        for grp in groups:
            g0 = grp[0][0]
            gn = sum(t[1] for t in grp)
            gl = sum(t[1] for t in grp if t[2] == 0)
            with ExitStack() as st:
                hg = self.sb(st, "a_h", [128, KC, 2304], BF16)
                rhg = [mkres(KC, "h") for _ in grp]
                xt = self.sb(st, "a_x", [128, KC, 512])
                rx = mkres(KC, "x")
                rp = self.sb(st, "a_rope", [128, 4, 2048])
                rrp = Res("rope")
                tmp = [self.sb(st, f"a_t{k}", [128, 512]) for k in range(4)]
                rtmp = mkres(4, "tmp")
                rstd = self.sb(st, "a_rstd", [128, 512])
                rrstd = Res("rstd")
                wsl = [self.sb(st, f"a_w{k}", [128, 1, KC * 128], BF16) for k in range(2)]
                abf = [self.sb(st, f"a_ab{k}", [128, 512], BF16) for k in range(2)]
                rabf = mkres(2, "abf")
                permf = self.sb(st, "a_permf", [128, 128])
                permb = self.sb(st, "a_permb", [128, 128], BF16)
                rperm = Res("perm")
                P.dma("sp", lambda e: e.dma_start(out=permf[:], in_=self.perm_in[:, :]), writes=[rperm])
                P.op("dve", lambda e: e.tensor_copy(out=permb[:], in_=permf[:]), reads=[rperm], writes=[rperm])
                rwsl2 = [mkres(2, "w") for _ in range(2)]
                wvs = [self.sb(st, f"a_wv{k}", [128, KC * 512], BF16) for k in range(1)]
                rwvs = mkres(1, "wv")
                ost = [self.sb(st, f"a_o{k}", [128, 512], BF16) for k in range(4)]
                rost = mkres(4, "ost")
                P.dma("sp", lambda e, g0=g0, gl=gl: e.dma_start(out=rp[:, :, 0:gl], in_=ropev[:, :, g0:g0 + gl]), writes=[rrp])
                for ti, (t0, n, s) in enumerate(grp):
                    off = t0 - g0
                    P.dma("sp", lambda e, t0=t0, n=n: e.dma_start(out=xt[:, :, 0:n], in_=XIv[:, :, t0:t0 + n]), reads=[rXI], writes=rx)
                    pst, rpst = self.ps[6 + ti % 2], self.rps[6 + ti % 2]
                    for c in range(KC):
                        tq, rtq = self.sqb[c % 2], self.rsqb[c % 2]
                        P.op("act", lambda e, tq=tq, c=c, n=n: e.activation(out=tq[:, 0:n], in_=xt[:, c, 0:n], func=AF.Square), reads=[rx[c]], writes=[rtq])
                        P.op("pe", lambda e, tq=tq, c=c, n=n, pst=pst: e.matmul(out=pst[:, 0:n], lhsT=self.onesb[:], rhs=tq[:, 0:n], start=(c == 0), stop=(c == KC - 1)),
                             reads=[rtq, self.rconst], writes=[rpst])
                    self.rstd_from_sumsq(pst, rpst, n, rstd, rrstd, tmp[2], rtmp[2], NORM_EPS)
                    for c in range(KC):
                        tq, rtq = tmp[2 + c % 2], rtmp[2 + c % 2]
                        P.op("dve", lambda e, tq=tq, c=c, n=n: e.tensor_tensor(out=tq[:, 0:n], in0=xt[:, c, 0:n], in1=rstd[:, 0:n], op=ALU.mult),
                             reads=[rx[c], rrstd], writes=[rtq])
                        P.op("act", lambda e, tq=tq, c=c, n=n, s=s, off=off: e.activation(out=hg[:, c, off:off + n], in_=tq[:, 0:n], func=AF.Identity,
                                                                                           scale=self.mv(i, s, j, 0, c), bias=self.mv(i, s, j, 1, c)),
                             reads=[rtq, self.rmodv], writes=[rhg[ti][c]])
                nw = 0
                no = 0
                k2 = 0
                pendq = None
                for which, (wa, dst, rdst, tc, tsn) in enumerate(((self.wk, self.KT, self.rKT, 2, 3), (self.wq, self.QT, self.rQT, 0, 1))):
                    wav = wa.rearrange("(o p) k -> o p k", p=128)
                    def issue_w(oc_, slot):
                        self.load_w_cast(wsl[slot][:, 0, :], rwsl2[slot][0], wav[oc_])
                    issue_w(0, nw % 2)
                    for oc in range(NHEAD):
                        wt, rwt2 = wsl[nw % 2], rwsl2[nw % 2]
                        nw += 1
                        if oc + 1 < NHEAD:
                            issue_w(oc + 1, nw % 2)
                        for ti, (t0, n, s) in enumerate(grp):
                            if which == 1 and s == 1:
                                continue
                            off = t0 - g0
                            pa, rpa = self.ps[k2 % 2], self.rps[k2 % 2]
                            pb, rpb = self.ps[2 + k2 % 2], self.rps[2 + k2 % 2]
                            ta, rta = tmp[k2 % 2], rtmp[k2 % 2]
                            tb, rtb = tmp[2 + k2 % 2], rtmp[2 + k2 % 2]
                            k2 += 1
                            o, ro = ost[no % 4], rost[no % 4]
                            no += 1
                            for kc in range(KC):
                                P.op("pe", lambda e, pa=pa, wt=wt, kc=kc, n=n, off=off: e.matmul(out=pa[:, 0:n], lhsT=wt[:, 0, kc * 128:(kc + 1) * 128], rhs=hg[:, kc, off:off + n],
                                                                                                  start=(kc == 0), stop=(kc == KC - 1)), reads=[rwt2[0], rhg[ti][kc]], writes=[rpa])
                            if s == 0:
                                ab, rab = abf[k2 % 2], rabf[k2 % 2]
                                P.op("act", lambda e, ab=ab, pa=pa, n=n: e.activation(out=ab[:, 0:n], in_=pa[:, 0:n], func=AF.Copy), reads=[rpa], writes=[rab])
                            else:
                                ab, rab = None, None
                            if pendq is not None:
                                pendq()

                            def post(s=s, ab=ab, rab=rab, pa=pa, rpa=rpa, pb=pb, rpb=rpb, ta=ta, rta=rta, tb=tb, rtb=rtb, o=o, ro=ro, n=n, off=off,
                                     tc=tc, tsn=tsn, dst=dst, rdst=rdst, oc=oc, t0=t0):
                                if s == 0:
                                    P.op("pe", lambda e: e.matmul(out=pb[:, 0:n], lhsT=permb[:], rhs=ab[:, 0:n], start=True, stop=True),
                                         reads=[rperm, rab], writes=[rpb])
                                    P.op("dve", lambda e: e.tensor_tensor(out=ta[:, 0:n], in0=pa[:, 0:n], in1=rp[:, tc, off:off + n], op=ALU.mult),
                                         reads=[rpa, rrp], writes=[rta])
                                    P.op("dve", lambda e: e.tensor_tensor(out=tb[:, 0:n], in0=pb[:, 0:n], in1=rp[:, tsn, off:off + n], op=ALU.mult),
                                         reads=[rpb, rrp], writes=[rtb])
                                    P.op("pool", lambda e: e.tensor_tensor(out=o[:, 0:n], in0=ta[:, 0:n], in1=tb[:, 0:n], op=ALU.add),
                                         reads=[rta, rtb], writes=[ro])
                                else:
                                    P.op("act", lambda e: e.activation(out=o[:, 0:n], in_=pa[:, 0:n], func=AF.Copy), reads=[rpa], writes=[ro])
                                P.dma("sp", lambda e: e.dma_start(out=dst[oc * 128:(oc + 1) * 128, t0:t0 + n], in_=o[:, 0:n]), reads=[ro], writes=[rdst])
                            pendq = post
                    if pendq is not None:
                        pendq()
                        pendq = None
                nblk = gn // 128
                for nb in range(4):
                    wt, rwt = wvs[0], rwvs[0]
                    self.load_w_cast(wt[:], rwt, wv_v[nb])
                    for tb in range(nblk):
                        ti = min(tb // 4, len(grp) - 1)
                        pv, rpv = self.ps[4 + tb % 2], self.rps[4 + tb % 2]
                        o, ro = ost[no % 4], rost[no % 4]
                        no += 1
                        for kc in range(KC):
                            P.op("pe", lambda e, pv=pv, wt=wt, kc=kc, tb=tb: e.matmul(out=pv[:, :], lhsT=hg[:, kc, tb * 128:(tb + 1) * 128], rhs=wt[:, kc * 512:(kc + 1) * 512],
                                                                                      start=(kc == 0), stop=(kc == KC - 1)), reads=[rwt, rhg[ti][kc]], writes=[rpv])
                        if tb % 2 == 0:
                            P.op("act", lambda e, o=o, pv=pv: e.activation(out=o[:, :], in_=pv[:, :], func=AF.Copy), reads=[rpv], writes=[ro])
                        else:
                            P.op("dve", lambda e, o=o, pv=pv: e.tensor_copy(out=o[:, :], in_=pv[:, :]), reads=[rpv], writes=[ro])
                        r0 = g0 + tb * 128
                        P.dma("sp", lambda e, o=o, r0=r0, nb=nb: e.dma_start(out=self.VV[r0:r0 + 128, nb * 512:(nb + 1) * 512], in_=o[:, :]), reads=[ro], writes=[self.rVV])
                P.drain(); P.flush()

    def attn_b(self):
        nc, P = self.nc, self.P
        NKB = NT // 128
        NQT = NL // 512
        VVv = self.VV.rearrange("(kb p) e -> p kb e", p=128)
        lam = self.lam
        with ExitStack() as st:
            kT = [self.sb(st, f"b_k{k}", [128, NT], BF16) for k in range(2)]
            qz = [[self.sb(st, f"b_q{c}{k}", [128, NL], BF16) for k in range(2)] for c in range(2)]
            rqz = Res("qz")
            for k in range(2):
                P.op("dve", lambda e, k=k: e.memset(qz[0][k][64:128, :], 0.0), writes=[rqz])
                P.op("dve", lambda e, k=k: e.memset(qz[1][k][0:64, :], 0.0), writes=[rqz])
            vh = [self.sb(st, f"b_v{k}", [128, NKB, 128], BF16) for k in range(2)]
            rkqv = mkres(2, "kqv")
            pt = [self.sb(st, f"b_p{k}", [128, 512], BF16) for k in range(4)]
            rpt = mkres(4, "p")
            ob = [self.sb(st, f"b_o{k}", [128, 512]) for k in range(2)]
            rob = mkres(2, "o")
            t2 = [self.sb(st, f"b_t{k}", [128, 512]) for k in range(3)]
            rt2 = mkres(3, "t")
            osb = [self.sb(st, f"b_ob{k}", [128, 512], BF16) for k in range(2)]
            rosb = mkres(2, "ob")
            zc = self.sb(st, "b_zc", [128, 512])
            rzc = Res("zc")
            onesb = self.sb(st, "b_ones", [128, 128], BF16)
            ronesb = Res("onesb")
            P.op("dve", lambda e: e.memset(onesb[:], 1.0), writes=[ronesb])
            ones = self.ones
            pS = [self.ps[0], self.ps[1]]
            rpS = [self.rps[0], self.rps[1]]
            pO = [self.ps[2], self.ps[3]]
            rpO = [self.rps[2], self.rps[3]]
            pZ = [self.ps[4], self.ps[5]]
            rpZ = [self.rps[4], self.rps[5]]
            pN = [self.ps[6], self.ps[7]]
            rpN = [self.rps[6], self.rps[7]]
            npt = 0
            tcount = 0
            pending = None

            def finalize(item):
                o, ro, hd, qt, k = item
                sq, rsq = t2[2], rt2[2]
                sb_, rsb_ = self.sqb[k], self.rsqb[k]
                P.op("dve", lambda e: e.tensor_tensor(out=sb_[:], in0=o[:], in1=o[:], op=ALU.mult), reads=[ro], writes=[rsb_])
                P.op("pe", lambda e: e.matmul(out=pN[k][:], lhsT=self.onesb[:], rhs=sb_[:], start=True, stop=True), reads=[rsb_, self.rconst], writes=[rpN[k]])
                P.op("act", lambda e: e.activation(out=sq[:], in_=pN[k][:], func=AF.Ln, bias=self.epsv(SUBLN_EPS), scale=1.0 / 128), reads=[rpN[k], self.rconst], writes=[rsq])
                P.op("act", lambda e: e.activation(out=sq[:], in_=sq[:], func=AF.Exp, scale=-0.5), reads=[rsq], writes=[rsq])
                P.op("dve", lambda e: e.tensor_tensor(out=o[:], in0=o[:], in1=sq[:], op=ALU.mult), reads=[ro, rsq], writes=[ro])
                ot, rot = osb[k], rosb[k]
                P.op("dve", lambda e: e.tensor_scalar(out=ot[:], in0=o[:], scalar1=lam[:, 6:7], scalar2=None, op0=ALU.mult), reads=[ro, self.rlam], writes=[rot])
                P.dma("sp", lambda e: e.dma_start(out=self.ON[hd * 128:(hd + 1) * 128, qt * 512:(qt + 1) * 512], in_=ot[:]), reads=[rot], writes=[self.rON])

            for hd in range(NHEAD):
                b = hd % 2
                self.run_jobs(self.bg_attnb, 10)
                P.dma("sp", lambda e, b=b, hd=hd: e.dma_start(out=kT[b][:], in_=self.KT[hd * 128:(hd + 1) * 128, :]), reads=[self.rKT], writes=[rkqv[b]])
                P.dma("sp", lambda e, b=b, hd=hd: e.dma_start(out=qz[0][b][0:64, :], in_=self.QT[hd * 128:hd * 128 + 64, :]), reads=[self.rQT], writes=[rkqv[b]])
                P.dma("sp", lambda e, b=b, hd=hd: e.dma_start(out=qz[1][b][64:128, :], in_=self.QT[hd * 128 + 64:(hd + 1) * 128, :]), reads=[self.rQT], writes=[rkqv[b]])
                P.dma("sp", lambda e, b=b, hd=hd: e.dma_start(out=vh[b][:], in_=VVv[:, :, hd * 128:(hd + 1) * 128]), reads=[self.rVV], writes=[rkqv[b]])
                for qt in range(NQT):
                    k = tcount % 2
                    tcount += 1
                    q0 = qt * 512
                    prev = None
                    for kb in range(NKB + 1):
                        if kb < NKB:
                            cur = []
                            for c in range(2):
                                p, rp_ = pt[npt % 4], rpt[npt % 4]
                                npt += 1
                                P.op("pe", lambda e, c=c, b=b, kb=kb, q0=q0: e.matmul(out=pS[c][:], lhsT=kT[b][:, kb * 128:(kb + 1) * 128],
                                                                                     rhs=qz[c][b][:, q0:q0 + 512], start=True, stop=True),
                                     reads=[rkqv[b], rqz], writes=[rpS[c]])
                                cur.append((p, rp_))
                            for c in range(2):
                                p, rp_ = cur[c]
                                P.op("act", lambda e, c=c, p=p: e.activation(out=p[:], in_=pS[c][:], func=AF.Exp), reads=[rpS[c]], writes=[rp_])
                        if prev is not None:
                            kp = kb - 1
                            for c in range(2):
                                p, rp_ = prev[c]
                                P.op("pe", lambda e, c=c, p=p, b=b, kp=kp: e.matmul(out=pO[c][:], lhsT=vh[b][:, kp, :], rhs=p[:], start=(kp == 0), stop=(kp == NKB - 1)),
                                     reads=[rkqv[b], rp_], writes=[rpO[c]])
                            for c in range(2):
                                p, rp_ = prev[c]
                                P.op("pe", lambda e, c=c, p=p, kp=kp: e.matmul(out=pZ[c][:], lhsT=onesb[:], rhs=p[:], start=(kp == 0), stop=(kp == NKB - 1)),
                                     reads=[ronesb, rp_], writes=[rpZ[c]])
                        prev = cur if kb < NKB else None
                        if kb == 12 and pending is not None:
                            finalize(pending)
                            pending = None
                    o, ro = ob[k], rob[k]
                    P.op("dve", lambda e, o=o: e.tensor_copy(out=o[:], in_=pO[0][:]), reads=[rpO[0]], writes=[ro])
                    P.op("dve", lambda e: e.tensor_copy(out=t2[1][:], in_=pO[1][:]), reads=[rpO[1]], writes=[rt2[1]])
                    P.op("dve", lambda e: e.tensor_copy(out=t2[0][:], in_=pZ[0][:]), reads=[rpZ[0]], writes=[rt2[0]])
                    P.op("dve", lambda e: e.tensor_copy(out=zc[:], in_=pZ[1][:]), reads=[rpZ[1]], writes=[rzc])
                    P.op("dve", lambda e: e.reciprocal(out=t2[0][:], in_=t2[0][:]), reads=[rt2[0]], writes=[rt2[0]])
                    P.op("dve", lambda e: e.reciprocal(out=zc[:], in_=zc[:]), reads=[rzc], writes=[rzc])
                    P.op("dve", lambda e, o=o: e.tensor_tensor(out=o[:], in0=o[:], in1=t2[0][:], op=ALU.mult), reads=[ro, rt2[0]], writes=[ro])
                    P.op("dve", lambda e: e.tensor_tensor(out=t2[1][:], in0=t2[1][:], in1=zc[:], op=ALU.mult), reads=[rt2[1], rzc], writes=[rt2[1]])
                    P.op("dve", lambda e, o=o: e.scalar_tensor_tensor(out=o[:], in0=t2[1][:], scalar=lam[:, 5:6], in1=o[:], op0=ALU.mult, op1=ALU.add),
                         reads=[rt2[1], ro, self.rlam], writes=[ro])
                    pending = (o, ro, hd, qt, k)
            finalize(pending)
            self.run_jobs(self.bg_attnb, len(self.bg_attnb))

    def attn_c(self, XI, rXI, XO, rXO):
        nc, P = self.nc, self.P
        i, j = 1, 1
        XIv = XI.rearrange("(c p) t -> p c t", p=128)
        XOv = XO.rearrange("(c p) t -> p c t", p=128)
        ONv = self.ON.rearrange("(c p) t -> p c t", p=128)
        wov = self.wo.rearrange("(o p) k -> o p k", p=128)
        ones = self.ones
        with ExitStack() as st:
            woa = self.sb(st, "c3_wo", [128, KC, KC * 128], BF16)
            rwoa = Res("wo")
            xt = self.sb(st, "c3_x", [128, KC, 512])
            rx = mkres(KC, "x")
            on = [self.sb(st, f"c3_on{k}", [128, KC, 512], BF16) for k in range(2)]
            ron = mkres(2, "on")
            y = self.sb(st, "c3_y", [128, KC, 512])
            ry = mkres(KC, "y")
            tmp = [self.sb(st, f"c3_t{k}", [128, 512]) for k in range(4)]
            rtmp = mkres(4, "tmp")
            rstd2 = self.sb(st, "c3_rstd2", [128, 512])
            rrstd2 = Res("rstd2")
            rwo_l = mkres(KC, "wo")
            for dc in range(KC):
                self.load_w_cast(woa[:, dc, :], rwo_l[dc], wov[dc])
            s = 0
            for g in range(8):
                t0, n = g * 512, 512
                ont, ront = on[g % 2], ron[g % 2]
                P.dma("sp", lambda e, ont=ont, t0=t0: e.dma_start(out=ont[:], in_=ONv[:, :, t0:t0 + 512]), reads=[self.rON], writes=[ront])
                P.dma("sp", lambda e, t0=t0: e.dma_start(out=xt[:], in_=XIv[:, :, t0:t0 + 512]), reads=[rXI], writes=rx)
                pst2, rpst2 = self.ps[6], self.rps[6]
                pend = []
                for dc in range(KC):
                    py, rpy = self.ps[4 + dc % 2], self.rps[4 + dc % 2]
                    for kc in range(KC):
                        P.op("pe", lambda e, py=py, dc=dc, kc=kc, ont=ont: e.matmul(out=py[:], lhsT=woa[:, dc, kc * 128:(kc + 1) * 128], rhs=ont[:, kc, :],
                                                                                   start=(kc == 0), stop=(kc == KC - 1)), reads=[rwo_l[dc], ront], writes=[rpy])
                    tq, rtq = self.sqb[dc % 2], self.rsqb[dc % 2]
                    P.op("dve", lambda e, py=py, dc=dc: e.tensor_copy(out=y[:, dc, :], in_=py[:]), reads=[rpy], writes=[ry[dc]])
                    P.op("act", lambda e, tq=tq, dc=dc: e.activation(out=tq[:], in_=y[:, dc, :], func=AF.Square), reads=[ry[dc]], writes=[rtq])
                    pend.append((tq, rtq, dc))
                    if len(pend) > 1:
                        self._stat_mm(pend.pop(0), pst2, rpst2, n)
                self._stat_mm(pend.pop(0), pst2, rpst2, n)
                self.rstd_from_sumsq(pst2, rpst2, n, rstd2, rrstd2, tmp[0], rtmp[0], NORM_EPS)
                for c in range(KC):
                    tq, rtq = tmp[c % 2], rtmp[c % 2]
                    P.op("dve", lambda e, tq=tq, c=c: e.tensor_tensor(out=tq[:], in0=y[:, c, :], in1=rstd2[:], op=ALU.mult), reads=[ry[c], rrstd2], writes=[rtq])
                    P.op("dve", lambda e, tq=tq, c=c: e.scalar_tensor_tensor(out=y[:, c, :], in0=tq[:], scalar=self.mv(i, s, j, 2, c), in1=xt[:, c, :],
                                                                              op0=ALU.mult, op1=ALU.add), reads=[rtq, rx[c], self.rmodv], writes=[ry[c]])
                P.dma("sp", lambda e, t0=t0: e.dma_start(out=XOv[:, :, t0:t0 + 512], in_=y[:]), reads=ry, writes=[rXO])


def _tile_rows(w, ncol_chunk=128):
    K, N = w.shape
    kc, oc = K // 128, N // 128
    return np.ascontiguousarray(w.reshape(kc, 128, oc, 128).transpose(2, 1, 0, 3).reshape(oc * 128, kc * 128))


def _pvec(v):
    return np.ascontiguousarray(v.reshape(-1, 128).T)


def prep_shared(inp):
    sh = {}
    sh["ident"] = np.eye(128, dtype=np.float32)
    aw = inp["ada_w"]
    sh["adaw"] = np.concatenate([_tile_rows(aw[i]) for i in range(2)], axis=0)
    for i in range(2):
        for w, (ki, ko) in ((1, ("ffn1_w_in", "ffn1_w_out")), (2, ("ffn2_w_in", "ffn2_w_out"))):
            wi = inp[ki][i]
            a = _tile_rows(wi[:, :FF]).reshape(FC, 128, KC * 128)
            u = _tile_rows(wi[:, FF:]).reshape(FC, 128, KC * 128)
            sh[f"win{i}{w}"] = np.ascontiguousarray(np.concatenate([a, u], axis=2).reshape(FC * 128, 2 * KC * 128))
            sh[f"wout{i}{w}"] = _tile_rows(inp[ko][i])
    pw1 = inp["conv_w_pw1"][0]
    a = _tile_rows(pw1[:, :D]).reshape(KC, 128, KC * 128)
    g = _tile_rows(pw1[:, D:]).reshape(KC, 128, KC * 128)
    sh["pw1"] = np.ascontiguousarray(np.concatenate([a, g], axis=2).reshape(KC * 128, 2 * KC * 128))
    sh["pw2"] = _tile_rows(inp["conv_w_pw2"][0])
    wqkv = inp["attn_w_qkv"][0]
    wq, wk, wv = wqkv[:, :D], wqkv[:, D:2 * D], wqkv[:, 2 * D:]
    r = np.arange(D)
    i32 = r % 32
    partner = np.where(i32 < 16, r + 16, r - 16)
    sh["wq"] = _tile_rows(wq)
    sh["wk"] = _tile_rows(wk)
    pm = np.zeros((128, 128), np.float32)
    m = np.arange(128)
    pm[np.where((m % 32) < 16, m + 16, m - 16), m] = 1.0
    sh["perm"] = pm
    sh["wv"] = np.ascontiguousarray(wv.reshape(KC, 128, 4, 512).transpose(2, 1, 0, 3).reshape(4 * 128, KC * 512))
    sh["wo"] = _tile_rows(inp["attn_w_o"][0])
    t = np.arange(NL)
    inv_freq = (10000.0 ** (-np.arange(0, 32, 2, dtype=np.float32) / 32)).astype(np.float32)
    rr = np.arange(128)
    jj = rr % 64
    pos = np.where((jj < 32)[:, None], (t // 64)[None, :], (t % 64)[None, :]).astype(np.float32)
    fr = inv_freq[(jj % 32) % 16][:, None]
    ang = (pos * fr).astype(np.float32)
    cs = np.cos(ang).astype(np.float32)
    sn = np.sin(ang).astype(np.float32)
    sgn = np.where(((jj % 32) < 16)[:, None], -1.0, 1.0).astype(np.float32)
    sn = sn * sgn
    sh["rope"] = np.ascontiguousarray(np.concatenate([cs * 0.125, sn * 0.125, cs, sn], axis=0).astype(np.float32))
    return sh


def prep_vecs(inp, b):
    v = np.zeros((128, NV), np.float32)
    cc = np.stack([inp["c"][b], inp["c_ctx"]], axis=0)
    v[:, V_CC:V_CC + 32] = cc.reshape(2, KC, 128).transpose(2, 1, 0).reshape(128, 32)
    v[:, V_ADAB:V_ADAB + 288] = inp["ada_b"].reshape(2, 144, 128).transpose(2, 0, 1).reshape(128, 288)
    v[:, V_NPRE:V_NPRE + 96] = inp["norm_pre"].reshape(2, 3, KC, 128).transpose(3, 0, 1, 2).reshape(128, 96)
    v[:, V_NPOST:V_NPOST + 96] = inp["norm_post"].reshape(2, 3, KC, 128).transpose(3, 0, 1, 2).reshape(128, 96)
    v[:, V_BPW1:V_BPW1 + 32] = inp["conv_b_pw1"][0].reshape(2, KC, 128).transpose(2, 0, 1).reshape(128, 32)
    v[:, V_WDW:V_WDW + 496] = inp["conv_w_dw"][0].reshape(CW, KC, 128).transpose(2, 1, 0).reshape(128, 496)
    v[:, V_BDW:V_BDW + 16] = _pvec(inp["conv_b_dw"][0])
    v[:, V_LNG:V_LNG + 16] = _pvec(inp["conv_ln_g"][0])
    v[:, V_LNB:V_LNB + 16] = _pvec(inp["conv_ln_b"][0])
    v[:, V_BPW2:V_BPW2 + 16] = _pvec(inp["conv_b_pw2"][0])
    lam = np.concatenate([inp["attn_lambda_q1"][0], inp["attn_lambda_k1"][0], inp["attn_lambda_q2"][0], inp["attn_lambda_k2"][0]])
    v[:, V_LAM:V_LAM + 256] = lam[None, :]
    v[:, V_SUBLN] = inp["attn_subln_g"][0]
    return v


_CACHE = {}


def run(inp, phases=None, dbg=None, cores=NCORES, trace=False):
    key = (None if phases is None else tuple(phases), dbg)
    if key not in _CACHE:
        b = Builder(phases, dbg)
        nc = b.build()
        _CACHE[key] = (nc, b.in_names)
    nc, in_names = _CACHE[key]
    sh = prep_shared(inp)
    in_maps = []
    for b in range(cores):
        m = dict(sh)
        m["x"] = np.ascontiguousarray(inp["x"][b])
        m["ctx"] = np.ascontiguousarray(inp["ctx"][b])
        m["vecs"] = prep_vecs(inp, b)
        in_maps.append({k: m[k] for k in in_names})
    return run_bass_kernel_spmd(nc, in_maps, core_ids=list(range(cores)), trace=trace)


def kernel(**inputs):
    inp = {k: np.asarray(v) for k, v in inputs.items()}
    res = run(inp)
    return np.stack([res.results[b]["out"] for b in range(NCORES)], axis=0)
```

```python
import math
from contextlib import ExitStack

import numpy as np
import concourse.bass as bass
import concourse.mybir as mybir
from concourse.bass_utils import run_bass_kernel_spmd

F32 = mybir.dt.float32
BF16 = mybir.dt.bfloat16
AF = mybir.ActivationFunctionType
ALU = mybir.AluOpType

NCORES = 8
D = 2048
NL = 4096
NCTX = 256
NT = NL + NCTX
FF = 5632
KC = D // 128
FC = FF // 128
CW = 31
PAD = 15
NHEAD = 16
NORM_EPS = 1e-6
SUBLN_EPS = 1e-5
LN_EPS = 1e-5

V_CC = 0
V_ADAB = V_CC + 32
V_NPRE = V_ADAB + 288
V_NPOST = V_NPRE + 96
V_BPW1 = V_NPOST + 96
V_WDW = V_BPW1 + 32
V_BDW = V_WDW + 496
V_LNG = V_BDW + 16
V_LNB = V_LNG + 16
V_BPW2 = V_LNB + 16
V_LAM = V_BPW2 + 16
V_SUBLN = V_LAM + 256
NV = V_SUBLN + 1


class Res:
    __slots__ = ("name", "w", "rs", "excl")

    def __init__(self, name="", excl=False):
        self.name = name
        self.excl = excl
        self.w = None
        self.rs = {}


def mkres(n, name=""):
    return [Res(f"{name}{i}") for i in range(n)]


class Eng:
    def __init__(self, name, sem):
        self.name = name
        self.sem = sem
        self.cnt = 0
        self.known = {}
        self.ops = []
        self.ring = []
        self.ring_i = 0


class Prog:
    def __init__(self, nc, stack):
        self.nc = nc
        self.sems = {}
        self.E = {}
        for name in ("pe", "act", "dve", "pool", "sp"):
            s = stack.enter_context(nc.semaphore(f"prog_{name}"))
            self.sems[name] = s
            self.E[name] = Eng(name, s)
        for q, n in (("sp", 24), ("pool", 24), ("act", 8)):
            for i in range(n):
                key = f"dma_{q}_{i}"
                s = stack.enter_context(nc.semaphore(key))
                self.sems[key] = s
                self.E[q].ring.append([key, 0])

    def _need(self, eng, toks):
        for key, val in toks:
            if key == eng.name and eng.name == "pe":
                continue
            if eng.known.get(key, 0) < val:
                eng.known[key] = val
                eng.ops.append(("wait", key, val))

    def _deps(self, eng, reads, writes, same_eng_war=False):
        toks = []
        for r in reads:
            if r.w is not None:
                toks.append(r.w)
        for r in writes:
            if r.w is not None and r.w[0] != eng.name:
                toks.append(r.w)
            for k, v in r.rs.items():
                if k != eng.name:
                    toks.append((k, v))
        self._need(eng, toks)

    def _commit(self, tok, reads, writes):
        for r in reads:
            if r.rs.get(tok[0], 0) < tok[1]:
                r.rs[tok[0]] = tok[1]
        for r in writes:
            r.w = tok
            r.rs = {}

    def op(self, eng, fn, reads=(), writes=()):
        e = self.E[eng]
        if any(r.excl for r in reads):
            writes = list(writes) + [r for r in reads if r.excl]
        self._deps(e, reads, writes)
        e.cnt += 1
        tok = (eng, e.cnt)
        e.ops.append(("op", fn, eng, 1))
        self._commit(tok, reads, writes)

    def dma(self, q, fn, reads=(), writes=()):
        e = self.E[q]
        toks = []
        for r in reads:
            if r.w is not None:
                toks.append(r.w)
        for r in writes:
            if r.w is not None:
                toks.append(r.w)
            toks.extend(r.rs.items())
        slot = e.ring[e.ring_i]
        e.ring_i = (e.ring_i + 1) % len(e.ring)
        if slot[1] > 0:
            toks.append((slot[0], slot[1]))
        self._need(e, toks)
        slot[1] += 16
        tok = (slot[0], slot[1])
        e.ops.append(("op", fn, slot[0], 16))
        self._commit(tok, reads, writes)

    def drain(self, q="sp"):
        e = self.E[q]
        toks = []
        for name, o in self.E.items():
            if o.cnt > 0 and name != q:
                toks.append((name, o.cnt))
            for key, val in o.ring:
                if val > 0:
                    toks.append((key, val))
        self._need(e, toks)

    def flush(self, name="blk"):
        nc = self.nc
        sems = self.sems
        with nc.Block() as block:
            def replay(handle, ops):
                for o in ops:
                    if o[0] == "wait":
                        handle.wait_ge(sems[o[1]], o[2])
                    else:
                        o[1](handle).then_inc(sems[o[2]], o[3])

            @block.tensor
            def _(h):
                replay(h, self.E["pe"].ops)

            @block.scalar
            def _(h):
                replay(h, self.E["act"].ops)

            @block.vector
            def _(h):
                replay(h, self.E["dve"].ops)

            @block.gpsimd
            def _(h):
                replay(h, self.E["pool"].ops)

            @block.sync
            def _(h):
                replay(h, self.E["sp"].ops)
        for e in self.E.values():
            e.ops = []


class Builder:
    def __init__(self, phases=None, dbg=None):
        self.phases = phases
        self.dbg = dbg
        self.nc = bass.Bass("TRN2", target_bir_lowering=False)
        self.stack = ExitStack()
        self.in_names = []

    def din(self, name, shape, dt=F32):
        self.in_names.append(name)
        return self.nc.dram_tensor(name, list(shape), dt, kind="ExternalInput").ap()

    def dint(self, name, shape, dt=F32):
        return self.nc.dram_tensor(name, list(shape), dt, kind="Internal").ap()

    def sb(self, st, name, shape, dt=F32):
        self._uid = getattr(self, "_uid", 0) + 1
        return st.enter_context(self.nc.sbuf_tensor(f"sb{self._uid}_{name}", list(shape), dt))

    def build(self):
        nc = self.nc
        st = self.stack
        with st:
            self.P = Prog(nc, st)
            self._declare()
            self._globals(st)
            self._run_phases()
        return nc

    def _declare(self):
        self.x_in = self.din("x", [NL, D])
        self.ctx_in = self.din("ctx", [NCTX, D])
        self.vecs_in = self.din("vecs", [128, NV])
        self.ident_in = self.din("ident", [128, 128])
        self.adaw = self.din("adaw", [2 * 144 * 128, 2048])
        self.win = {}
        self.wout = {}
        for i in range(2):
            for w in (1, 2):
                self.win[(i, w)] = self.din(f"win{i}{w}", [FC * 128, 2 * KC * 128])
                self.wout[(i, w)] = self.din(f"wout{i}{w}", [KC * 128, FC * 128])
        self.pw1 = self.din("pw1", [KC * 128, 2 * KC * 128])
        self.pw2 = self.din("pw2", [KC * 128, KC * 128])
        self.wq = self.din("wq", [KC * 128, KC * 128])
        self.wk = self.din("wk", [KC * 128, KC * 128])
        self.perm_in = self.din("perm", [128, 128])
        self.wv = self.din("wv", [4 * 128, KC * 512])
        self.wo = self.din("wo", [KC * 128, KC * 128])
        self.rope = self.din("rope", [4 * 128, NL])
        self.out = self.nc.dram_tensor("out", [NL, D], F32, kind="ExternalOutput").ap()
        self.XA = self.dint("XA", [D, NT])
        self.XB = self.dint("XB", [D, NT])
        self.QT = self.dint("QT", [D, NL], BF16)
        self.KT = self.dint("KT", [D, NT], BF16)
        self.VV = self.dint("VV", [NT, D], BF16)
        self.ON = self.dint("ON", [D, NL], BF16)
        self.rXA = Res("XA")
        self.rXB = Res("XB")
        self.rQT = Res("QT")
        self.rKT = Res("KT")
        self.rVV = Res("VV")
        self.rON = Res("ON")
        if self.dbg is not None:
            self.dbg_out = self.nc.dram_tensor("dbg", [D, NT], F32, kind="ExternalOutput").ap()
            self.dbgv_out = self.nc.dram_tensor("dbgv", [128, 576], F32, kind="ExternalOutput").ap()

    def _globals(self, st):
        nc, P = self.nc, self.P
        self.ps = [st.enter_context(nc.psum_tensor(f"ps{i}", [128, 512], F32)) for i in range(8)]
        self.rps = [Res(f"ps{i}", excl=True) for i in range(8)]
        self.vecs = self.sb(st, "vecs", [128, NV])
        self.rvecs = Res("vecs")
        self.ident = self.sb(st, "ident", [128, 128])
        self.ones = self.sb(st, "ones", [128, 128])
        self.rconst = Res("const")
        self.modv = self.sb(st, "modv", [128, 2 * 2 * 3 * 3 * 16])
        self.rmodv = Res("modv")
        self.scb = self.sb(st, "scb", [128, 32], BF16)
        self.rscb = Res("scb")
        self.lam = self.sb(st, "lamv", [128, 8])
        self.rlam = Res("lam")
        vecs, ident, ones = self.vecs, self.ident, self.ones
        P.dma("sp", lambda e: e.dma_start(out=vecs[:], in_=self.vecs_in[:, :]), writes=[self.rvecs])
        P.dma("sp", lambda e: e.dma_start(out=ident[:], in_=self.ident_in[:, :]), writes=[self.rconst])
        P.op("dve", lambda e: e.memset(ones[:], 1.0), writes=[self.rconst])
        self.onesb = self.sb(st, "onesb", [128, 128], BF16)
        P.op("dve", lambda e: e.memset(self.onesb[:], 1.0), writes=[self.rconst])
        self.sqb = [self.sb(st, f"sqb{k}", [128, 512], BF16) for k in range(2)]
        self.rsqb = mkres(2, "sqb")
        self.setup_eps(st)

    def mv(self, i, s, j, kind, c):
        off = ((((i * 2 + s) * 3 + j) * 3 + kind) * 16) + c
        return self.modv[:, off:off + 1]

    def vcol(self, off, n=1):
        return self.vecs[:, off:off + n]

    def _run_phases(self):
        P = self.P
        ph = self.phases
        def want(name):
            return ph is None or name in ph
        if want("tin"):
            self.phase_transpose_in(self.XA, self.rXA)
            P.drain(); P.flush()
        self.mod1_in_conv = want("conv") and want("mod")
        if want("mod"):
            self.phase_mod((0,) if self.mod1_in_conv else (0, 1))
            P.drain(); P.flush()
        full = ph is None
        self.bg_conv = []
        self.bg_attnb = []
        bg_f20 = []
        if full:
            p = self.ffn_prep(0, 2); p["pre"] = True; self.bg_conv = p["jobs"]
            p = self.ffn_prep(1, 1); p["pre"] = True; bg_f20 = p["jobs"]
            p = self.ffn_prep(1, 2); p["pre"] = True; self.bg_attnb = p["jobs"]
        if want("ffn1_0"):
            self.phase_ffn(0, 1, 0, self.XA, self.rXA, self.XB, self.rXB, True)
            P.drain(); P.flush()
        if want("conv"):
            self.phase_conv(self.XB, self.rXB, self.XA, self.rXA)
            P.drain(); P.flush()
        if want("ffn2_0"):
            self.phase_ffn(0, 2, 2, self.XA, self.rXA, self.XB, self.rXB, True, bg=bg_f20)
            P.drain(); P.flush()
        if want("ffn1_1"):
            self.phase_ffn(1, 1, 0, self.XB, self.rXB, self.XA, self.rXA, True)
            P.drain(); P.flush()
        if want("attn"):
            self.phase_attn(self.XA, self.rXA, self.XB, self.rXB)
            P.drain(); P.flush()
        if want("ffn2_1"):
            self.phase_ffn(1, 2, 2, self.XB, self.rXB, self.XA, self.rXA, False)
            P.drain(); P.flush()
        if self.dbg is not None:
            src, rsrc = (self.XA, self.rXA) if self.dbg == "A" else (self.XB, self.rXB)
            rdbg = Res("dbg")
            P.dma("sp", lambda e: e.dma_start(out=self.dbgv_out[:, :], in_=self.modv[:]), reads=[self.rmodv], writes=[Res()])
            for c in range(KC):
                P.dma("sp", lambda e, c=c: e.dma_start(out=self.dbg_out[c * 128:(c + 1) * 128, :], in_=src[c * 128:(c + 1) * 128, :]), reads=[rsrc], writes=[rdbg])
        if want("tout"):
            self.phase_transpose_out(self.XA, self.rXA)
        P.drain(); P.flush()

    def phase_transpose_in(self, XO, rXO):
        nc, P = self.nc, self.P
        with ExitStack() as st:
            xin = [self.sb(st, f"ti_x{i}", [128, D]) for i in range(2)]
            rxin = mkres(2)
            stg = [self.sb(st, f"ti_s{i}", [128, KC, 512]) for i in range(2)]
            rstg = mkres(2)
            ident = self.ident
            XOv = XO.rearrange("(c p) t -> p c t", p=128)
            groups = [(self.x_in, g * 512, 4, g * 512) for g in range(8)] + [(self.ctx_in, 0, 2, NL)]
            blk = 0
            for gi, (src, r0, nb, c0) in enumerate(groups):
                sg, rsg = stg[gi % 2], rstg[gi % 2]
                for tb in range(nb):
                    xb, rxb = xin[blk % 2], rxin[blk % 2]
                    blk += 1
                    rr = r0 + tb * 128
                    P.dma("sp", lambda e, xb=xb, src=src, rr=rr: e.dma_start(out=xb[:], in_=src[rr:rr + 128, :]), writes=[rxb])
                    for b4 in range(4):
                        pst, rpst = self.ps[b4 + 4 * (tb % 2)], self.rps[b4 + 4 * (tb % 2)]
                        for cc in range(4):
                            c = b4 * 4 + cc
                            P.op("pe", lambda e, pst=pst, cc=cc, xb=xb, c=c: e.transpose(
                                out=pst[:, cc * 128:(cc + 1) * 128], in_=xb[:, c * 128:(c + 1) * 128], identity=ident[:]),
                                reads=[rxb, self.rconst], writes=[rpst])
                        dst = sg[:, b4 * 4:(b4 + 1) * 4, tb * 128:(tb + 1) * 128]
                        srcp = pst[:].rearrange("p (c t) -> p c t", c=4)
                        if b4 % 2 == 0:
                            P.op("act", lambda e, dst=dst, srcp=srcp: e.activation(out=dst, in_=srcp, func=AF.Copy), reads=[rpst], writes=[rsg])
                        else:
                            P.op("dve", lambda e, dst=dst, srcp=srcp: e.tensor_copy(out=dst, in_=srcp), reads=[rpst], writes=[rsg])
                n = nb * 128
                P.dma("sp", lambda e, sg=sg, c0=c0, n=n: e.dma_start(out=XOv[:, :, c0:c0 + n], in_=sg[:, :, 0:n]), reads=[rsg], writes=[rXO])

    def phase_transpose_out(self, XI, rXI):
        nc, P = self.nc, self.P
        with ExitStack() as st:
            xt = [self.sb(st, f"to_x{i}", [128, KC, 512]) for i in range(2)]
            rxt = mkres(2)
            ot = [self.sb(st, f"to_o{i}", [128, D]) for i in range(2)]
            rot = mkres(2)
            rout = Res("out")
            ident = self.ident
            XIv = XI.rearrange("(c p) t -> p c t", p=128)
            blk = 0
            for g in range(8):
                xg, rxg = xt[g % 2], rxt[g % 2]
                P.dma("sp", lambda e, xg=xg, g=g: e.dma_start(out=xg[:], in_=XIv[:, :, g * 512:(g + 1) * 512]), reads=[rXI], writes=[rxg])
                for tb in range(4):
                    o, ro = ot[blk % 2], rot[blk % 2]
                    blk += 1
                    for b4 in range(4):
                        pst, rpst = self.ps[b4 + 4 * (tb % 2)], self.rps[b4 + 4 * (tb % 2)]
                        for cc in range(4):
                            c = b4 * 4 + cc
                            P.op("pe", lambda e, pst=pst, cc=cc, xg=xg, c=c, tb=tb: e.transpose(
                                out=pst[:, cc * 128:(cc + 1) * 128], in_=xg[:, c, tb * 128:(tb + 1) * 128], identity=ident[:]),
                                reads=[rxg, self.rconst], writes=[rpst])
                        dst = o[:, b4 * 512:(b4 + 1) * 512]
                        if b4 % 2 == 0:
                            P.op("act", lambda e, dst=dst, pst=pst: e.activation(out=dst, in_=pst[:], func=AF.Copy), reads=[rpst], writes=[ro])
                        else:
                            P.op("dve", lambda e, dst=dst, pst=pst: e.tensor_copy(out=dst, in_=pst[:]), reads=[rpst], writes=[ro])
                    r0 = g * 512 + tb * 128
                    P.dma("sp", lambda e, o=o, r0=r0: e.dma_start(out=self.out[r0:r0 + 128, :], in_=o[:]), reads=[ro], writes=[rout])

    def mod_pieces(self, layer, wb, rwb, raw, rraw, banks):
        P = self.P
        OCB = 2
        vecs = self.vecs
        scb = self.scb
        adv = self.adaw.rearrange("(g o p) k -> g p o k", p=128, o=OCB)
        ng = 144 // OCB
        pieces = []
        for g in range(ng):
            def dma_part(g=g):
                w, rw = wb[g % len(wb)], rwb[g % len(wb)]
                gg = layer * ng + g
                for o in range(OCB):
                    P.dma("pool", lambda e, o=o: e.dma_start(out=w[:, o, :], in_=adv[gg][:, o, :], max_dma_last_dim=2048), writes=[rw[o]])

            def compute_part(g=g):
                w, rw = wb[g % len(wb)], rwb[g % len(wb)]
                pst, rpst = self.ps[banks[g % 2]], self.rps[banks[g % 2]]
                for o in range(OCB):
                    for kc in range(KC):
                        P.op("pe", lambda e, o=o, kc=kc: e.matmul(
                            out=pst[:, o * 2:o * 2 + 2], lhsT=w[:, o, kc * 128:(kc + 1) * 128], rhs=scb[:, kc * 2:kc * 2 + 2],
                            start=(kc == 0), stop=(kc == KC - 1)), reads=[rw[o], self.rscb], writes=[rpst])
                for o in range(OCB):
                    oc = g * OCB + o
                    ioc = layer * 144 + oc
                    P.op("dve", lambda e, o=o, oc=oc, ioc=ioc: e.tensor_scalar(
                        out=raw[:, oc * 2:oc * 2 + 2], in0=pst[:, o * 2:o * 2 + 2], scalar1=vecs[:, V_ADAB + ioc:V_ADAB + ioc + 1],
                        scalar2=None, op0=ALU.add), reads=[rpst, self.rvecs], writes=[rraw])
            pieces.append((dma_part, compute_part))
        return pieces

    def mod_derive(self, i, raw, rraw):
        P = self.P
        vecs = self.vecs
        modv = self.modv
        for s in range(2):
            for j in range(3):
                wgt = 0.5 if j != 1 else 1.0

                def rawv(r):
                    b0 = (((3 * j + r) * 16) * 2) + s
                    return raw[:, b0:b0 + 31:2]
                offA = (((i * 2 + s) * 3 + j) * 3 + 0) * 16
                offB = offA + 16
                offG = offA + 32
                npre = vecs[:, V_NPRE + (i * 3 + j) * 16:V_NPRE + (i * 3 + j) * 16 + 16]
                npost = vecs[:, V_NPOST + (i * 3 + j) * 16:V_NPOST + (i * 3 + j) * 16 + 16]
                P.op("dve", lambda e, o=offA, a=rawv(1), b=npre: e.scalar_tensor_tensor(
                    out=modv[:, o:o + 16], in0=a, scalar=1.0, in1=b, op0=ALU.add, op1=ALU.mult),
                    reads=[rraw, self.rvecs], writes=[self.rmodv])
                P.op("dve", lambda e, o=offB, a=rawv(0): e.tensor_copy(out=modv[:, o:o + 16], in_=a),
                     reads=[rraw], writes=[self.rmodv])
                P.op("dve", lambda e, o=offG, a=rawv(2), b=npost, wgt=wgt: e.scalar_tensor_tensor(
                    out=modv[:, o:o + 16], in0=a, scalar=wgt, in1=b, op0=ALU.mult, op1=ALU.mult),
                    reads=[rraw, self.rvecs], writes=[self.rmodv])

    def phase_mod(self, layers=(0,)):
        nc, P = self.nc, self.P
        with ExitStack() as st:
            sc = self.sb(st, "mod_sc", [128, 32])
            rsc = Res("sc")
            vecs = self.vecs
            P.op("act", lambda e: e.activation(out=sc[:], in_=vecs[:, V_CC:V_CC + 32], func=AF.Silu), reads=[self.rvecs], writes=[rsc])
            P.op("dve", lambda e: e.tensor_copy(out=self.scb[:], in_=sc[:]), reads=[rsc], writes=[self.rscb])
            wb = [self.sb(st, f"mod_w{k}", [128, 2, 2048], BF16) for k in range(6)]
            rwb = [mkres(2) for _ in range(6)]
            for layer in layers:
                raw = self.sb(st, f"mod_raw{layer}", [128, 144 * 2])
                rraw = Res("raw")
                for dma_part, compute_part in self.mod_pieces(layer, wb, rwb, raw, rraw, (0, 1)):
                    dma_part()
                    compute_part()
                self.mod_derive(layer, raw, rraw)

    def rstd_from_sumsq(self, pst, rpst, n, out, rout, tmp, rtmp, eps):
        P = self.P
        P.op("act", lambda e: e.activation(out=tmp[:, 0:n], in_=pst[:, 0:n], func=AF.Sqrt, bias=self.epsv(eps), scale=1.0 / D),
             reads=[rpst, self.rconst], writes=[rtmp])
        P.op("dve", lambda e: e.reciprocal(out=out[:, 0:n], in_=tmp[:, 0:n]), reads=[rtmp], writes=[rout])

    def _stat_mm(self, item, pst, rpst, n, last=KC - 1):
        tq, rtq, dc = item
        ones = self.onesb
        self.P.op("pe", lambda e: e.matmul(out=pst[:, 0:n], lhsT=ones[:], rhs=tq[:, 0:n], start=(dc == 0), stop=(dc == last)),
                  reads=[rtq, self.rconst], writes=[rpst])

    def epsv(self, eps):
        return self.epst[:, self.eps_idx[eps]:self.eps_idx[eps] + 1]

    def setup_eps(self, st):
        self.epst = self.sb(st, "epst", [128, 4])
        self.eps_idx = {}
        for k, v in enumerate(sorted({NORM_EPS, SUBLN_EPS, LN_EPS})):
            self.eps_idx[v] = k
            self.P.op("dve", lambda e, k=k, v=v: e.memset(self.epst[:, k:k + 1], v), writes=[self.rconst])

    def load_w_cast(self, dst, rdst, src_ap):
        self.P.dma("pool", lambda e: e.dma_start(out=dst, in_=src_ap, max_dma_last_dim=2048), writes=[rdst])

    def ffn_prep(self, i, w):
        if not hasattr(self, "_ffn_prep"):
            self._ffn_prep = {}
        if (i, w) in self._ffn_prep:
            return self._ffn_prep[(i, w)]
        P = self.P
        NQ, QF = 4, FC // 4
        win, wout = self.win[(i, w)], self.wout[(i, w)]
        winv = win.rearrange("(f p) k -> f p k", p=128)
        woutv = wout.rearrange("(d p) k -> d p k", p=128)
        WinS = self.dint(f"wins{i}{w}", [FC * 2 * 128, KC * 128], BF16).rearrange("(f h p) k -> f h p k", h=2, p=128)
        WoutS = self.dint(f"wouts{i}{w}", [KC * NQ * 128, QF * 128], BF16).rearrange("(d q p) k -> d q p k", q=NQ, p=128)
        rWinS = [[Res("wins") for _ in range(2)] for _ in range(FC)]
        rWoutS = [[Res("wouts") for _ in range(NQ)] for _ in range(KC)]
        jobs = []
        for fc in range(FC):
            for half in range(2):
                jobs.append(lambda fc=fc, half=half: P.dma("pool", lambda e: e.dma_start(
                    out=WinS[fc, half], in_=winv[fc][:, half * KC * 128:(half + 1) * KC * 128], max_dma_last_dim=2048), writes=[rWinS[fc][half]]))
        for dc in range(KC):
            for q in range(NQ):
                jobs.append(lambda dc=dc, q=q: P.dma("pool", lambda e: e.dma_start(
                    out=WoutS[dc, q], in_=woutv[dc][:, q * QF * 128:(q + 1) * QF * 128], max_dma_last_dim=2048), writes=[rWoutS[dc][q]]))
        d = {"WinS": WinS, "WoutS": WoutS, "rWinS": rWinS, "rWoutS": rWoutS, "jobs": jobs, "pre": False}
        self._ffn_prep[(i, w)] = d
        return d

    def run_jobs(self, jobs, n):
        for _ in range(n):
            if jobs:
                jobs.pop(0)()

    def phase_ffn(self, i, w, j, XI, rXI, XO, rXO, with_ctx, bg=None):
        nc, P = self.nc, self.P
        prep = self.ffn_prep(i, w)
        pre = prep["pre"]
        bg = bg if bg is not None else []
        win, wout = self.win[(i, w)], self.wout[(i, w)]
        winv = win.rearrange("(f p) k -> f p k", p=128)
        woutv = wout.rearrange("(d p) k -> d p k", p=128)
        XIv = XI.rearrange("(c p) t -> p c t", p=128)
        XOv = XO.rearrange("(c p) t -> p c t", p=128)
        tiles = [(g * 512, 512, 0) for g in range(8)] + ([(NL, 256, 1)] if with_ctx else [])
        NQ = 4
        QF = FC // NQ
        with ExitStack() as st:
            xts = [self.sb(st, f"f_x{k}", [128, KC, 512]) for k in range(2)]
            rxs = [mkres(KC, "x") for _ in range(2)]
            h = self.sb(st, "f_h", [128, KC, 512], BF16)
            rh = mkres(KC, "h")
            act = self.sb(st, "f_act", [128, FC, 512], BF16)
            ract = mkres(FC, "act")
            y = self.sb(st, "f_y", [128, KC, 512])
            ry = mkres(KC, "y")
            wi = [self.sb(st, f"f_wi{k}", [128, KC * 128], BF16) for k in range(4)]
            rwi = mkres(4, "wi")
            wo = [self.sb(st, f"f_wo{k}", [128, QF * 128], BF16) for k in range(4)]
            rwo = mkres(4, "wo")
            tmp = [self.sb(st, f"f_t{k}", [128, 512]) for k in range(4)]
            rtmp = mkres(4, "tmp")
            rstd = self.sb(st, "f_rstd", [128, 512])
            rrstd = Res("rstd")
            rstd2 = self.sb(st, "f_rstd2", [128, 512])
            rrstd2 = Res("rstd2")
            ones = self.ones
            cnt = {"wi": 0, "wo": 0}
            WinS, WoutS, rWinS, rWoutS = prep["WinS"], prep["WoutS"], prep["rWinS"], prep["rWoutS"]

            def load_x(k):
                t0, n, s = tiles[k]
                xt, rx = xts[k % 2], rxs[k % 2]
                P.dma("pool", lambda e: e.dma_start(out=xt[:, :, 0:n], in_=XIv[:, :, t0:t0 + n]), reads=[rXI], writes=rx)

            def prenorm(k):
                t0, n, s = tiles[k]
                xt, rx = xts[k % 2], rxs[k % 2]
                pst, rpst = self.ps[6], self.rps[6]
                for c in range(KC):
                    tq, rtq = self.sqb[c % 2], self.rsqb[c % 2]
                    P.op("act", lambda e, tq=tq, c=c: e.activation(out=tq[:, 0:n], in_=xt[:, c, 0:n], func=AF.Square), reads=[rx[c]], writes=[rtq])
                    P.op("pe", lambda e, tq=tq, c=c: e.matmul(out=pst[:, 0:n], lhsT=self.onesb[:], rhs=tq[:, 0:n], start=(c == 0), stop=(c == KC - 1)),
                         reads=[rtq, self.rconst], writes=[rpst])
                self.rstd_from_sumsq(pst, rpst, n, rstd, rrstd, tmp[2], rtmp[2], NORM_EPS)
                for c in range(KC):
                    tq, rtq = tmp[2 + c % 2], rtmp[2 + c % 2]
                    P.op("dve", lambda e, tq=tq, c=c: e.tensor_tensor(out=tq[:, 0:n], in0=xt[:, c, 0:n], in1=rstd[:, 0:n], op=ALU.mult),
                         reads=[rx[c], rrstd], writes=[rtq])
                    P.op("act", lambda e, tq=tq, c=c: e.activation(out=h[:, c, 0:n], in_=tq[:, 0:n], func=AF.Identity,
                                                                   scale=self.mv(i, s, j, 0, c), bias=self.mv(i, s, j, 1, c)),
                         reads=[rtq, self.rmodv], writes=[rh[c]])

            def instage(k):
                t0, n, s = tiles[k]
                for fc in range(FC):
                    pa, rpa = self.ps[fc % 2], self.rps[fc % 2]
                    pu, rpu = self.ps[2 + fc % 2], self.rps[2 + fc % 2]
                    for half, (pp, rpp) in enumerate(((pa, rpa), (pu, rpu))):
                        wt, rwt = wi[cnt["wi"] % 4], rwi[cnt["wi"] % 4]
                        cnt["wi"] += 1
                        if k == 0 and not pre:
                            self.load_w_cast(wt[:], rwt, winv[fc][:, half * KC * 128:(half + 1) * KC * 128])
                            if len(tiles) > 1:
                                P.dma("sp", lambda e, wt=wt, fc=fc, half=half: e.dma_start(out=WinS[fc, half], in_=wt[:]), reads=[rwt], writes=[rWinS[fc][half]])
                        else:
                            P.dma("sp", lambda e, wt=wt, fc=fc, half=half: e.dma_start(out=wt[:], in_=WinS[fc, half]), reads=[rWinS[fc][half]], writes=[rwt])
                        for kc in range(KC):
                            P.op("pe", lambda e, pp=pp, wt=wt, kc=kc: e.matmul(out=pp[:, 0:n], lhsT=wt[:, kc * 128:(kc + 1) * 128], rhs=h[:, kc, 0:n],
                                                                                start=(kc == 0), stop=(kc == KC - 1)), reads=[rwt, rh[kc]], writes=[rpp])
                    tq, rtq = tmp[fc % 2], rtmp[fc % 2]
                    P.op("act", lambda e, tq=tq, pa=pa: e.activation(out=tq[:, 0:n], in_=pa[:, 0:n], func=AF.Silu), reads=[rpa], writes=[rtq])
                    P.op("dve", lambda e, tq=tq, pu=pu, fc=fc: e.tensor_tensor(out=act[:, fc, 0:n], in0=pu[:, 0:n], in1=tq[:, 0:n], op=ALU.mult),
                         reads=[rpu, rtq], writes=[ract[fc]])

            def outstage(k):
                t0, n, s = tiles[k]
                pst2, rpst2 = self.ps[7], self.rps[7]
                pend = []
                for dc in range(KC):
                    py, rpy = self.ps[4 + dc % 2], self.rps[4 + dc % 2]
                    for q in range(NQ):
                        wt, rwt = wo[cnt["wo"] % 4], rwo[cnt["wo"] % 4]
                        cnt["wo"] += 1
                        if k == 0 and not pre:
                            self.load_w_cast(wt[:], rwt, woutv[dc][:, q * QF * 128:(q + 1) * QF * 128])
                            if len(tiles) > 1:
                                P.dma("sp", lambda e, wt=wt, dc=dc, q=q: e.dma_start(out=WoutS[dc, q], in_=wt[:]), reads=[rwt], writes=[rWoutS[dc][q]])
                        else:
                            P.dma("sp", lambda e, wt=wt, dc=dc, q=q: e.dma_start(out=wt[:], in_=WoutS[dc, q]), reads=[rWoutS[dc][q]], writes=[rwt])
                        for f in range(QF):
                            fc = q * QF + f
                            P.op("pe", lambda e, py=py, wt=wt, f=f, fc=fc: e.matmul(out=py[:, 0:n], lhsT=wt[:, f * 128:(f + 1) * 128], rhs=act[:, fc, 0:n],
                                                                                     start=(fc == 0), stop=(fc == FC - 1)), reads=[rwt, ract[fc]], writes=[rpy])
                    tq, rtq = self.sqb[dc % 2], self.rsqb[dc % 2]
                    P.op("dve", lambda e, py=py, dc=dc: e.tensor_copy(out=y[:, dc, 0:n], in_=py[:, 0:n]), reads=[rpy], writes=[ry[dc]])
                    P.op("act", lambda e, tq=tq, dc=dc: e.activation(out=tq[:, 0:n], in_=y[:, dc, 0:n], func=AF.Square), reads=[ry[dc]], writes=[rtq])
                    pend.append((tq, rtq, dc))
                    if len(pend) > 1:
                        self._stat_mm(pend.pop(0), pst2, rpst2, n)
                self._stat_mm(pend.pop(0), pst2, rpst2, n)
                self.rstd_from_sumsq(pst2, rpst2, n, rstd2, rrstd2, tmp[0], rtmp[0], NORM_EPS)

            def post(k):
                t0, n, s = tiles[k]
                xt, rx = xts[k % 2], rxs[k % 2]
                for c in range(KC):
                    tq, rtq = tmp[c % 2], rtmp[c % 2]
                    P.op("dve", lambda e, tq=tq, c=c: e.tensor_tensor(out=tq[:, 0:n], in0=y[:, c, 0:n], in1=rstd2[:, 0:n], op=ALU.mult),
                         reads=[ry[c], rrstd2], writes=[rtq])
                    P.op("dve", lambda e, tq=tq, c=c: e.scalar_tensor_tensor(out=y[:, c, 0:n], in0=tq[:, 0:n], scalar=self.mv(i, s, j, 2, c), in1=xt[:, c, 0:n],
                                                                              op0=ALU.mult, op1=ALU.add),
                         reads=[rtq, rx[c], self.rmodv], writes=[ry[c]])
                P.dma("pool", lambda e: e.dma_start(out=XOv[:, :, t0:t0 + n], in_=y[:, :, 0:n]), reads=ry, writes=[rXO])

            NTI = len(tiles)
            load_x(0)
            if NTI > 1:
                load_x(1)
            prenorm(0)
            bper = (len(bg) + NTI - 1) // NTI
            for k in range(NTI):
                instage(k)
                self.run_jobs(bg, bper)
                if k + 1 < NTI:
                    prenorm(k + 1)
                outstage(k)
                post(k)
                if k + 2 < NTI:
                    load_x(k + 2)
            self.run_jobs(bg, len(bg))

    def phase_conv(self, XI, rXI, XO, rXO):
        nc, P = self.nc, self.P
        i, j = 0, 1
        E = 512 + 2 * PAD
        pw1v = self.pw1.rearrange("(o p) k -> o p k", p=128)
        pw2v = self.pw2.rearrange("(o p) k -> o p k", p=128)
        XIv = XI.rearrange("(c p) t -> p c t", p=128)
        XOv = XO.rearrange("(c p) t -> p c t", p=128)
        tiles = [(g * 512, 512, 0, 0, NL) for g in range(8)] + [(NL, 256, 1, NL, NT)]
        vecs = self.vecs
        ones = self.ones
        with ExitStack() as st:
            xe = self.sb(st, "c_x", [128, KC, E])
            rx = mkres(KC, "x")
            h = self.sb(st, "c_h", [128, KC, E], BF16)
            rh = mkres(KC, "h")
            u = self.sb(st, "c_u", [128, KC, E], BF16)
            ru = mkres(KC, "u")
            dg = [self.sb(st, f"c_dg{k}", [128, CW, 128], BF16) for k in range(2)]
            rdg = [mkres(CW, "dg") for _ in range(2)]
            v = self.sb(st, "c_v", [128, KC, 512])
            rv = mkres(KC, "v")
            zt = self.sb(st, "c_z", [128, KC, 512], BF16)
            rz = mkres(KC, "z")
            w1 = [self.sb(st, f"c_w1{k}", [128, 2 * KC * 128], BF16) for k in range(2)]
            rw1 = mkres(2, "w1")
            w2 = [self.sb(st, f"c_w2{k}", [128, KC * 128], BF16) for k in range(2)]
            rw2 = mkres(2, "w2")
            tmp = [self.sb(st, f"c_t{k}", [128, E]) for k in range(4)]
            rtmp = mkres(4, "tmp")
            rstd_e = self.sb(st, "c_rstde", [128, E])
            rrstd_e = Res("rstde")
            mean = self.sb(st, "c_mean", [128, 512])
            rmean = Res("mean")
            rstd = self.sb(st, "c_rstd", [128, 512])
            rrstd = Res("rstd")
            rstd2 = self.sb(st, "c_rstd2", [128, 512])
            rrstd2 = Res("rstd2")
            nw1 = 0
            nw2 = 0
            mwb = [self.sb(st, f"c_mw{k}", [128, 2, 2048], BF16) for k in range(2)]
            rmwb = [mkres(2, "mw") for _ in range(2)]
            mraw = self.sb(st, "c_mraw", [128, 144 * 2])
            rmraw = Res("mraw")
            mpieces = self.mod_pieces(1, mwb, rmwb, mraw, rmraw, (6, 7)) if self.mod1_in_conv else []
            mq_dma = [p[0] for p in mpieces]
            mq_cmp = [p[1] for p in mpieces]
            mper = (len(mpieces) + len(tiles) - 1) // len(tiles)
            W1S = self.dint("pw1s", [KC * 128, 2 * KC * 128], BF16).rearrange("(o p) k -> o p k", p=128)
            W2S = self.dint("pw2s", [KC * 128, KC * 128], BF16).rearrange("(o p) k -> o p k", p=128)
            rW1S = mkres(KC, "w1s")
            rW2S = mkres(KC, "w2s")
            for tidx, (t0, n, s, s0, s1) in enumerate(tiles):
                lo = max(t0 - PAD, s0)
                hi = min(t0 + n + PAD, s1)
                ne = hi - lo
                eo = lo - (t0 - PAD)
                pieces = [(eo, eo + min(ne, 512))]
                if ne > 512:
                    pieces.append((eo + 512, eo + ne))
                P.dma("pool", lambda e, lo=lo, hi=hi, eo=eo, ne=ne: e.dma_start(out=xe[:, :, eo:eo + ne], in_=XIv[:, :, lo:hi]), reads=[rXI], writes=rx)
                for pi, (a, b) in enumerate(pieces):
                    pst, rpst = self.ps[6 + pi], self.rps[6 + pi]
                    m = b - a
                    for c in range(KC):
                        tq, rtq = self.sqb[c % 2], self.rsqb[c % 2]
                        P.op("act", lambda e, tq=tq, c=c, a=a, b=b, m=m: e.activation(out=tq[:, 0:m], in_=xe[:, c, a:b], func=AF.Square), reads=[rx[c]], writes=[rtq])
                        P.op("pe", lambda e, tq=tq, c=c, m=m, pst=pst: e.matmul(out=pst[:, 0:m], lhsT=self.onesb[:], rhs=tq[:, 0:m], start=(c == 0), stop=(c == KC - 1)),
                             reads=[rtq, self.rconst], writes=[rpst])
                    tq, rtq = tmp[2], rtmp[2]
                    P.op("act", lambda e, tq=tq, pst=pst, m=m: e.activation(out=tq[:, 0:m], in_=pst[:, 0:m], func=AF.Sqrt, bias=self.epsv(NORM_EPS), scale=1.0 / D),
                         reads=[rpst, self.rconst], writes=[rtq])
                    P.op("dve", lambda e, tq=tq, a=a, b=b, m=m: e.reciprocal(out=rstd_e[:, a:b], in_=tq[:, 0:m]), reads=[rtq], writes=[rrstd_e])
                for c in range(KC):
                    tq = tmp[2 + c % 2]
                    rtq = rtmp[2 + c % 2]
                    P.op("dve", lambda e, c=c, eo=eo, ne=ne, tq=tq: e.tensor_tensor(out=tq[:, 0:ne], in0=xe[:, c, eo:eo + ne], in1=rstd_e[:, eo:eo + ne], op=ALU.mult),
                         reads=[rx[c], rrstd_e], writes=[rtq])
                    P.op("act", lambda e, c=c, eo=eo, ne=ne, s=s, tq=tq: e.activation(out=h[:, c, eo:eo + ne], in_=tq[:, 0:ne], func=AF.Identity,
                                                                                      scale=self.mv(i, s, j, 0, c), bias=self.mv(i, s, j, 1, c)),
                         reads=[rtq, self.rmodv], writes=[rh[c]])
                if eo > 0:
                    P.op("dve", lambda e, eo=eo: e.memset(u[:, :, 0:eo], 0.0), reads=rh, writes=ru)
                if eo + ne < n + 2 * PAD:
                    P.op("dve", lambda e, eo=eo, ne=ne, n=n: e.memset(u[:, :, eo + ne:n + 2 * PAD], 0.0), reads=rh, writes=ru)
                k2 = 0
                issued = 0
                for oc in range(KC):
                    outstanding = len(mq_cmp) - len(mq_dma)
                    if outstanding >= 2 or (outstanding > 0 and (issued >= mper or not mq_dma)):
                        mq_cmp.pop(0)()
                    if issued < mper and mq_dma and (len(mq_cmp) - len(mq_dma)) < 2:
                        mq_dma.pop(0)()
                        issued += 1
                    wt, rwt = w1[nw1 % 2], rw1[nw1 % 2]
                    nw1 += 1
                    if tidx == 0:
                        self.load_w_cast(wt[:], rwt, pw1v[oc])
                        P.dma("sp", lambda e, wt=wt, oc=oc: e.dma_start(out=W1S[oc], in_=wt[:]), reads=[rwt], writes=[rW1S[oc]])
                    else:
                        P.dma("sp", lambda e, wt=wt, oc=oc: e.dma_start(out=wt[:], in_=W1S[oc]), reads=[rW1S[oc]], writes=[rwt])
                    for (a, b) in pieces:
                        m = b - a
                        pa, rpa = self.ps[k2 % 2], self.rps[k2 % 2]
                        pg, rpg = self.ps[2 + k2 % 2], self.rps[2 + k2 % 2]
                        tq, rtq = tmp[k2 % 2], rtmp[k2 % 2]
                        k2 += 1
                        for kc in range(KC):
                            P.op("pe", lambda e, pa=pa, wt=wt, kc=kc, a=a, b=b, m=m: e.matmul(out=pa[:, 0:m], lhsT=wt[:, kc * 128:(kc + 1) * 128], rhs=h[:, kc, a:b],
                                                                                              start=(kc == 0), stop=(kc == KC - 1)), reads=[rwt, rh[kc]], writes=[rpa])
                        for kc in range(KC):
                            P.op("pe", lambda e, pg=pg, wt=wt, kc=kc, a=a, b=b, m=m: e.matmul(out=pg[:, 0:m], lhsT=wt[:, (KC + kc) * 128:(KC + kc + 1) * 128], rhs=h[:, kc, a:b],
                                                                                              start=(kc == 0), stop=(kc == KC - 1)), reads=[rwt, rh[kc]], writes=[rpg])
                        P.op("act", lambda e, tq=tq, pg=pg, m=m, oc=oc: e.activation(out=tq[:, 0:m], in_=pg[:, 0:m], func=AF.Sigmoid,
                                                                                      bias=vecs[:, V_BPW1 + 16 + oc:V_BPW1 + 16 + oc + 1]),
                             reads=[rpg, self.rvecs], writes=[rtq])
                        P.op("dve", lambda e, tq=tq, pa=pa, m=m, oc=oc, a=a, b=b: e.scalar_tensor_tensor(
                            out=u[:, oc, a:b], in0=pa[:, 0:m], scalar=vecs[:, V_BPW1 + oc:V_BPW1 + oc + 1], in1=tq[:, 0:m], op0=ALU.add, op1=ALU.mult),
                            reads=[rpa, rtq, self.rvecs], writes=[ru[oc]])
                while len(mq_cmp) > len(mq_dma):
                    mq_cmp.pop(0)()
                self.run_jobs(self.bg_conv, (152 + len(tiles) - 1) // len(tiles))
                ps1, rps1 = self.ps[6], self.rps[6]
                ps2, rps2 = self.ps[7], self.rps[7]

                def ln_stats(c, n=n):
                    tq, rtq = self.sqb[c % 2], self.rsqb[c % 2]
                    P.op("pe", lambda e: e.matmul(out=ps1[:, 0:n], lhsT=ones[:], rhs=v[:, c, 0:n], start=(c == 0), stop=(c == KC - 1)),
                         reads=[rv[c], self.rconst], writes=[rps1])
                    P.op("act", lambda e: e.activation(out=tq[:, 0:n], in_=v[:, c, 0:n], func=AF.Square), reads=[rv[c]], writes=[rtq])
                    P.op("pe", lambda e: e.matmul(out=ps2[:, 0:n], lhsT=self.onesb[:], rhs=tq[:, 0:n], start=(c == 0), stop=(c == KC - 1)),
                         reads=[rtq, self.rconst], writes=[rps2])
                for c in range(KC):
                    dgt, rdgt = dg[c % 2], rdg[c % 2]
                    for k in range(CW):
                        wk = vecs[:, V_WDW + c * CW + k:V_WDW + c * CW + k + 1]
                        if k % 2 == 0:
                            P.op("act", lambda e, dgt=dgt, k=k, wk=wk: e.activation(out=dgt[:, k, :], in_=self.ident[:], func=AF.Identity, scale=wk),
                                 reads=[self.rconst, self.rvecs], writes=[rdgt[k]])
                        else:
                            P.op("dve", lambda e, dgt=dgt, k=k, wk=wk: e.tensor_scalar(out=dgt[:, k, :], in0=self.ident[:], scalar1=wk, scalar2=None, op0=ALU.mult),
                                 reads=[self.rconst, self.rvecs], writes=[rdgt[k]])
                    pv, rpv = self.ps[4 + c % 2], self.rps[4 + c % 2]
                    for k in range(CW):
                        P.op("pe", lambda e, pv=pv, dgt=dgt, k=k, c=c, n=n: e.matmul(out=pv[:, 0:n], lhsT=dgt[:, k, :], rhs=u[:, c, k:k + n], start=(k == 0), stop=(k == CW - 1)),
                             reads=[rdgt[k], ru[c]], writes=[rpv])
                    P.op("dve", lambda e, pv=pv, c=c, n=n: e.tensor_scalar(out=v[:, c, 0:n], in0=pv[:, 0:n], scalar1=vecs[:, V_BDW + c:V_BDW + c + 1], scalar2=None, op0=ALU.add),
                         reads=[rpv, self.rvecs], writes=[rv[c]])
                    if c >= 1:
                        ln_stats(c - 1)
                ln_stats(KC - 1)
                P.op("dve", lambda e, n=n: e.tensor_scalar(out=mean[:, 0:n], in0=ps1[:, 0:n], scalar1=1.0 / D, scalar2=None, op0=ALU.mult), reads=[rps1], writes=[rmean])
                P.op("act", lambda e, n=n: e.activation(out=tmp[2][:, 0:n], in_=mean[:, 0:n], func=AF.Square), reads=[rmean], writes=[rtmp[2]])
                P.op("dve", lambda e, n=n: e.scalar_tensor_tensor(out=tmp[3][:, 0:n], in0=ps2[:, 0:n], scalar=1.0 / D, in1=tmp[2][:, 0:n], op0=ALU.mult, op1=ALU.subtract),
                     reads=[rps2, rtmp[2]], writes=[rtmp[3]])
                P.op("act", lambda e, n=n: e.activation(out=tmp[2][:, 0:n], in_=tmp[3][:, 0:n], func=AF.Sqrt, bias=self.epsv(LN_EPS), scale=1.0),
                     reads=[rtmp[3], self.rconst], writes=[rtmp[2]])
                P.op("dve", lambda e, n=n: e.reciprocal(out=rstd[:, 0:n], in_=tmp[2][:, 0:n]), reads=[rtmp[2]], writes=[rrstd])
                for c in range(KC):
                    tq, rtq = tmp[c % 2], rtmp[c % 2]
                    P.op("dve", lambda e, tq=tq, c=c, n=n: e.tensor_tensor(out=tq[:, 0:n], in0=v[:, c, 0:n], in1=mean[:, 0:n], op=ALU.subtract), reads=[rv[c], rmean], writes=[rtq])
                    P.op("dve", lambda e, c=c, n=n, tq=tq: e.tensor_tensor(out=v[:, c, 0:n], in0=tq[:, 0:n], in1=rstd[:, 0:n], op=ALU.mult), reads=[rtq, rrstd], writes=[rv[c]])
                    P.op("act", lambda e, c=c, n=n: e.activation(out=zt[:, c, 0:n], in_=v[:, c, 0:n], func=AF.Silu,
                                                                 scale=vecs[:, V_LNG + c:V_LNG + c + 1], bias=vecs[:, V_LNB + c:V_LNB + c + 1]),
                         reads=[rv[c], self.rvecs], writes=[rz[c]])
                y, ry = v, rv
                pst2, rpst2 = self.ps[6], self.rps[6]
                pend = []
                for dc in range(KC):
                    wt, rwt = w2[nw2 % 2], rw2[nw2 % 2]
                    nw2 += 1
                    if tidx == 0:
                        self.load_w_cast(wt[:], rwt, pw2v[dc])
                        P.dma("sp", lambda e, wt=wt, dc=dc: e.dma_start(out=W2S[dc], in_=wt[:]), reads=[rwt], writes=[rW2S[dc]])
                    else:
                        P.dma("sp", lambda e, wt=wt, dc=dc: e.dma_start(out=wt[:], in_=W2S[dc]), reads=[rW2S[dc]], writes=[rwt])
                    py, rpy = self.ps[4 + dc % 2], self.rps[4 + dc % 2]
                    for kc in range(KC):
                        P.op("pe", lambda e, py=py, wt=wt, kc=kc, n=n: e.matmul(out=py[:, 0:n], lhsT=wt[:, kc * 128:(kc + 1) * 128], rhs=zt[:, kc, 0:n],
                                                                                 start=(kc == 0), stop=(kc == KC - 1)), reads=[rwt, rz[kc]], writes=[rpy])
                    tq, rtq = self.sqb[dc % 2], self.rsqb[dc % 2]
                    P.op("dve", lambda e, py=py, dc=dc, n=n: e.tensor_scalar(out=y[:, dc, 0:n], in0=py[:, 0:n], scalar1=vecs[:, V_BPW2 + dc:V_BPW2 + dc + 1],
                                                                              scalar2=None, op0=ALU.add), reads=[rpy, self.rvecs], writes=[ry[dc]])
                    P.op("act", lambda e, tq=tq, dc=dc, n=n: e.activation(out=tq[:, 0:n], in_=y[:, dc, 0:n], func=AF.Square), reads=[ry[dc]], writes=[rtq])
                    pend.append((tq, rtq, dc))
                    if len(pend) > 1:
                        self._stat_mm(pend.pop(0), pst2, rpst2, n)
                self._stat_mm(pend.pop(0), pst2, rpst2, n)
                self.rstd_from_sumsq(pst2, rpst2, n, rstd2, rrstd2, tmp[0], rtmp[0], NORM_EPS)
                for c in range(KC):
                    tq, rtq = tmp[c % 2], rtmp[c % 2]
                    P.op("dve", lambda e, tq=tq, c=c, n=n: e.tensor_tensor(out=tq[:, 0:n], in0=y[:, c, 0:n], in1=rstd2[:, 0:n], op=ALU.mult),
                         reads=[ry[c], rrstd2], writes=[rtq])
                    P.op("dve", lambda e, tq=tq, c=c, n=n, s=s: e.scalar_tensor_tensor(out=y[:, c, 0:n], in0=tq[:, 0:n], scalar=self.mv(i, s, j, 2, c), in1=xe[:, c, PAD:PAD + n],
                                                                                        op0=ALU.mult, op1=ALU.add),
                         reads=[rtq, rx[c], self.rmodv], writes=[ry[c]])
                P.dma("sp", lambda e, t0=t0, n=n: e.dma_start(out=XOv[:, :, t0:t0 + n], in_=y[:, :, 0:n]), reads=ry, writes=[rXO])
            self.run_jobs(self.bg_conv, len(self.bg_conv))
            if self.mod1_in_conv:
                while mq_cmp:
                    if len(mq_dma) == len(mq_cmp):
                        mq_dma.pop(0)()
                    mq_cmp.pop(0)()
                self.mod_derive(1, mraw, rmraw)

    def phase_attn(self, XI, rXI, XO, rXO):
        self.attn_a(XI, rXI)
        self.P.drain(); self.P.flush()
        self.attn_b()
        self.P.drain(); self.P.flush()
        self.attn_c(XI, rXI, XO, rXO)

    def attn_a(self, XI, rXI):
        nc, P = self.nc, self.P
        i, j = 1, 1
        XIv = XI.rearrange("(c p) t -> p c t", p=128)
        ropev = self.rope.rearrange("(k p) t -> p k t", p=128)
        wv_v = self.wv.rearrange("(b p) k -> b p k", p=128)
        ones = self.ones
        vecs = self.vecs
        lam_init = 0.8 - 0.6 * math.exp(-0.3 * 1)
        lam = self.lam
        with ExitStack() as st0:
            lt = self.sb(st0, "a_lt", [128, 128])
            rlt = Res("lt")
            for q in range(2):
                P.op("dve", lambda e, q=q: e.tensor_tensor(out=lt[:, q * 64:(q + 1) * 64], in0=vecs[:, V_LAM + q * 128:V_LAM + q * 128 + 64],
                                                            in1=vecs[:, V_LAM + q * 128 + 64:V_LAM + q * 128 + 128], op=ALU.mult),
                     reads=[self.rvecs], writes=[rlt])
                P.op("dve", lambda e, q=q: e.tensor_reduce(out=lam[:, q:q + 1], in_=lt[:, q * 64:(q + 1) * 64], axis=mybir.AxisListType.X, op=ALU.add),
                     reads=[rlt], writes=[self.rlam])
            P.op("act", lambda e: e.activation(out=lam[:, 2:4], in_=lam[:, 0:2], func=AF.Exp), reads=[self.rlam], writes=[self.rlam])
            P.op("dve", lambda e: e.tensor_tensor(out=lam[:, 4:5], in0=lam[:, 3:4], in1=lam[:, 2:3], op=ALU.subtract), reads=[self.rlam], writes=[self.rlam])
            P.op("dve", lambda e: e.tensor_scalar(out=lam[:, 5:6], in0=lam[:, 4:5], scalar1=-lam_init, scalar2=None, op0=ALU.add), reads=[self.rlam], writes=[self.rlam])
            P.op("dve", lambda e: e.tensor_scalar(out=lam[:, 6:7], in0=vecs[:, V_SUBLN:V_SUBLN + 1], scalar1=1.0 - lam_init, scalar2=None, op0=ALU.mult),
                 reads=[self.rvecs], writes=[self.rlam])
            P.drain(); P.flush()
        groups = [[(g * 512, 512, 0) for g in range(4)], [(g * 512, 512, 0) for g in range(4, 8)] + [(NL, 256, 1)]]
        for grp in groups:
            g0 = grp[0][0]
            gn = sum(t[1] for t in grp)
            gl = sum(t[1] for t in grp if t[2] == 0)
            with ExitStack() as st:
                hg = self.sb(st, "a_h", [128, KC, 2304], BF16)
                rhg = [mkres(KC, "h") for _ in grp]
                xt = self.sb(st, "a_x", [128, KC, 512])
                rx = mkres(KC, "x")
                rp = self.sb(st, "a_rope", [128, 4, 2048])
                rrp = Res("rope")
                tmp = [self.sb(st, f"a_t{k}", [128, 512]) for k in range(4)]
                rtmp = mkres(4, "tmp")
                rstd = self.sb(st, "a_rstd", [128, 512])
                rrstd = Res("rstd")
                wsl = [self.sb(st, f"a_w{k}", [128, 1, KC * 128], BF16) for k in range(2)]
                abf = [self.sb(st, f"a_ab{k}", [128, 512], BF16) for k in range(2)]
                rabf = mkres(2, "abf")
                permf = self.sb(st, "a_permf", [128, 128])
                permb = self.sb(st, "a_permb", [128, 128], BF16)
                rperm = Res("perm")
                P.dma("sp", lambda e: e.dma_start(out=permf[:], in_=self.perm_in[:, :]), writes=[rperm])
                P.op("dve", lambda e: e.tensor_copy(out=permb[:], in_=permf[:]), reads=[rperm], writes=[rperm])
                rwsl2 = [mkres(2, "w") for _ in range(2)]
                wvs = [self.sb(st, f"a_wv{k}", [128, KC * 512], BF16) for k in range(1)]
                rwvs = mkres(1, "wv")
                ost = [self.sb(st, f"a_o{k}", [128, 512], BF16) for k in range(4)]
                rost = mkres(4, "ost")
                P.dma("sp", lambda e, g0=g0, gl=gl: e.dma_start(out=rp[:, :, 0:gl], in_=ropev[:, :, g0:g0 + gl]), writes=[rrp])
                for ti, (t0, n, s) in enumerate(grp):
                    off = t0 - g0
                    P.dma("sp", lambda e, t0=t0, n=n: e.dma_start(out=xt[:, :, 0:n], in_=XIv[:, :, t0:t0 + n]), reads=[rXI], writes=rx)
                    pst, rpst = self.ps[6 + ti % 2], self.rps[6 + ti % 2]
                    for c in range(KC):
                        tq, rtq = self.sqb[c % 2], self.rsqb[c % 2]
                        P.op("act", lambda e, tq=tq, c=c, n=n: e.activation(out=tq[:, 0:n], in_=xt[:, c, 0:n], func=AF.Square), reads=[rx[c]], writes=[rtq])
                        P.op("pe", lambda e, tq=tq, c=c, n=n, pst=pst: e.matmul(out=pst[:, 0:n], lhsT=self.onesb[:], rhs=tq[:, 0:n], start=(c == 0), stop=(c == KC - 1)),
                             reads=[rtq, self.rconst], writes=[rpst])
                    self.rstd_from_sumsq(pst, rpst, n, rstd, rrstd, tmp[2], rtmp[2], NORM_EPS)
                    for c in range(KC):
                        tq, rtq = tmp[2 + c % 2], rtmp[2 + c % 2]
                        P.op("dve", lambda e, tq=tq, c=c, n=n: e.tensor_tensor(out=tq[:, 0:n], in0=xt[:, c, 0:n], in1=rstd[:, 0:n], op=ALU.mult),
                             reads=[rx[c], rrstd], writes=[rtq])
                        P.op("act", lambda e, tq=tq, c=c, n=n, s=s, off=off: e.activation(out=hg[:, c, off:off + n], in_=tq[:, 0:n], func=AF.Identity,
                                                                                           scale=self.mv(i, s, j, 0, c), bias=self.mv(i, s, j, 1, c)),
                             reads=[rtq, self.rmodv], writes=[rhg[ti][c]])
                nw = 0
                no = 0
                k2 = 0
                pendq = None
                for which, (wa, dst, rdst, tc, tsn) in enumerate(((self.wk, self.KT, self.rKT, 2, 3), (self.wq, self.QT, self.rQT, 0, 1))):
                    wav = wa.rearrange("(o p) k -> o p k", p=128)
                    def issue_w(oc_, slot):
                        self.load_w_cast(wsl[slot][:, 0, :], rwsl2[slot][0], wav[oc_])
                    issue_w(0, nw % 2)
                    for oc in range(NHEAD):
                        wt, rwt2 = wsl[nw % 2], rwsl2[nw % 2]
                        nw += 1
                        if oc + 1 < NHEAD:
                            issue_w(oc + 1, nw % 2)
                        for ti, (t0, n, s) in enumerate(grp):
                            if which == 1 and s == 1:
                                continue
                            off = t0 - g0
                            pa, rpa = self.ps[k2 % 2], self.rps[k2 % 2]
                            pb, rpb = self.ps[2 + k2 % 2], self.rps[2 + k2 % 2]
                            ta, rta = tmp[k2 % 2], rtmp[k2 % 2]
                            tb, rtb = tmp[2 + k2 % 2], rtmp[2 + k2 % 2]
                            k2 += 1
                            o, ro = ost[no % 4], rost[no % 4]
                            no += 1
                            for kc in range(KC):
                                P.op("pe", lambda e, pa=pa, wt=wt, kc=kc, n=n, off=off: e.matmul(out=pa[:, 0:n], lhsT=wt[:, 0, kc * 128:(kc + 1) * 128], rhs=hg[:, kc, off:off + n],
                                                                                                  start=(kc == 0), stop=(kc == KC - 1)), reads=[rwt2[0], rhg[ti][kc]], writes=[rpa])
                            if s == 0:
                                ab, rab = abf[k2 % 2], rabf[k2 % 2]
                                P.op("act", lambda e, ab=ab, pa=pa, n=n: e.activation(out=ab[:, 0:n], in_=pa[:, 0:n], func=AF.Copy), reads=[rpa], writes=[rab])
                            else:
                                ab, rab = None, None
                            if pendq is not None:
                                pendq()

                            def post(s=s, ab=ab, rab=rab, pa=pa, rpa=rpa, pb=pb, rpb=rpb, ta=ta, rta=rta, tb=tb, rtb=rtb, o=o, ro=ro, n=n, off=off,
                                     tc=tc, tsn=tsn, dst=dst, rdst=rdst, oc=oc, t0=t0):
                                if s == 0:
                                    P.op("pe", lambda e: e.matmul(out=pb[:, 0:n], lhsT=permb[:], rhs=ab[:, 0:n], start=True, stop=True),
                                         reads=[rperm, rab], writes=[rpb])
                                    P.op("dve", lambda e: e.tensor_tensor(out=ta[:, 0:n], in0=pa[:, 0:n], in1=rp[:, tc, off:off + n], op=ALU.mult),
                                         reads=[rpa, rrp], writes=[rta])
                                    P.op("dve", lambda e: e.tensor_tensor(out=tb[:, 0:n], in0=pb[:, 0:n], in1=rp[:, tsn, off:off + n], op=ALU.mult),
                                         reads=[rpb, rrp], writes=[rtb])
                                    P.op("pool", lambda e: e.tensor_tensor(out=o[:, 0:n], in0=ta[:, 0:n], in1=tb[:, 0:n], op=ALU.add),
                                         reads=[rta, rtb], writes=[ro])
                                else:
                                    P.op("act", lambda e: e.activation(out=o[:, 0:n], in_=pa[:, 0:n], func=AF.Copy), reads=[rpa], writes=[ro])
                                P.dma("sp", lambda e: e.dma_start(out=dst[oc * 128:(oc + 1) * 128, t0:t0 + n], in_=o[:, 0:n]), reads=[ro], writes=[rdst])
                            pendq = post
                    if pendq is not None:
                        pendq()
                        pendq = None
                nblk = gn // 128
                for nb in range(4):
                    wt, rwt = wvs[0], rwvs[0]
                    self.load_w_cast(wt[:], rwt, wv_v[nb])
                    for tb in range(nblk):
                        ti = min(tb // 4, len(grp) - 1)
                        pv, rpv = self.ps[4 + tb % 2], self.rps[4 + tb % 2]
                        o, ro = ost[no % 4], rost[no % 4]
                        no += 1
                        for kc in range(KC):
                            P.op("pe", lambda e, pv=pv, wt=wt, kc=kc, tb=tb: e.matmul(out=pv[:, :], lhsT=hg[:, kc, tb * 128:(tb + 1) * 128], rhs=wt[:, kc * 512:(kc + 1) * 512],
                                                                                      start=(kc == 0), stop=(kc == KC - 1)), reads=[rwt, rhg[ti][kc]], writes=[rpv])
                        if tb % 2 == 0:
                            P.op("act", lambda e, o=o, pv=pv: e.activation(out=o[:, :], in_=pv[:, :], func=AF.Copy), reads=[rpv], writes=[ro])
                        else:
                            P.op("dve", lambda e, o=o, pv=pv: e.tensor_copy(out=o[:, :], in_=pv[:, :]), reads=[rpv], writes=[ro])
                        r0 = g0 + tb * 128
                        P.dma("sp", lambda e, o=o, r0=r0, nb=nb: e.dma_start(out=self.VV[r0:r0 + 128, nb * 512:(nb + 1) * 512], in_=o[:, :]), reads=[ro], writes=[self.rVV])
                P.drain(); P.flush()

    def attn_b(self):
        nc, P = self.nc, self.P
        NKB = NT // 128
        NQT = NL // 512
        VVv = self.VV.rearrange("(kb p) e -> p kb e", p=128)
        lam = self.lam
        with ExitStack() as st:
            kT = [self.sb(st, f"b_k{k}", [128, NT], BF16) for k in range(2)]
            qz = [[self.sb(st, f"b_q{c}{k}", [128, NL], BF16) for k in range(2)] for c in range(2)]
            rqz = Res("qz")
            for k in range(2):
                P.op("dve", lambda e, k=k: e.memset(qz[0][k][64:128, :], 0.0), writes=[rqz])
                P.op("dve", lambda e, k=k: e.memset(qz[1][k][0:64, :], 0.0), writes=[rqz])
            vh = [self.sb(st, f"b_v{k}", [128, NKB, 128], BF16) for k in range(2)]
            rkqv = mkres(2, "kqv")
            pt = [self.sb(st, f"b_p{k}", [128, 512], BF16) for k in range(4)]
            rpt = mkres(4, "p")
            ob = [self.sb(st, f"b_o{k}", [128, 512]) for k in range(2)]
            rob = mkres(2, "o")
            t2 = [self.sb(st, f"b_t{k}", [128, 512]) for k in range(3)]
            rt2 = mkres(3, "t")
            osb = [self.sb(st, f"b_ob{k}", [128, 512], BF16) for k in range(2)]
            rosb = mkres(2, "ob")
            zc = self.sb(st, "b_zc", [128, 512])
            rzc = Res("zc")
            onesb = self.sb(st, "b_ones", [128, 128], BF16)
            ronesb = Res("onesb")
            P.op("dve", lambda e: e.memset(onesb[:], 1.0), writes=[ronesb])
            ones = self.ones
            pS = [self.ps[0], self.ps[1]]
            rpS = [self.rps[0], self.rps[1]]
            pO = [self.ps[2], self.ps[3]]
            rpO = [self.rps[2], self.rps[3]]
            pZ = [self.ps[4], self.ps[5]]
            rpZ = [self.rps[4], self.rps[5]]
            pN = [self.ps[6], self.ps[7]]
            rpN = [self.rps[6], self.rps[7]]
            npt = 0
            tcount = 0
            pending = None

            def finalize(item):
                o, ro, hd, qt, k = item
                sq, rsq = t2[2], rt2[2]
                sb_, rsb_ = self.sqb[k], self.rsqb[k]
                P.op("dve", lambda e: e.tensor_tensor(out=sb_[:], in0=o[:], in1=o[:], op=ALU.mult), reads=[ro], writes=[rsb_])
                P.op("pe", lambda e: e.matmul(out=pN[k][:], lhsT=self.onesb[:], rhs=sb_[:], start=True, stop=True), reads=[rsb_, self.rconst], writes=[rpN[k]])
                P.op("act", lambda e: e.activation(out=sq[:], in_=pN[k][:], func=AF.Ln, bias=self.epsv(SUBLN_EPS), scale=1.0 / 128), reads=[rpN[k], self.rconst], writes=[rsq])
                P.op("act", lambda e: e.activation(out=sq[:], in_=sq[:], func=AF.Exp, scale=-0.5), reads=[rsq], writes=[rsq])
                P.op("dve", lambda e: e.tensor_tensor(out=o[:], in0=o[:], in1=sq[:], op=ALU.mult), reads=[ro, rsq], writes=[ro])
                ot, rot = osb[k], rosb[k]
                P.op("dve", lambda e: e.tensor_scalar(out=ot[:], in0=o[:], scalar1=lam[:, 6:7], scalar2=None, op0=ALU.mult), reads=[ro, self.rlam], writes=[rot])
                P.dma("sp", lambda e: e.dma_start(out=self.ON[hd * 128:(hd + 1) * 128, qt * 512:(qt + 1) * 512], in_=ot[:]), reads=[rot], writes=[self.rON])

            for hd in range(NHEAD):
                b = hd % 2
                self.run_jobs(self.bg_attnb, 10)
                P.dma("sp", lambda e, b=b, hd=hd: e.dma_start(out=kT[b][:], in_=self.KT[hd * 128:(hd + 1) * 128, :]), reads=[self.rKT], writes=[rkqv[b]])
                P.dma("sp", lambda e, b=b, hd=hd: e.dma_start(out=qz[0][b][0:64, :], in_=self.QT[hd * 128:hd * 128 + 64, :]), reads=[self.rQT], writes=[rkqv[b]])
                P.dma("sp", lambda e, b=b, hd=hd: e.dma_start(out=qz[1][b][64:128, :], in_=self.QT[hd * 128 + 64:(hd + 1) * 128, :]), reads=[self.rQT], writes=[rkqv[b]])
                P.dma("sp", lambda e, b=b, hd=hd: e.dma_start(out=vh[b][:], in_=VVv[:, :, hd * 128:(hd + 1) * 128]), reads=[self.rVV], writes=[rkqv[b]])
                for qt in range(NQT):
                    k = tcount % 2
                    tcount += 1
                    q0 = qt * 512
                    prev = None
                    for kb in range(NKB + 1):
                        if kb < NKB:
                            cur = []
                            for c in range(2):
                                p, rp_ = pt[npt % 4], rpt[npt % 4]
                                npt += 1
                                P.op("pe", lambda e, c=c, b=b, kb=kb, q0=q0: e.matmul(out=pS[c][:], lhsT=kT[b][:, kb * 128:(kb + 1) * 128],
                                                                                     rhs=qz[c][b][:, q0:q0 + 512], start=True, stop=True),
                                     reads=[rkqv[b], rqz], writes=[rpS[c]])
                                cur.append((p, rp_))
                            for c in range(2):
                                p, rp_ = cur[c]
                                P.op("act", lambda e, c=c, p=p: e.activation(out=p[:], in_=pS[c][:], func=AF.Exp), reads=[rpS[c]], writes=[rp_])
                        if prev is not None:
                            kp = kb - 1
                            for c in range(2):
                                p, rp_ = prev[c]
                                P.op("pe", lambda e, c=c, p=p, b=b, kp=kp: e.matmul(out=pO[c][:], lhsT=vh[b][:, kp, :], rhs=p[:], start=(kp == 0), stop=(kp == NKB - 1)),
                                     reads=[rkqv[b], rp_], writes=[rpO[c]])
                            for c in range(2):
                                p, rp_ = prev[c]
                                P.op("pe", lambda e, c=c, p=p, kp=kp: e.matmul(out=pZ[c][:], lhsT=onesb[:], rhs=p[:], start=(kp == 0), stop=(kp == NKB - 1)),
                                     reads=[ronesb, rp_], writes=[rpZ[c]])
                        prev = cur if kb < NKB else None
                        if kb == 12 and pending is not None:
                            finalize(pending)
                            pending = None
                    o, ro = ob[k], rob[k]
                    P.op("dve", lambda e, o=o: e.tensor_copy(out=o[:], in_=pO[0][:]), reads=[rpO[0]], writes=[ro])
                    P.op("dve", lambda e: e.tensor_copy(out=t2[1][:], in_=pO[1][:]), reads=[rpO[1]], writes=[rt2[1]])
                    P.op("dve", lambda e: e.tensor_copy(out=t2[0][:], in_=pZ[0][:]), reads=[rpZ[0]], writes=[rt2[0]])
                    P.op("dve", lambda e: e.tensor_copy(out=zc[:], in_=pZ[1][:]), reads=[rpZ[1]], writes=[rzc])
                    P.op("dve", lambda e: e.reciprocal(out=t2[0][:], in_=t2[0][:]), reads=[rt2[0]], writes=[rt2[0]])
                    P.op("dve", lambda e: e.reciprocal(out=zc[:], in_=zc[:]), reads=[rzc], writes=[rzc])
                    P.op("dve", lambda e, o=o: e.tensor_tensor(out=o[:], in0=o[:], in1=t2[0][:], op=ALU.mult), reads=[ro, rt2[0]], writes=[ro])
                    P.op("dve", lambda e: e.tensor_tensor(out=t2[1][:], in0=t2[1][:], in1=zc[:], op=ALU.mult), reads=[rt2[1], rzc], writes=[rt2[1]])
                    P.op("dve", lambda e, o=o: e.scalar_tensor_tensor(out=o[:], in0=t2[1][:], scalar=lam[:, 5:6], in1=o[:], op0=ALU.mult, op1=ALU.add),
                         reads=[rt2[1], ro, self.rlam], writes=[ro])
                    pending = (o, ro, hd, qt, k)
            finalize(pending)
            self.run_jobs(self.bg_attnb, len(self.bg_attnb))

    def attn_c(self, XI, rXI, XO, rXO):
        nc, P = self.nc, self.P
        i, j = 1, 1
        XIv = XI.rearrange("(c p) t -> p c t", p=128)
        XOv = XO.rearrange("(c p) t -> p c t", p=128)
        ONv = self.ON.rearrange("(c p) t -> p c t", p=128)
        wov = self.wo.rearrange("(o p) k -> o p k", p=128)
        ones = self.ones
        with ExitStack() as st:
            woa = self.sb(st, "c3_wo", [128, KC, KC * 128], BF16)
            rwoa = Res("wo")
            xt = self.sb(st, "c3_x", [128, KC, 512])
            rx = mkres(KC, "x")
            on = [self.sb(st, f"c3_on{k}", [128, KC, 512], BF16) for k in range(2)]
            ron = mkres(2, "on")
            y = self.sb(st, "c3_y", [128, KC, 512])
            ry = mkres(KC, "y")
            tmp = [self.sb(st, f"c3_t{k}", [128, 512]) for k in range(4)]
            rtmp = mkres(4, "tmp")
            rstd2 = self.sb(st, "c3_rstd2", [128, 512])
            rrstd2 = Res("rstd2")
            rwo_l = mkres(KC, "wo")
            for dc in range(KC):
                self.load_w_cast(woa[:, dc, :], rwo_l[dc], wov[dc])
            s = 0
            for g in range(8):
                t0, n = g * 512, 512
                ont, ront = on[g % 2], ron[g % 2]
                P.dma("sp", lambda e, ont=ont, t0=t0: e.dma_start(out=ont[:], in_=ONv[:, :, t0:t0 + 512]), reads=[self.rON], writes=[ront])
                P.dma("sp", lambda e, t0=t0: e.dma_start(out=xt[:], in_=XIv[:, :, t0:t0 + 512]), reads=[rXI], writes=rx)
                pst2, rpst2 = self.ps[6], self.rps[6]
                pend = []
                for dc in range(KC):
                    py, rpy = self.ps[4 + dc % 2], self.rps[4 + dc % 2]
                    for kc in range(KC):
                        P.op("pe", lambda e, py=py, dc=dc, kc=kc, ont=ont: e.matmul(out=py[:], lhsT=woa[:, dc, kc * 128:(kc + 1) * 128], rhs=ont[:, kc, :],
                                                                                   start=(kc == 0), stop=(kc == KC - 1)), reads=[rwo_l[dc], ront], writes=[rpy])
                    tq, rtq = self.sqb[dc % 2], self.rsqb[dc % 2]
                    P.op("dve", lambda e, py=py, dc=dc: e.tensor_copy(out=y[:, dc, :], in_=py[:]), reads=[rpy], writes=[ry[dc]])
                    P.op("act", lambda e, tq=tq, dc=dc: e.activation(out=tq[:], in_=y[:, dc, :], func=AF.Square), reads=[ry[dc]], writes=[rtq])
                    pend.append((tq, rtq, dc))
                    if len(pend) > 1:
                        self._stat_mm(pend.pop(0), pst2, rpst2, n)
                self._stat_mm(pend.pop(0), pst2, rpst2, n)
                self.rstd_from_sumsq(pst2, rpst2, n, rstd2, rrstd2, tmp[0], rtmp[0], NORM_EPS)
                for c in range(KC):
                    tq, rtq = tmp[c % 2], rtmp[c % 2]
                    P.op("dve", lambda e, tq=tq, c=c: e.tensor_tensor(out=tq[:], in0=y[:, c, :], in1=rstd2[:], op=ALU.mult), reads=[ry[c], rrstd2], writes=[rtq])
                    P.op("dve", lambda e, tq=tq, c=c: e.scalar_tensor_tensor(out=y[:, c, :], in0=tq[:], scalar=self.mv(i, s, j, 2, c), in1=xt[:, c, :],
                                                                              op0=ALU.mult, op1=ALU.add), reads=[rtq, rx[c], self.rmodv], writes=[ry[c]])
                P.dma("sp", lambda e, t0=t0: e.dma_start(out=XOv[:, :, t0:t0 + 512], in_=y[:]), reads=ry, writes=[rXO])


def _tile_rows(w, ncol_chunk=128):
    K, N = w.shape
    kc, oc = K // 128, N // 128
    return np.ascontiguousarray(w.reshape(kc, 128, oc, 128).transpose(2, 1, 0, 3).reshape(oc * 128, kc * 128))


def _pvec(v):
    return np.ascontiguousarray(v.reshape(-1, 128).T)


def prep_shared(inp):
    sh = {}
    sh["ident"] = np.eye(128, dtype=np.float32)
    aw = inp["ada_w"]
    sh["adaw"] = np.concatenate([_tile_rows(aw[i]) for i in range(2)], axis=0)
    for i in range(2):
        for w, (ki, ko) in ((1, ("ffn1_w_in", "ffn1_w_out")), (2, ("ffn2_w_in", "ffn2_w_out"))):
            wi = inp[ki][i]
            a = _tile_rows(wi[:, :FF]).reshape(FC, 128, KC * 128)
            u = _tile_rows(wi[:, FF:]).reshape(FC, 128, KC * 128)
            sh[f"win{i}{w}"] = np.ascontiguousarray(np.concatenate([a, u], axis=2).reshape(FC * 128, 2 * KC * 128))
            sh[f"wout{i}{w}"] = _tile_rows(inp[ko][i])
    pw1 = inp["conv_w_pw1"][0]
    a = _tile_rows(pw1[:, :D]).reshape(KC, 128, KC * 128)
    g = _tile_rows(pw1[:, D:]).reshape(KC, 128, KC * 128)
    sh["pw1"] = np.ascontiguousarray(np.concatenate([a, g], axis=2).reshape(KC * 128, 2 * KC * 128))
    sh["pw2"] = _tile_rows(inp["conv_w_pw2"][0])
    wqkv = inp["attn_w_qkv"][0]
    wq, wk, wv = wqkv[:, :D], wqkv[:, D:2 * D], wqkv[:, 2 * D:]
    r = np.arange(D)
    i32 = r % 32
    partner = np.where(i32 < 16, r + 16, r - 16)
    sh["wq"] = _tile_rows(wq)
    sh["wk"] = _tile_rows(wk)
    pm = np.zeros((128, 128), np.float32)
    m = np.arange(128)
    pm[np.where((m % 32) < 16, m + 16, m - 16), m] = 1.0
    sh["perm"] = pm
    sh["wv"] = np.ascontiguousarray(wv.reshape(KC, 128, 4, 512).transpose(2, 1, 0, 3).reshape(4 * 128, KC * 512))
    sh["wo"] = _tile_rows(inp["attn_w_o"][0])
    t = np.arange(NL)
    inv_freq = (10000.0 ** (-np.arange(0, 32, 2, dtype=np.float32) / 32)).astype(np.float32)
    rr = np.arange(128)
    jj = rr % 64
    pos = np.where((jj < 32)[:, None], (t // 64)[None, :], (t % 64)[None, :]).astype(np.float32)
    fr = inv_freq[(jj % 32) % 16][:, None]
    ang = (pos * fr).astype(np.float32)
    cs = np.cos(ang).astype(np.float32)
    sn = np.sin(ang).astype(np.float32)
    sgn = np.where(((jj % 32) < 16)[:, None], -1.0, 1.0).astype(np.float32)
    sn = sn * sgn
    sh["rope"] = np.ascontiguousarray(np.concatenate([cs * 0.125, sn * 0.125, cs, sn], axis=0).astype(np.float32))
    return sh


def prep_vecs(inp, b):
    v = np.zeros((128, NV), np.float32)
    cc = np.stack([inp["c"][b], inp["c_ctx"]], axis=0)
    v[:, V_CC:V_CC + 32] = cc.reshape(2, KC, 128).transpose(2, 1, 0).reshape(128, 32)
    v[:, V_ADAB:V_ADAB + 288] = inp["ada_b"].reshape(2, 144, 128).transpose(2, 0, 1).reshape(128, 288)
    v[:, V_NPRE:V_NPRE + 96] = inp["norm_pre"].reshape(2, 3, KC, 128).transpose(3, 0, 1, 2).reshape(128, 96)
    v[:, V_NPOST:V_NPOST + 96] = inp["norm_post"].reshape(2, 3, KC, 128).transpose(3, 0, 1, 2).reshape(128, 96)
    v[:, V_BPW1:V_BPW1 + 32] = inp["conv_b_pw1"][0].reshape(2, KC, 128).transpose(2, 0, 1).reshape(128, 32)
    v[:, V_WDW:V_WDW + 496] = inp["conv_w_dw"][0].reshape(CW, KC, 128).transpose(2, 1, 0).reshape(128, 496)
    v[:, V_BDW:V_BDW + 16] = _pvec(inp["conv_b_dw"][0])
    v[:, V_LNG:V_LNG + 16] = _pvec(inp["conv_ln_g"][0])
    v[:, V_LNB:V_LNB + 16] = _pvec(inp["conv_ln_b"][0])
    v[:, V_BPW2:V_BPW2 + 16] = _pvec(inp["conv_b_pw2"][0])
    lam = np.concatenate([inp["attn_lambda_q1"][0], inp["attn_lambda_k1"][0], inp["attn_lambda_q2"][0], inp["attn_lambda_k2"][0]])
    v[:, V_LAM:V_LAM + 256] = lam[None, :]
    v[:, V_SUBLN] = inp["attn_subln_g"][0]
    return v


_CACHE = {}


def run(inp, phases=None, dbg=None, cores=NCORES, trace=False):
    key = (None if phases is None else tuple(phases), dbg)
    if key not in _CACHE:
        b = Builder(phases, dbg)
        nc = b.build()
        _CACHE[key] = (nc, b.in_names)
    nc, in_names = _CACHE[key]
    sh = prep_shared(inp)
    in_maps = []
    for b in range(cores):
        m = dict(sh)
        m["x"] = np.ascontiguousarray(inp["x"][b])
        m["ctx"] = np.ascontiguousarray(inp["ctx"][b])
        m["vecs"] = prep_vecs(inp, b)
        in_maps.append({k: m[k] for k in in_names})
    return run_bass_kernel_spmd(nc, in_maps, core_ids=list(range(cores)), trace=trace)


def kernel(**inputs):
    inp = {k: np.asarray(v) for k, v in inputs.items()}
    res = run(inp)
    return np.stack([res.results[b]["out"] for b in range(NCORES)], axis=0)
```

```python
import math
from contextlib import ExitStack

import numpy as np
import concourse.bass as bass
import concourse.mybir as mybir
from concourse.bass_utils import run_bass_kernel_spmd

F32 = mybir.dt.float32
BF16 = mybir.dt.bfloat16
AF = mybir.ActivationFunctionType
ALU = mybir.AluOpType

NCORES = 8
D = 2048
NL = 4096
NCTX = 256
NT = NL + NCTX
FF = 5632
KC = D // 128
FC = FF // 128
CW = 31
PAD = 15
NHEAD = 16
NORM_EPS = 1e-6
SUBLN_EPS = 1e-5
LN_EPS = 1e-5

V_CC = 0
V_ADAB = V_CC + 32
V_NPRE = V_ADAB + 288
V_NPOST = V_NPRE + 96
V_BPW1 = V_NPOST + 96
V_WDW = V_BPW1 + 32
V_BDW = V_WDW + 496
V_LNG = V_BDW + 16
V_LNB = V_LNG + 16
V_BPW2 = V_LNB + 16
V_LAM = V_BPW2 + 16
V_SUBLN = V_LAM + 256
NV = V_SUBLN + 1


class Res:
    __slots__ = ("name", "w", "rs", "excl")

    def __init__(self, name="", excl=False):
        self.name = name
        self.excl = excl
        self.w = None
        self.rs = {}


def mkres(n, name=""):
    return [Res(f"{name}{i}") for i in range(n)]


class Eng:
    def __init__(self, name, sem):
        self.name = name
        self.sem = sem
        self.cnt = 0
        self.known = {}
        self.ops = []
        self.ring = []
        self.ring_i = 0


class Prog:
    def __init__(self, nc, stack):
        self.nc = nc
        self.sems = {}
        self.E = {}
        for name in ("pe", "act", "dve", "pool", "sp"):
            s = stack.enter_context(nc.semaphore(f"prog_{name}"))
            self.sems[name] = s
            self.E[name] = Eng(name, s)
        for q, n in (("sp", 24), ("pool", 24), ("act", 8)):
            for i in range(n):
                key = f"dma_{q}_{i}"
                s = stack.enter_context(nc.semaphore(key))
                self.sems[key] = s
                self.E[q].ring.append([key, 0])

    def _need(self, eng, toks):
        for key, val in toks:
            if key == eng.name and eng.name == "pe":
                continue
            if eng.known.get(key, 0) < val:
                eng.known[key] = val
                eng.ops.append(("wait", key, val))

    def _deps(self, eng, reads, writes, same_eng_war=False):
        toks = []
        for r in reads:
            if r.w is not None:
                toks.append(r.w)
        for r in writes:
            if r.w is not None and r.w[0] != eng.name:
                toks.append(r.w)
            for k, v in r.rs.items():
                if k != eng.name:
                    toks.append((k, v))
        self._need(eng, toks)

    def _commit(self, tok, reads, writes):
        for r in reads:
            if r.rs.get(tok[0], 0) < tok[1]:
                r.rs[tok[0]] = tok[1]
        for r in writes:
            r.w = tok
            r.rs = {}

    def op(self, eng, fn, reads=(), writes=()):
        e = self.E[eng]
        if any(r.excl for r in reads):
            writes = list(writes) + [r for r in reads if r.excl]
        self._deps(e, reads, writes)
        e.cnt += 1
        tok = (eng, e.cnt)
        e.ops.append(("op", fn, eng, 1))
        self._commit(tok, reads, writes)

    def dma(self, q, fn, reads=(), writes=()):
        e = self.E[q]
        toks = []
        for r in reads:
            if r.w is not None:
                toks.append(r.w)
        for r in writes:
            if r.w is not None:
                toks.append(r.w)
            toks.extend(r.rs.items())
        slot = e.ring[e.ring_i]
        e.ring_i = (e.ring_i + 1) % len(e.ring)
        if slot[1] > 0:
            toks.append((slot[0], slot[1]))
        self._need(e, toks)
        slot[1] += 16
        tok = (slot[0], slot[1])
        e.ops.append(("op", fn, slot[0], 16))
        self._commit(tok, reads, writes)

    def drain(self, q="sp"):
        e = self.E[q]
        toks = []
        for name, o in self.E.items():
            if o.cnt > 0 and name != q:
                toks.append((name, o.cnt))
            for key, val in o.ring:
                if val > 0:
                    toks.append((key, val))
        self._need(e, toks)

    def flush(self, name="blk"):
        nc = self.nc
        sems = self.sems
        with nc.Block() as block:
            def replay(handle, ops):
                for o in ops:
                    if o[0] == "wait":
                        handle.wait_ge(sems[o[1]], o[2])
                    else:
                        o[1](handle).then_inc(sems[o[2]], o[3])

            @block.tensor
            def _(h):
                replay(h, self.E["pe"].ops)

            @block.scalar
            def _(h):
                replay(h, self.E["act"].ops)

            @block.vector
            def _(h):
                replay(h, self.E["dve"].ops)

            @block.gpsimd
            def _(h):
                replay(h, self.E["pool"].ops)

            @block.sync
            def _(h):
                replay(h, self.E["sp"].ops)
        for e in self.E.values():
            e.ops = []


class Builder:
    def __init__(self, phases=None, dbg=None):
        self.phases = phases
        self.dbg = dbg
        self.nc = bass.Bass("TRN2", target_bir_lowering=False)
        self.stack = ExitStack()
        self.in_names = []

    def din(self, name, shape, dt=F32):
        self.in_names.append(name)
        return self.nc.dram_tensor(name, list(shape), dt, kind="ExternalInput").ap()

    def dint(self, name, shape, dt=F32):
        return self.nc.dram_tensor(name, list(shape), dt, kind="Internal").ap()

    def sb(self, st, name, shape, dt=F32):
        self._uid = getattr(self, "_uid", 0) + 1
        return st.enter_context(self.nc.sbuf_tensor(f"sb{self._uid}_{name}", list(shape), dt))

    def build(self):
        nc = self.nc
        st = self.stack
        with st:
            self.P = Prog(nc, st)
            self._declare()
            self._globals(st)
            self._run_phases()
        return nc

    def _declare(self):
        self.x_in = self.din("x", [NL, D])
        self.ctx_in = self.din("ctx", [NCTX, D])
        self.vecs_in = self.din("vecs", [128, NV])
        self.ident_in = self.din("ident", [128, 128])
        self.adaw = self.din("adaw", [2 * 144 * 128, 2048])
        self.win = {}
        self.wout = {}
        for i in range(2):
            for w in (1, 2):
                self.win[(i, w)] = self.din(f"win{i}{w}", [FC * 128, 2 * KC * 128])
                self.wout[(i, w)] = self.din(f"wout{i}{w}", [KC * 128, FC * 128])
        self.pw1 = self.din("pw1", [KC * 128, 2 * KC * 128])
        self.pw2 = self.din("pw2", [KC * 128, KC * 128])
        self.wq = self.din("wq", [KC * 128, KC * 128])
        self.wk = self.din("wk", [KC * 128, KC * 128])
        self.perm_in = self.din("perm", [128, 128])
        self.wv = self.din("wv", [4 * 128, KC * 512])
        self.wo = self.din("wo", [KC * 128, KC * 128])
        self.rope = self.din("rope", [4 * 128, NL])
        self.out = self.nc.dram_tensor("out", [NL, D], F32, kind="ExternalOutput").ap()
        self.XA = self.dint("XA", [D, NT])
        self.XB = self.dint("XB", [D, NT])
        self.QT = self.dint("QT", [D, NL], BF16)
        self.KT = self.dint("KT", [D, NT], BF16)
        self.VV = self.dint("VV", [NT, D], BF16)
        self.ON = self.dint("ON", [D, NL], BF16)
        self.rXA = Res("XA")
        self.rXB = Res("XB")
        self.rQT = Res("QT")
        self.rKT = Res("KT")
        self.rVV = Res("VV")
        self.rON = Res("ON")
        if self.dbg is not None:
            self.dbg_out = self.nc.dram_tensor("dbg", [D, NT], F32, kind="ExternalOutput").ap()
            self.dbgv_out = self.nc.dram_tensor("dbgv", [128, 576], F32, kind="ExternalOutput").ap()

    def _globals(self, st):
        nc, P = self.nc, self.P
        self.ps = [st.enter_context(nc.psum_tensor(f"ps{i}", [128, 512], F32)) for i in range(8)]
        self.rps = [Res(f"ps{i}", excl=True) for i in range(8)]
        self.vecs = self.sb(st, "vecs", [128, NV])
        self.rvecs = Res("vecs")
        self.ident = self.sb(st, "ident", [128, 128])
        self.ones = self.sb(st, "ones", [128, 128])
        self.rconst = Res("const")
        self.modv = self.sb(st, "modv", [128, 2 * 2 * 3 * 3 * 16])
        self.rmodv = Res("modv")
        self.scb = self.sb(st, "scb", [128, 32], BF16)
        self.rscb = Res("scb")
        self.lam = self.sb(st, "lamv", [128, 8])
        self.rlam = Res("lam")
        vecs, ident, ones = self.vecs, self.ident, self.ones
        P.dma("sp", lambda e: e.dma_start(out=vecs[:], in_=self.vecs_in[:, :]), writes=[self.rvecs])
        P.dma("sp", lambda e: e.dma_start(out=ident[:], in_=self.ident_in[:, :]), writes=[self.rconst])
        P.op("dve", lambda e: e.memset(ones[:], 1.0), writes=[self.rconst])
        self.onesb = self.sb(st, "onesb", [128, 128], BF16)
        P.op("dve", lambda e: e.memset(self.onesb[:], 1.0), writes=[self.rconst])
        self.sqb = [self.sb(st, f"sqb{k}", [128, 512], BF16) for k in range(2)]
        self.rsqb = mkres(2, "sqb")
        self.setup_eps(st)

    def mv(self, i, s, j, kind, c):
        off = ((((i * 2 + s) * 3 + j) * 3 + kind) * 16) + c
        return self.modv[:, off:off + 1]

    def vcol(self, off, n=1):
        return self.vecs[:, off:off + n]

    def _run_phases(self):
        P = self.P
        ph = self.phases
        def want(name):
            return ph is None or name in ph
        if want("tin"):
            self.phase_transpose_in(self.XA, self.rXA)
            P.drain(); P.flush()
        self.mod1_in_conv = want("conv") and want("mod")
        if want("mod"):
            self.phase_mod((0,) if self.mod1_in_conv else (0, 1))
            P.drain(); P.flush()
        full = ph is None
        self.bg_conv = []
        self.bg_attnb = []
        bg_f20 = []
        if full:
            p = self.ffn_prep(0, 2); p["pre"] = True; self.bg_conv = p["jobs"]
            p = self.ffn_prep(1, 1); p["pre"] = True; bg_f20 = p["jobs"]
            p = self.ffn_prep(1, 2); p["pre"] = True; self.bg_attnb = p["jobs"]
        if want("ffn1_0"):
            self.phase_ffn(0, 1, 0, self.XA, self.rXA, self.XB, self.rXB, True)
            P.drain(); P.flush()
        if want("conv"):
            self.phase_conv(self.XB, self.rXB, self.XA, self.rXA)
            P.drain(); P.flush()
        if want("ffn2_0"):
            self.phase_ffn(0, 2, 2, self.XA, self.rXA, self.XB, self.rXB, True, bg=bg_f20)
            P.drain(); P.flush()
        if want("ffn1_1"):
            self.phase_ffn(1, 1, 0, self.XB, self.rXB, self.XA, self.rXA, True)
            P.drain(); P.flush()
        if want("attn"):
            self.phase_attn(self.XA, self.rXA, self.XB, self.rXB)
            P.drain(); P.flush()
        if want("ffn2_1"):
            self.phase_ffn(1, 2, 2, self.XB, self.rXB, self.XA, self.rXA, False)
            P.drain(); P.flush()
        if self.dbg is not None:
            src, rsrc = (self.XA, self.rXA) if self.dbg == "A" else (self.XB, self.rXB)
            rdbg = Res("dbg")
            P.dma("sp", lambda e: e.dma_start(out=self.dbgv_out[:, :], in_=self.modv[:]), reads=[self.rmodv], writes=[Res()])
            for c in range(KC):
                P.dma("sp", lambda e, c=c: e.dma_start(out=self.dbg_out[c * 128:(c + 1) * 128, :], in_=src[c * 128:(c + 1) * 128, :]), reads=[rsrc], writes=[rdbg])
        if want("tout"):
            self.phase_transpose_out(self.XA, self.rXA)
        P.drain(); P.flush()

    def phase_transpose_in(self, XO, rXO):
        nc, P = self.nc, self.P
        with ExitStack() as st:
            xin = [self.sb(st, f"ti_x{i}", [128, D]) for i in range(2)]
            rxin = mkres(2)
            stg = [self.sb(st, f"ti_s{i}", [128, KC, 512]) for i in range(2)]
            rstg = mkres(2)
            ident = self.ident
            XOv = XO.rearrange("(c p) t -> p c t", p=128)
            groups = [(self.x_in, g * 512, 4, g * 512) for g in range(8)] + [(self.ctx_in, 0, 2, NL)]
            blk = 0
            for gi, (src, r0, nb, c0) in enumerate(groups):
                sg, rsg = stg[gi % 2], rstg[gi % 2]
                for tb in range(nb):
                    xb, rxb = xin[blk % 2], rxin[blk % 2]
                    blk += 1
                    rr = r0 + tb * 128
                    P.dma("sp", lambda e, xb=xb, src=src, rr=rr: e.dma_start(out=xb[:], in_=src[rr:rr + 128, :]), writes=[rxb])
                    for b4 in range(4):
                        pst, rpst = self.ps[b4 + 4 * (tb % 2)], self.rps[b4 + 4 * (tb % 2)]
                        for cc in range(4):
                            c = b4 * 4 + cc
                            P.op("pe", lambda e, pst=pst, cc=cc, xb=xb, c=c: e.transpose(
                                out=pst[:, cc * 128:(cc + 1) * 128], in_=xb[:, c * 128:(c + 1) * 128], identity=ident[:]),
                                reads=[rxb, self.rconst], writes=[rpst])
                        dst = sg[:, b4 * 4:(b4 + 1) * 4, tb * 128:(tb + 1) * 128]
                        srcp = pst[:].rearrange("p (c t) -> p c t", c=4)
                        if b4 % 2 == 0:
                            P.op("act", lambda e, dst=dst, srcp=srcp: e.activation(out=dst, in_=srcp, func=AF.Copy), reads=[rpst], writes=[rsg])
                        else:
                            P.op("dve", lambda e, dst=dst, srcp=srcp: e.tensor_copy(out=dst, in_=srcp), reads=[rpst], writes=[rsg])
                n = nb * 128
                P.dma("sp", lambda e, sg=sg, c0=c0, n=n: e.dma_start(out=XOv[:, :, c0:c0 + n], in_=sg[:, :, 0:n]), reads=[rsg], writes=[rXO])

    def phase_transpose_out(self, XI, rXI):
        nc, P = self.nc, self.P
        with ExitStack() as st:
            xt = [self.sb(st, f"to_x{i}", [128, KC, 512]) for i in range(2)]
            rxt = mkres(2)
            ot = [self.sb(st, f"to_o{i}", [128, D]) for i in range(2)]
            rot = mkres(2)
            rout = Res("out")
            ident = self.ident
            XIv = XI.rearrange("(c p) t -> p c t", p=128)
            blk = 0
            for g in range(8):
                xg, rxg = xt[g % 2], rxt[g % 2]
                P.dma("sp", lambda e, xg=xg, g=g: e.dma_start(out=xg[:], in_=XIv[:, :, g * 512:(g + 1) * 512]), reads=[rXI], writes=[rxg])
                for tb in range(4):
                    o, ro = ot[blk % 2], rot[blk % 2]
                    blk += 1
                    for b4 in range(4):
                        pst, rpst = self.ps[b4 + 4 * (tb % 2)], self.rps[b4 + 4 * (tb % 2)]
                        for cc in range(4):
                            c = b4 * 4 + cc
                            P.op("pe", lambda e, pst=pst, cc=cc, xg=xg, c=c, tb=tb: e.transpose(
                                out=pst[:, cc * 128:(cc + 1) * 128], in_=xg[:, c, tb * 128:(tb + 1) * 128], identity=ident[:]),
                                reads=[rxg, self.rconst], writes=[rpst])
                        dst = o[:, b4 * 512:(b4 + 1) * 512]
                        if b4 % 2 == 0:
                            P.op("act", lambda e, dst=dst, pst=pst: e.activation(out=dst, in_=pst[:], func=AF.Copy), reads=[rpst], writes=[ro])
                        else:
                            P.op("dve", lambda e, dst=dst, pst=pst: e.tensor_copy(out=dst, in_=pst[:]), reads=[rpst], writes=[ro])
                    r0 = g * 512 + tb * 128
                    P.dma("sp", lambda e, o=o, r0=r0: e.dma_start(out=self.out[r0:r0 + 128, :], in_=o[:]), reads=[ro], writes=[rout])

    def mod_pieces(self, layer, wb, rwb, raw, rraw, banks):
        P = self.P
        OCB = 2
        vecs = self.vecs
        scb = self.scb
        adv = self.adaw.rearrange("(g o p) k -> g p o k", p=128, o=OCB)
        ng = 144 // OCB
        pieces = []
        for g in range(ng):
            def dma_part(g=g):
                w, rw = wb[g % len(wb)], rwb[g % len(wb)]
                gg = layer * ng + g
                for o in range(OCB):
                    P.dma("pool", lambda e, o=o: e.dma_start(out=w[:, o, :], in_=adv[gg][:, o, :], max_dma_last_dim=2048), writes=[rw[o]])

            def compute_part(g=g):
                w, rw = wb[g % len(wb)], rwb[g % len(wb)]
                pst, rpst = self.ps[banks[g % 2]], self.rps[banks[g % 2]]
                for o in range(OCB):
                    for kc in range(KC):
                        P.op("pe", lambda e, o=o, kc=kc: e.matmul(
                            out=pst[:, o * 2:o * 2 + 2], lhsT=w[:, o, kc * 128:(kc + 1) * 128], rhs=scb[:, kc * 2:kc * 2 + 2],
                            start=(kc == 0), stop=(kc == KC - 1)), reads=[rw[o], self.rscb], writes=[rpst])
                for o in range(OCB):
                    oc = g * OCB + o
                    ioc = layer * 144 + oc
                    P.op("dve", lambda e, o=o, oc=oc, ioc=ioc: e.tensor_scalar(
                        out=raw[:, oc * 2:oc * 2 + 2], in0=pst[:, o * 2:o * 2 + 2], scalar1=vecs[:, V_ADAB + ioc:V_ADAB + ioc + 1],
                        scalar2=None, op0=ALU.add), reads=[rpst, self.rvecs], writes=[rraw])
            pieces.append((dma_part, compute_part))
        return pieces

    def mod_derive(self, i, raw, rraw):
        P = self.P
        vecs = self.vecs
        modv = self.modv
        for s in range(2):
            for j in range(3):
                wgt = 0.5 if j != 1 else 1.0

                def rawv(r):
                    b0 = (((3 * j + r) * 16) * 2) + s
                    return raw[:, b0:b0 + 31:2]
                offA = (((i * 2 + s) * 3 + j) * 3 + 0) * 16
                offB = offA + 16
                offG = offA + 32
                npre = vecs[:, V_NPRE + (i * 3 + j) * 16:V_NPRE + (i * 3 + j) * 16 + 16]
                npost = vecs[:, V_NPOST + (i * 3 + j) * 16:V_NPOST + (i * 3 + j) * 16 + 16]
                P.op("dve", lambda e, o=offA, a=rawv(1), b=npre: e.scalar_tensor_tensor(
                    out=modv[:, o:o + 16], in0=a, scalar=1.0, in1=b, op0=ALU.add, op1=ALU.mult),
                    reads=[rraw, self.rvecs], writes=[self.rmodv])
                P.op("dve", lambda e, o=offB, a=rawv(0): e.tensor_copy(out=modv[:, o:o + 16], in_=a),
                     reads=[rraw], writes=[self.rmodv])
                P.op("dve", lambda e, o=offG, a=rawv(2), b=npost, wgt=wgt: e.scalar_tensor_tensor(
                    out=modv[:, o:o + 16], in0=a, scalar=wgt, in1=b, op0=ALU.mult, op1=ALU.mult),
                    reads=[rraw, self.rvecs], writes=[self.rmodv])

    def phase_mod(self, layers=(0,)):
        nc, P = self.nc, self.P
        with ExitStack() as st:
            sc = self.sb(st, "mod_sc", [128, 32])
            rsc = Res("sc")
            vecs = self.vecs
            P.op("act", lambda e: e.activation(out=sc[:], in_=vecs[:, V_CC:V_CC + 32], func=AF.Silu), reads=[self.rvecs], writes=[rsc])
            P.op("dve", lambda e: e.tensor_copy(out=self.scb[:], in_=sc[:]), reads=[rsc], writes=[self.rscb])
            wb = [self.sb(st, f"mod_w{k}", [128, 2, 2048], BF16) for k in range(6)]
            rwb = [mkres(2) for _ in range(6)]
            for layer in layers:
                raw = self.sb(st, f"mod_raw{layer}", [128, 144 * 2])
                rraw = Res("raw")
                for dma_part, compute_part in self.mod_pieces(layer, wb, rwb, raw, rraw, (0, 1)):
                    dma_part()
                    compute_part()
                self.mod_derive(layer, raw, rraw)

    def rstd_from_sumsq(self, pst, rpst, n, out, rout, tmp, rtmp, eps):
        P = self.P
        P.op("act", lambda e: e.activation(out=tmp[:, 0:n], in_=pst[:, 0:n], func=AF.Sqrt, bias=self.epsv(eps), scale=1.0 / D),
             reads=[rpst, self.rconst], writes=[rtmp])
        P.op("dve", lambda e: e.reciprocal(out=out[:, 0:n], in_=tmp[:, 0:n]), reads=[rtmp], writes=[rout])

    def _stat_mm(self, item, pst, rpst, n, last=KC - 1):
        tq, rtq, dc = item
        ones = self.onesb
        self.P.op("pe", lambda e: e.matmul(out=pst[:, 0:n], lhsT=ones[:], rhs=tq[:, 0:n], start=(dc == 0), stop=(dc == last)),
                  reads=[rtq, self.rconst], writes=[rpst])

    def epsv(self, eps):
        return self.epst[:, self.eps_idx[eps]:self.eps_idx[eps] + 1]

    def setup_eps(self, st):
        self.epst = self.sb(st, "epst", [128, 4])
        self.eps_idx = {}
        for k, v in enumerate(sorted({NORM_EPS, SUBLN_EPS, LN_EPS})):
            self.eps_idx[v] = k
            self.P.op("dve", lambda e, k=k, v=v: e.memset(self.epst[:, k:k + 1], v), writes=[self.rconst])

    def load_w_cast(self, dst, rdst, src_ap):
        self.P.dma("pool", lambda e: e.dma_start(out=dst, in_=src_ap, max_dma_last_dim=2048), writes=[rdst])

    def ffn_prep(self, i, w):
        if not hasattr(self, "_ffn_prep"):
            self._ffn_prep = {}
        if (i, w) in self._ffn_prep:
            return self._ffn_prep[(i, w)]
        P = self.P
        NQ, QF = 4, FC // 4
        win, wout = self.win[(i, w)], self.wout[(i, w)]
        winv = win.rearrange("(f p) k -> f p k", p=128)
        woutv = wout.rearrange("(d p) k -> d p k", p=128)
        WinS = self.dint(f"wins{i}{w}", [FC * 2 * 128, KC * 128], BF16).rearrange("(f h p) k -> f h p k", h=2, p=128)
        WoutS = self.dint(f"wouts{i}{w}", [KC * NQ * 128, QF * 128], BF16).rearrange("(d q p) k -> d q p k", q=NQ, p=128)
        rWinS = [[Res("wins") for _ in range(2)] for _ in range(FC)]
        rWoutS = [[Res("wouts") for _ in range(NQ)] for _ in range(KC)]
        jobs = []
        for fc in range(FC):
            for half in range(2):
                jobs.append(lambda fc=fc, half=half: P.dma("pool", lambda e: e.dma_start(
                    out=WinS[fc, half], in_=winv[fc][:, half * KC * 128:(half + 1) * KC * 128], max_dma_last_dim=2048), writes=[rWinS[fc][half]]))
        for dc in range(KC):
            for q in range(NQ):
                jobs.append(lambda dc=dc, q=q: P.dma("pool", lambda e: e.dma_start(
                    out=WoutS[dc, q], in_=woutv[dc][:, q * QF * 128:(q + 1) * QF * 128], max_dma_last_dim=2048), writes=[rWoutS[dc][q]]))
        d = {"WinS": WinS, "WoutS": WoutS, "rWinS": rWinS, "rWoutS": rWoutS, "jobs": jobs, "pre": False}
        self._ffn_prep[(i, w)] = d
        return d

    def run_jobs(self, jobs, n):
        for _ in range(n):
            if jobs:
                jobs.pop(0)()

    def phase_ffn(self, i, w, j, XI, rXI, XO, rXO, with_ctx, bg=None):
        nc, P = self.nc, self.P
        prep = self.ffn_prep(i, w)
        pre = prep["pre"]
        bg = bg if bg is not None else []
        win, wout = self.win[(i, w)], self.wout[(i, w)]
        winv = win.rearrange("(f p) k -> f p k", p=128)
        woutv = wout.rearrange("(d p) k -> d p k", p=128)
        XIv = XI.rearrange("(c p) t -> p c t", p=128)
        XOv = XO.rearrange("(c p) t -> p c t", p=128)
        tiles = [(g * 512, 512, 0) for g in range(8)] + ([(NL, 256, 1)] if with_ctx else [])
        NQ = 4
        QF = FC // NQ
        with ExitStack() as st:
            xts = [self.sb(st, f"f_x{k}", [128, KC, 512]) for k in range(2)]
            rxs = [mkres(KC, "x") for _ in range(2)]
            h = self.sb(st, "f_h", [128, KC, 512], BF16)
            rh = mkres(KC, "h")
            act = self.sb(st, "f_act", [128, FC, 512], BF16)
            ract = mkres(FC, "act")
            y = self.sb(st, "f_y", [128, KC, 512])
            ry = mkres(KC, "y")
            wi = [self.sb(st, f"f_wi{k}", [128, KC * 128], BF16) for k in range(4)]
            rwi = mkres(4, "wi")
            wo = [self.sb(st, f"f_wo{k}", [128, QF * 128], BF16) for k in range(4)]
            rwo = mkres(4, "wo")
            tmp = [self.sb(st, f"f_t{k}", [128, 512]) for k in range(4)]
            rtmp = mkres(4, "tmp")
            rstd = self.sb(st, "f_rstd", [128, 512])
            rrstd = Res("rstd")
            rstd2 = self.sb(st, "f_rstd2", [128, 512])
            rrstd2 = Res("rstd2")
            ones = self.ones
            cnt = {"wi": 0, "wo": 0}
            WinS, WoutS, rWinS, rWoutS = prep["WinS"], prep["WoutS"], prep["rWinS"], prep["rWoutS"]

            def load_x(k):
                t0, n, s = tiles[k]
                xt, rx = xts[k % 2], rxs[k % 2]
                P.dma("pool", lambda e: e.dma_start(out=xt[:, :, 0:n], in_=XIv[:, :, t0:t0 + n]), reads=[rXI], writes=rx)

            def prenorm(k):
                t0, n, s = tiles[k]
                xt, rx = xts[k % 2], rxs[k % 2]
                pst, rpst = self.ps[6], self.rps[6]
                for c in range(KC):
                    tq, rtq = self.sqb[c % 2], self.rsqb[c % 2]
                    P.op("act", lambda e, tq=tq, c=c: e.activation(out=tq[:, 0:n], in_=xt[:, c, 0:n], func=AF.Square), reads=[rx[c]], writes=[rtq])
                    P.op("pe", lambda e, tq=tq, c=c: e.matmul(out=pst[:, 0:n], lhsT=self.onesb[:], rhs=tq[:, 0:n], start=(c == 0), stop=(c == KC - 1)),
                         reads=[rtq, self.rconst], writes=[rpst])
                self.rstd_from_sumsq(pst, rpst, n, rstd, rrstd, tmp[2], rtmp[2], NORM_EPS)
                for c in range(KC):
                    tq, rtq = tmp[2 + c % 2], rtmp[2 + c % 2]
                    P.op("dve", lambda e, tq=tq, c=c: e.tensor_tensor(out=tq[:, 0:n], in0=xt[:, c, 0:n], in1=rstd[:, 0:n], op=ALU.mult),
                         reads=[rx[c], rrstd], writes=[rtq])
                    P.op("act", lambda e, tq=tq, c=c: e.activation(out=h[:, c, 0:n], in_=tq[:, 0:n], func=AF.Identity,
                                                                   scale=self.mv(i, s, j, 0, c), bias=self.mv(i, s, j, 1, c)),
                         reads=[rtq, self.rmodv], writes=[rh[c]])

            def instage(k):
                t0, n, s = tiles[k]
                for fc in range(FC):
                    pa, rpa = self.ps[fc % 2], self.rps[fc % 2]
                    pu, rpu = self.ps[2 + fc % 2], self.rps[2 + fc % 2]
                    for half, (pp, rpp) in enumerate(((pa, rpa), (pu, rpu))):
                        wt, rwt = wi[cnt["wi"] % 4], rwi[cnt["wi"] % 4]
                        cnt["wi"] += 1
                        if k == 0 and not pre:
                            self.load_w_cast(wt[:], rwt, winv[fc][:, half * KC * 128:(half + 1) * KC * 128])
                            if len(tiles) > 1:
                                P.dma("sp", lambda e, wt=wt, fc=fc, half=half: e.dma_start(out=WinS[fc, half], in_=wt[:]), reads=[rwt], writes=[rWinS[fc][half]])
                        else:
                            P.dma("sp", lambda e, wt=wt, fc=fc, half=half: e.dma_start(out=wt[:], in_=WinS[fc, half]), reads=[rWinS[fc][half]], writes=[rwt])
                        for kc in range(KC):
                            P.op("pe", lambda e, pp=pp, wt=wt, kc=kc: e.matmul(out=pp[:, 0:n], lhsT=wt[:, kc * 128:(kc + 1) * 128], rhs=h[:, kc, 0:n],
                                                                                start=(kc == 0), stop=(kc == KC - 1)), reads=[rwt, rh[kc]], writes=[rpp])
                    tq, rtq = tmp[fc % 2], rtmp[fc % 2]
                    P.op("act", lambda e, tq=tq, pa=pa: e.activation(out=tq[:, 0:n], in_=pa[:, 0:n], func=AF.Silu), reads=[rpa], writes=[rtq])
                    P.op("dve", lambda e, tq=tq, pu=pu, fc=fc: e.tensor_tensor(out=act[:, fc, 0:n], in0=pu[:, 0:n], in1=tq[:, 0:n], op=ALU.mult),
                         reads=[rpu, rtq], writes=[ract[fc]])

            def outstage(k):
                t0, n, s = tiles[k]
                pst2, rpst2 = self.ps[7], self.rps[7]
                pend = []
                for dc in range(KC):
                    py, rpy = self.ps[4 + dc % 2], self.rps[4 + dc % 2]
                    for q in range(NQ):
                        wt, rwt = wo[cnt["wo"] % 4], rwo[cnt["wo"] % 4]
                        cnt["wo"] += 1
                        if k == 0 and not pre:
                            self.load_w_cast(wt[:], rwt, woutv[dc][:, q * QF * 128:(q + 1) * QF * 128])
                            if len(tiles) > 1:
                                P.dma("sp", lambda e, wt=wt, dc=dc, q=q: e.dma_start(out=WoutS[dc, q], in_=wt[:]), reads=[rwt], writes=[rWoutS[dc][q]])
                        else:
                            P.dma("sp", lambda e, wt=wt, dc=dc, q=q: e.dma_start(out=wt[:], in_=WoutS[dc, q]), reads=[rWoutS[dc][q]], writes=[rwt])
                        for f in range(QF):
                            fc = q * QF + f
                            P.op("pe", lambda e, py=py, wt=wt, f=f, fc=fc: e.matmul(out=py[:, 0:n], lhsT=wt[:, f * 128:(f + 1) * 128], rhs=act[:, fc, 0:n],
                                                                                     start=(fc == 0), stop=(fc == FC - 1)), reads=[rwt, ract[fc]], writes=[rpy])
                    tq, rtq = self.sqb[dc % 2], self.rsqb[dc % 2]
                    P.op("dve", lambda e, py=py, dc=dc: e.tensor_copy(out=y[:, dc, 0:n], in_=py[:, 0:n]), reads=[rpy], writes=[ry[dc]])
                    P.op("act", lambda e, tq=tq, dc=dc: e.activation(out=tq[:, 0:n], in_=y[:, dc, 0:n], func=AF.Square), reads=[ry[dc]], writes=[rtq])
                    pend.append((tq, rtq, dc))
                    if len(pend) > 1:
                        self._stat_mm(pend.pop(0), pst2, rpst2, n)
                self._stat_mm(pend.pop(0), pst2, rpst2, n)
                self.rstd_from_sumsq(pst2, rpst2, n, rstd2, rrstd2, tmp[0], rtmp[0], NORM_EPS)

            def post(k):
                t0, n, s = tiles[k]
                xt, rx = xts[k % 2], rxs[k % 2]
                for c in range(KC):
                    tq, rtq = tmp[c % 2], rtmp[c % 2]
                    P.op("dve", lambda e, tq=tq, c=c: e.tensor_tensor(out=tq[:, 0:n], in0=y[:, c, 0:n], in1=rstd2[:, 0:n], op=ALU.mult),
                         reads=[ry[c], rrstd2], writes=[rtq])
                    P.op("dve", lambda e, tq=tq, c=c: e.scalar_tensor_tensor(out=y[:, c, 0:n], in0=tq[:, 0:n], scalar=self.mv(i, s, j, 2, c), in1=xt[:, c, 0:n],
                                                                              op0=ALU.mult, op1=ALU.add),
                         reads=[rtq, rx[c], self.rmodv], writes=[ry[c]])
                P.dma("pool", lambda e: e.dma_start(out=XOv[:, :, t0:t0 + n], in_=y[:, :, 0:n]), reads=ry, writes=[rXO])

            NTI = len(tiles)
            load_x(0)
            if NTI > 1:
                load_x(1)
            prenorm(0)
            bper = (len(bg) + NTI - 1) // NTI
            for k in range(NTI):
                instage(k)
                self.run_jobs(bg, bper)
                if k + 1 < NTI:
                    prenorm(k + 1)
                outstage(k)
                post(k)
                if k + 2 < NTI:
                    load_x(k + 2)
            self.run_jobs(bg, len(bg))

    def phase_conv(self, XI, rXI, XO, rXO):
        nc, P = self.nc, self.P
        i, j = 0, 1
        E = 512 + 2 * PAD
        pw1v = self.pw1.rearrange("(o p) k -> o p k", p=128)
        pw2v = self.pw2.rearrange("(o p) k -> o p k", p=128)
        XIv = XI.rearrange("(c p) t -> p c t", p=128)
        XOv = XO.rearrange("(c p) t -> p c t", p=128)
        tiles = [(g * 512, 512, 0, 0, NL) for g in range(8)] + [(NL, 256, 1, NL, NT)]
        vecs = self.vecs
        ones = self.ones
        with ExitStack() as st:
            xe = self.sb(st, "c_x", [128, KC, E])
            rx = mkres(KC, "x")
            h = self.sb(st, "c_h", [128, KC, E], BF16)
            rh = mkres(KC, "h")
            u = self.sb(st, "c_u", [128, KC, E], BF16)
            ru = mkres(KC, "u")
            dg = [self.sb(st, f"c_dg{k}", [128, CW, 128], BF16) for k in range(2)]
            rdg = [mkres(CW, "dg") for _ in range(2)]
            v = self.sb(st, "c_v", [128, KC, 512])
            rv = mkres(KC, "v")
            zt = self.sb(st, "c_z", [128, KC, 512], BF16)
            rz = mkres(KC, "z")
            w1 = [self.sb(st, f"c_w1{k}", [128, 2 * KC * 128], BF16) for k in range(2)]
            rw1 = mkres(2, "w1")
            w2 = [self.sb(st, f"c_w2{k}", [128, KC * 128], BF16) for k in range(2)]
            rw2 = mkres(2, "w2")
            tmp = [self.sb(st, f"c_t{k}", [128, E]) for k in range(4)]
            rtmp = mkres(4, "tmp")
            rstd_e = self.sb(st, "c_rstde", [128, E])
            rrstd_e = Res("rstde")
            mean = self.sb(st, "c_mean", [128, 512])
            rmean = Res("mean")
            rstd = self.sb(st, "c_rstd", [128, 512])
            rrstd = Res("rstd")
            rstd2 = self.sb(st, "c_rstd2", [128, 512])
            rrstd2 = Res("rstd2")
            nw1 = 0
            nw2 = 0
            mwb = [self.sb(st, f"c_mw{k}", [128, 2, 2048], BF16) for k in range(2)]
            rmwb = [mkres(2, "mw") for _ in range(2)]
            mraw = self.sb(st, "c_mraw", [128, 144 * 2])
            rmraw = Res("mraw")
            mpieces = self.mod_pieces(1, mwb, rmwb, mraw, rmraw, (6, 7)) if self.mod1_in_conv else []
            mq_dma = [p[0] for p in mpieces]
            mq_cmp = [p[1] for p in mpieces]
            mper = (len(mpieces) + len(tiles) - 1) // len(tiles)
            W1S = self.dint("pw1s", [KC * 128, 2 * KC * 128], BF16).rearrange("(o p) k -> o p k", p=128)
            W2S = self.dint("pw2s", [KC * 128, KC * 128], BF16).rearrange("(o p) k -> o p k", p=128)
            rW1S = mkres(KC, "w1s")
            rW2S = mkres(KC, "w2s")
            for tidx, (t0, n, s, s0, s1) in enumerate(tiles):
                lo = max(t0 - PAD, s0)
                hi = min(t0 + n + PAD, s1)
                ne = hi - lo
                eo = lo - (t0 - PAD)
                pieces = [(eo, eo + min(ne, 512))]
                if ne > 512:
                    pieces.append((eo + 512, eo + ne))
                P.dma("pool", lambda e, lo=lo, hi=hi, eo=eo, ne=ne: e.dma_start(out=xe[:, :, eo:eo + ne], in_=XIv[:, :, lo:hi]), reads=[rXI], writes=rx)
                for pi, (a, b) in enumerate(pieces):
                    pst, rpst = self.ps[6 + pi], self.rps[6 + pi]
                    m = b - a
                    for c in range(KC):
                        tq, rtq = self.sqb[c % 2], self.rsqb[c % 2]
                        P.op("act", lambda e, tq=tq, c=c, a=a, b=b, m=m: e.activation(out=tq[:, 0:m], in_=xe[:, c, a:b], func=AF.Square), reads=[rx[c]], writes=[rtq])
                        P.op("pe", lambda e, tq=tq, c=c, m=m, pst=pst: e.matmul(out=pst[:, 0:m], lhsT=self.onesb[:], rhs=tq[:, 0:m], start=(c == 0), stop=(c == KC - 1)),
                             reads=[rtq, self.rconst], writes=[rpst])
                    tq, rtq = tmp[2], rtmp[2]
                    P.op("act", lambda e, tq=tq, pst=pst, m=m: e.activation(out=tq[:, 0:m], in_=pst[:, 0:m], func=AF.Sqrt, bias=self.epsv(NORM_EPS), scale=1.0 / D),
                         reads=[rpst, self.rconst], writes=[rtq])
                    P.op("dve", lambda e, tq=tq, a=a, b=b, m=m: e.reciprocal(out=rstd_e[:, a:b], in_=tq[:, 0:m]), reads=[rtq], writes=[rrstd_e])
                for c in range(KC):
                    tq = tmp[2 + c % 2]
                    rtq = rtmp[2 + c % 2]
                    P.op("dve", lambda e, c=c, eo=eo, ne=ne, tq=tq: e.tensor_tensor(out=tq[:, 0:ne], in0=xe[:, c, eo:eo + ne], in1=rstd_e[:, eo:eo + ne], op=ALU.mult),
                         reads=[rx[c], rrstd_e], writes=[rtq])
                    P.op("act", lambda e, c=c, eo=eo, ne=ne, s=s, tq=tq: e.activation(out=h[:, c, eo:eo + ne], in_=tq[:, 0:ne], func=AF.Identity,
                                                                                      scale=self.mv(i, s, j, 0, c), bias=self.mv(i, s, j, 1, c)),
                         reads=[rtq, self.rmodv], writes=[rh[c]])
                if eo > 0:
                    P.op("dve", lambda e, eo=eo: e.memset(u[:, :, 0:eo], 0.0), reads=rh, writes=ru)
                if eo + ne < n + 2 * PAD:
                    P.op("dve", lambda e, eo=eo, ne=ne, n=n: e.memset(u[:, :, eo + ne:n + 2 * PAD], 0.0), reads=rh, writes=ru)
                k2 = 0
                issued = 0
                for oc in range(KC):
                    outstanding = len(mq_cmp) - len(mq_dma)
                    if outstanding >= 2 or (outstanding > 0 and (issued >= mper or not mq_dma)):
                        mq_cmp.pop(0)()
                    if issued < mper and mq_dma and (len(mq_cmp) - len(mq_dma)) < 2:
                        mq_dma.pop(0)()
                        issued += 1
                    wt, rwt = w1[nw1 % 2], rw1[nw1 % 2]
                    nw1 += 1
                    if tidx == 0:
                        self.load_w_cast(wt[:], rwt, pw1v[oc])
                        P.dma("sp", lambda e, wt=wt, oc=oc: e.dma_start(out=W1S[oc], in_=wt[:]), reads=[rwt], writes=[rW1S[oc]])
                    else:
                        P.dma("sp", lambda e, wt=wt, oc=oc: e.dma_start(out=wt[:], in_=W1S[oc]), reads=[rW1S[oc]], writes=[rwt])
                    for (a, b) in pieces:
                        m = b - a
                        pa, rpa = self.ps[k2 % 2], self.rps[k2 % 2]
                        pg, rpg = self.ps[2 + k2 % 2], self.rps[2 + k2 % 2]
                        tq, rtq = tmp[k2 % 2], rtmp[k2 % 2]
                        k2 += 1
                        for kc in range(KC):
                            P.op("pe", lambda e, pa=pa, wt=wt, kc=kc, a=a, b=b, m=m: e.matmul(out=pa[:, 0:m], lhsT=wt[:, kc * 128:(kc + 1) * 128], rhs=h[:, kc, a:b],
                                                                                              start=(kc == 0), stop=(kc == KC - 1)), reads=[rwt, rh[kc]], writes=[rpa])
                        for kc in range(KC):
                            P.op("pe", lambda e, pg=pg, wt=wt, kc=kc, a=a, b=b, m=m: e.matmul(out=pg[:, 0:m], lhsT=wt[:, (KC + kc) * 128:(KC + kc + 1) * 128], rhs=h[:, kc, a:b],
                                                                                              start=(kc == 0), stop=(kc == KC - 1)), reads=[rwt, rh[kc]], writes=[rpg])
                        P.op("act", lambda e, tq=tq, pg=pg, m=m, oc=oc: e.activation(out=tq[:, 0:m], in_=pg[:, 0:m], func=AF.Sigmoid,
                                                                                      bias=vecs[:, V_BPW1 + 16 + oc:V_BPW1 + 16 + oc + 1]),
                             reads=[rpg, self.rvecs], writes=[rtq])
                        P.op("dve", lambda e, tq=tq, pa=pa, m=m, oc=oc, a=a, b=b: e.scalar_tensor_tensor(
                            out=u[:, oc, a:b], in0=pa[:, 0:m], scalar=vecs[:, V_BPW1 + oc:V_BPW1 + oc + 1], in1=tq[:, 0:m], op0=ALU.add, op1=ALU.mult),
                            reads=[rpa, rtq, self.rvecs], writes=[ru[oc]])
                while len(mq_cmp) > len(mq_dma):
                    mq_cmp.pop(0)()
                self.run_jobs(self.bg_conv, (152 + len(tiles) - 1) // len(tiles))
                ps1, rps1 = self.ps[6], self.rps[6]
                ps2, rps2 = self.ps[7], self.rps[7]

                def ln_stats(c, n=n):
                    tq, rtq = self.sqb[c % 2], self.rsqb[c % 2]
                    P.op("pe", lambda e: e.matmul(out=ps1[:, 0:n], lhsT=ones[:], rhs=v[:, c, 0:n], start=(c == 0), stop=(c == KC - 1)),
                         reads=[rv[c], self.rconst], writes=[rps1])
                    P.op("act", lambda e: e.activation(out=tq[:, 0:n], in_=v[:, c, 0:n], func=AF.Square), reads=[rv[c]], writes=[rtq])
                    P.op("pe", lambda e: e.matmul(out=ps2[:, 0:n], lhsT=self.onesb[:], rhs=tq[:, 0:n], start=(c == 0), stop=(c == KC - 1)),
                         reads=[rtq, self.rconst], writes=[rps2])
                for c in range(KC):
                    dgt, rdgt = dg[c % 2], rdg[c % 2]
                    for k in range(CW):
                        wk = vecs[:, V_WDW + c * CW + k:V_WDW + c * CW + k + 1]
                        if k % 2 == 0:
                            P.op("act", lambda e, dgt=dgt, k=k, wk=wk: e.activation(out=dgt[:, k, :], in_=self.ident[:], func=AF.Identity, scale=wk),
                                 reads=[self.rconst, self.rvecs], writes=[rdgt[k]])
                        else:
                            P.op("dve", lambda e, dgt=dgt, k=k, wk=wk: e.tensor_scalar(out=dgt[:, k, :], in0=self.ident[:], scalar1=wk, scalar2=None, op0=ALU.mult),
                                 reads=[self.rconst, self.rvecs], writes=[rdgt[k]])
                    pv, rpv = self.ps[4 + c % 2], self.rps[4 + c % 2]
                    for k in range(CW):
                        P.op("pe", lambda e, pv=pv, dgt=dgt, k=k, c=c, n=n: e.matmul(out=pv[:, 0:n], lhsT=dgt[:, k, :], rhs=u[:, c, k:k + n], start=(k == 0), stop=(k == CW - 1)),
                             reads=[rdgt[k], ru[c]], writes=[rpv])
                    P.op("dve", lambda e, pv=pv, c=c, n=n: e.tensor_scalar(out=v[:, c, 0:n], in0=pv[:, 0:n], scalar1=vecs[:, V_BDW + c:V_BDW + c + 1], scalar2=None, op0=ALU.add),
                         reads=[rpv, self.rvecs], writes=[rv[c]])
                    if c >= 1:
                        ln_stats(c - 1)
                ln_stats(KC - 1)
                P.op("dve", lambda e, n=n: e.tensor_scalar(out=mean[:, 0:n], in0=ps1[:, 0:n], scalar1=1.0 / D, scalar2=None, op0=ALU.mult), reads=[rps1], writes=[rmean])
                P.op("act", lambda e, n=n: e.activation(out=tmp[2][:, 0:n], in_=mean[:, 0:n], func=AF.Square), reads=[rmean], writes=[rtmp[2]])
                P.op("dve", lambda e, n=n: e.scalar_tensor_tensor(out=tmp[3][:, 0:n], in0=ps2[:, 0:n], scalar=1.0 / D, in1=tmp[2][:, 0:n], op0=ALU.mult, op1=ALU.subtract),
                     reads=[rps2, rtmp[2]], writes=[rtmp[3]])
                P.op("act", lambda e, n=n: e.activation(out=tmp[2][:, 0:n], in_=tmp[3][:, 0:n], func=AF.Sqrt, bias=self.epsv(LN_EPS), scale=1.0),
                     reads=[rtmp[3], self.rconst], writes=[rtmp[2]])
                P.op("dve", lambda e, n=n: e.reciprocal(out=rstd[:, 0:n], in_=tmp[2][:, 0:n]), reads=[rtmp[2]], writes=[rrstd])
                for c in range(KC):
                    tq, rtq = tmp[c % 2], rtmp[c % 2]
                    P.op("dve", lambda e, tq=tq, c=c, n=n: e.tensor_tensor(out=tq[:, 0:n], in0=v[:, c, 0:n], in1=mean[:, 0:n], op=ALU.subtract), reads=[rv[c], rmean], writes=[rtq])
                    P.op("dve", lambda e, c=c, n=n, tq=tq: e.tensor_tensor(out=v[:, c, 0:n], in0=tq[:, 0:n], in1=rstd[:, 0:n], op=ALU.mult), reads=[rtq, rrstd], writes=[rv[c]])
                    P.op("act", lambda e, c=c, n=n: e.activation(out=zt[:, c, 0:n], in_=v[:, c, 0:n], func=AF.Silu,
                                                                 scale=vecs[:, V_LNG + c:V_LNG + c + 1], bias=vecs[:, V_LNB + c:V_LNB + c + 1]),
                         reads=[rv[c], self.rvecs], writes=[rz[c]])
                y, ry = v, rv
                pst2, rpst2 = self.ps[6], self.rps[6]
                pend = []
                for dc in range(KC):
                    wt, rwt = w2[nw2 % 2], rw2[nw2 % 2]
                    nw2 += 1
                    if tidx == 0:
                        self.load_w_cast(wt[:], rwt, pw2v[dc])
                        P.dma("sp", lambda e, wt=wt, dc=dc: e.dma_start(out=W2S[dc], in_=wt[:]), reads=[rwt], writes=[rW2S[dc]])
                    else:
                        P.dma("sp", lambda e, wt=wt, dc=dc: e.dma_start(out=wt[:], in_=W2S[dc]), reads=[rW2S[dc]], writes=[rwt])
                    py, rpy = self.ps[4 + dc % 2], self.rps[4 + dc % 2]
                    for kc in range(KC):
                        P.op("pe", lambda e, py=py, wt=wt, kc=kc, n=n: e.matmul(out=py[:, 0:n], lhsT=wt[:, kc * 128:(kc + 1) * 128], rhs=zt[:, kc, 0:n],
                                                                                 start=(kc == 0), stop=(kc == KC - 1)), reads=[rwt, rz[kc]], writes=[rpy])
                    tq, rtq = self.sqb[dc % 2], self.rsqb[dc % 2]
                    P.op("dve", lambda e, py=py, dc=dc, n=n: e.tensor_scalar(out=y[:, dc, 0:n], in0=py[:, 0:n], scalar1=vecs[:, V_BPW2 + dc:V_BPW2 + dc + 1],
                                                                              scalar2=None, op0=ALU.add), reads=[rpy, self.rvecs], writes=[ry[dc]])
                    P.op("act", lambda e, tq=tq, dc=dc, n=n: e.activation(out=tq[:, 0:n], in_=y[:, dc, 0:n], func=AF.Square), reads=[ry[dc]], writes=[rtq])
                    pend.append((tq, rtq, dc))
                    if len(pend) > 1:
                        self._stat_mm(pend.pop(0), pst2, rpst2, n)
                self._stat_mm(pend.pop(0), pst2, rpst2, n)
                self.rstd_from_sumsq(pst2, rpst2, n, rstd2, rrstd2, tmp[0], rtmp[0], NORM_EPS)
                for c in range(KC):
                    tq, rtq = tmp[c % 2], rtmp[c % 2]
                    P.op("dve", lambda e, tq=tq, c=c, n=n: e.tensor_tensor(out=tq[:, 0:n], in0=y[:, c, 0:n], in1=rstd2[:, 0:n], op=ALU.mult),
                         reads=[ry[c], rrstd2], writes=[rtq])
                    P.op("dve", lambda e, tq=tq, c=c, n=n, s=s: e.scalar_tensor_tensor(out=y[:, c, 0:n], in0=tq[:, 0:n], scalar=self.mv(i, s, j, 2, c), in1=xe[:, c, PAD:PAD + n],
                                                                                        op0=ALU.mult, op1=ALU.add),
                         reads=[rtq, rx[c], self.rmodv], writes=[ry[c]])
                P.dma("sp", lambda e, t0=t0, n=n: e.dma_start(out=XOv[:, :, t0:t0 + n], in_=y[:, :, 0:n]), reads=ry, writes=[rXO])
            self.run_jobs(self.bg_conv, len(self.bg_conv))
            if self.mod1_in_conv:
                while mq_cmp:
                    if len(mq_dma) == len(mq_cmp):
                        mq_dma.pop(0)()
                    mq_cmp.pop(0)()
                self.mod_derive(1, mraw, rmraw)

    def phase_attn(self, XI, rXI, XO, rXO):
        self.attn_a(XI, rXI)
        self.P.drain(); self.P.flush()
        self.attn_b()
        self.P.drain(); self.P.flush()
        self.attn_c(XI, rXI, XO, rXO)

    def attn_a(self, XI, rXI):
        nc, P = self.nc, self.P
        i, j = 1, 1
        XIv = XI.rearrange("(c p) t -> p c t", p=128)
        ropev = self.rope.rearrange("(k p) t -> p k t", p=128)
        wv_v = self.wv.rearrange("(b p) k -> b p k", p=128)
        ones = self.ones
        vecs = self.vecs
        lam_init = 0.8 - 0.6 * math.exp(-0.3 * 1)
        lam = self.lam
        with ExitStack() as st0:
            lt = self.sb(st0, "a_lt", [128, 128])
            rlt = Res("lt")
            for q in range(2):
                P.op("dve", lambda e, q=q: e.tensor_tensor(out=lt[:, q * 64:(q + 1) * 64], in0=vecs[:, V_LAM + q * 128:V_LAM + q * 128 + 64],
                                                            in1=vecs[:, V_LAM + q * 128 + 64:V_LAM + q * 128 + 128], op=ALU.mult),
                     reads=[self.rvecs], writes=[rlt])
                P.op("dve", lambda e, q=q: e.tensor_reduce(out=lam[:, q:q + 1], in_=lt[:, q * 64:(q + 1) * 64], axis=mybir.AxisListType.X, op=ALU.add),
                     reads=[rlt], writes=[self.rlam])
            P.op("act", lambda e: e.activation(out=lam[:, 2:4], in_=lam[:, 0:2], func=AF.Exp), reads=[self.rlam], writes=[self.rlam])
            P.op("dve", lambda e: e.tensor_tensor(out=lam[:, 4:5], in0=lam[:, 3:4], in1=lam[:, 2:3], op=ALU.subtract), reads=[self.rlam], writes=[self.rlam])
            P.op("dve", lambda e: e.tensor_scalar(out=lam[:, 5:6], in0=lam[:, 4:5], scalar1=-lam_init, scalar2=None, op0=ALU.add), reads=[self.rlam], writes=[self.rlam])
            P.op("dve", lambda e: e.tensor_scalar(out=lam[:, 6:7], in0=vecs[:, V_SUBLN:V_SUBLN + 1], scalar1=1.0 - lam_init, scalar2=None, op0=ALU.mult),
                 reads=[self.rvecs], writes=[self.rlam])
            P.drain(); P.flush()
        groups = [[(g * 512, 512, 0) for g in range(4)], [(g * 512, 512, 0) for g in range(4, 8)] + [(NL, 256, 1)]]
        for grp in groups:
            g0 = grp[0][0]
            gn = sum(t[1] for t in grp)
            gl = sum(t[1] for t in grp if t[2] == 0)
            with ExitStack() as st:
                hg = self.sb(st, "a_h", [128, KC, 2304], BF16)
                rhg = [mkres(KC, "h") for _ in grp]
                xt = self.sb(st, "a_x", [128, KC, 512])
                rx = mkres(KC, "x")
                rp = self.sb(st, "a_rope", [128, 4, 2048])
                rrp = Res("rope")
                tmp = [self.sb(st, f"a_t{k}", [128, 512]) for k in range(4)]
                rtmp = mkres(4, "tmp")
                rstd = self.sb(st, "a_rstd", [128, 512])
                rrstd = Res("rstd")
                wsl = [self.sb(st, f"a_w{k}", [128, 1, KC * 128], BF16) for k in range(2)]
                abf = [self.sb(st, f"a_ab{k}", [128, 512], BF16) for k in range(2)]
                rabf = mkres(2, "abf")
                permf = self.sb(st, "a_permf", [128, 128])
                permb = self.sb(st, "a_permb", [128, 128], BF16)
                rperm = Res("perm")
                P.dma("sp", lambda e: e.dma_start(out=permf[:], in_=self.perm_in[:, :]), writes=[rperm])
                P.op("dve", lambda e: e.tensor_copy(out=permb[:], in_=permf[:]), reads=[rperm], writes=[rperm])
                rwsl2 = [mkres(2, "w") for _ in range(2)]
                wvs = [self.sb(st, f"a_wv{k}", [128, KC * 512], BF16) for k in range(1)]
                rwvs = mkres(1, "wv")
                ost = [self.sb(st, f"a_o{k}", [128, 512], BF16) for k in range(4)]
                rost = mkres(4, "ost")
                P.dma("sp", lambda e, g0=g0, gl=gl: e.dma_start(out=rp[:, :, 0:gl], in_=ropev[:, :, g0:g0 + gl]), writes=[rrp])
                for ti, (t0, n, s) in enumerate(grp):
                    off = t0 - g0
                    P.dma("sp", lambda e, t0=t0, n=n: e.dma_start(out=xt[:, :, 0:n], in_=XIv[:, :, t0:t0 + n]), reads=[rXI], writes=rx)
                    pst, rpst = self.ps[6 + ti % 2], self.rps[6 + ti % 2]
                    for c in range(KC):
                        tq, rtq = self.sqb[c % 2], self.rsqb[c % 2]
                        P.op("act", lambda e, tq=tq, c=c, n=n: e.activation(out=tq[:, 0:n], in_=xt[:, c, 0:n], func=AF.Square), reads=[rx[c]], writes=[rtq])
                        P.op("pe", lambda e, tq=tq, c=c, n=n, pst=pst: e.matmul(out=pst[:, 0:n], lhsT=self.onesb[:], rhs=tq[:, 0:n], start=(c == 0), stop=(c == KC - 1)),
                             reads=[rtq, self.rconst], writes=[rpst])
                    self.rstd_from_sumsq(pst, rpst, n, rstd, rrstd, tmp[2], rtmp[2], NORM_EPS)
                    for c in range(KC):
                        tq, rtq = tmp[2 + c % 2], rtmp[2 + c % 2]
                        P.op("dve", lambda e, tq=tq, c=c, n=n: e.tensor_tensor(out=tq[:, 0:n], in0=xt[:, c, 0:n], in1=rstd[:, 0:n], op=ALU.mult),
                             reads=[rx[c], rrstd], writes=[rtq])
                        P.op("act", lambda e, tq=tq, c=c, n=n, s=s, off=off: e.activation(out=hg[:, c, off:off + n], in_=tq[:, 0:n], func=AF.Identity,
                                                                                           scale=self.mv(i, s, j, 0, c), bias=self.mv(i, s, j, 1, c)),
                             reads=[rtq, self.rmodv], writes=[rhg[ti][c]])
                nw = 0
                no = 0
                k2 = 0
                pendq = None
                for which, (wa, dst, rdst, tc, tsn) in enumerate(((self.wk, self.KT, self.rKT, 2, 3), (self.wq, self.QT, self.rQT, 0, 1))):
                    wav = wa.rearrange("(o p) k -> o p k", p=128)
                    def issue_w(oc_, slot):
                        self.load_w_cast(wsl[slot][:, 0, :], rwsl2[slot][0], wav[oc_])
                    issue_w(0, nw % 2)
                    for oc in range(NHEAD):
                        wt, rwt2 = wsl[nw % 2], rwsl2[nw % 2]
                        nw += 1
                        if oc + 1 < NHEAD:
                            issue_w(oc + 1, nw % 2)
                        for ti, (t0, n, s) in enumerate(grp):
                            if which == 1 and s == 1:
                                continue
                            off = t0 - g0
                            pa, rpa = self.ps[k2 % 2], self.rps[k2 % 2]
                            pb, rpb = self.ps[2 + k2 % 2], self.rps[2 + k2 % 2]
                            ta, rta = tmp[k2 % 2], rtmp[k2 % 2]
                            tb, rtb = tmp[2 + k2 % 2], rtmp[2 + k2 % 2]
                            k2 += 1
                            o, ro = ost[no % 4], rost[no % 4]
                            no += 1
                            for kc in range(KC):
                                P.op("pe", lambda e, pa=pa, wt=wt, kc=kc, n=n, off=off: e.matmul(out=pa[:, 0:n], lhsT=wt[:, 0, kc * 128:(kc + 1) * 128], rhs=hg[:, kc, off:off + n],
                                                                                                  start=(kc == 0), stop=(kc == KC - 1)), reads=[rwt2[0], rhg[ti][kc]], writes=[rpa])
                            if s == 0:
                                ab, rab = abf[k2 % 2], rabf[k2 % 2]
                                P.op("act", lambda e, ab=ab, pa=pa, n=n: e.activation(out=ab[:, 0:n], in_=pa[:, 0:n], func=AF.Copy), reads=[rpa], writes=[rab])
                            else:
                                ab, rab = None, None
                            if pendq is not None:
                                pendq()

                            def post(s=s, ab=ab, rab=rab, pa=pa, rpa=rpa, pb=pb, rpb=rpb, ta=ta, rta=rta, tb=tb, rtb=rtb, o=o, ro=ro, n=n, off=off,
                                     tc=tc, tsn=tsn, dst=dst, rdst=rdst, oc=oc, t0=t0):
                                if s == 0:
                                    P.op("pe", lambda e: e.matmul(out=pb[:, 0:n], lhsT=permb[:], rhs=ab[:, 0:n], start=True, stop=True),
                                         reads=[rperm, rab], writes=[rpb])
                                    P.op("dve", lambda e: e.tensor_tensor(out=ta[:, 0:n], in0=pa[:, 0:n], in1=rp[:, tc, off:off + n], op=ALU.mult),
                                         reads=[rpa, rrp], writes=[rta])
                                    P.op("dve", lambda e: e.tensor_tensor(out=tb[:, 0:n], in0=pb[:, 0:n], in1=rp[:, tsn, off:off + n], op=ALU.mult),
                                         reads=[rpb, rrp], writes=[rtb])
                                    P.op("pool", lambda e: e.tensor_tensor(out=o[:, 0:n], in0=ta[:, 0:n], in1=tb[:, 0:n], op=ALU.add),
                                         reads=[rta, rtb], writes=[ro])
                                else:
                                    P.op("act", lambda e: e.activation(out=o[:, 0:n], in_=pa[:, 0:n], func=AF.Copy), reads=[rpa], writes=[ro])
                                P.dma("sp", lambda e: e.dma_start(out=dst[oc * 128:(oc + 1) * 128, t0:t0 + n], in_=o[:, 0:n]), reads=[ro], writes=[rdst])
                            pendq = post
                    if pendq is not None:
                        pendq()
                        pendq = None
                nblk = gn // 128
                for nb in range(4):
                    wt, rwt = wvs[0], rwvs[0]
                    self.load_w_cast(wt[:], rwt, wv_v[nb])
                    for tb in range(nblk):
                        ti = min(tb // 4, len(grp) - 1)
                        pv, rpv = self.ps[4 + tb % 2], self.rps[4 + tb % 2]
                        o, ro = ost[no % 4], rost[no % 4]
                        no += 1
                        for kc in range(KC):
                            P.op("pe", lambda e, pv=pv, wt=wt, kc=kc, tb=tb: e.matmul(out=pv[:, :], lhsT=hg[:, kc, tb * 128:(tb + 1) * 128], rhs=wt[:, kc * 512:(kc + 1) * 512],
                                                                                      start=(kc == 0), stop=(kc == KC - 1)), reads=[rwt, rhg[ti][kc]], writes=[rpv])
                        if tb % 2 == 0:
                            P.op("act", lambda e, o=o, pv=pv: e.activation(out=o[:, :], in_=pv[:, :], func=AF.Copy), reads=[rpv], writes=[ro])
                        else:
                            P.op("dve", lambda e, o=o, pv=pv: e.tensor_copy(out=o[:, :], in_=pv[:, :]), reads=[rpv], writes=[ro])
                        r0 = g0 + tb * 128
                        P.dma("sp", lambda e, o=o, r0=r0, nb=nb: e.dma_start(out=self.VV[r0:r0 + 128, nb * 512:(nb + 1) * 512], in_=o[:, :]), reads=[ro], writes=[self.rVV])
                P.drain(); P.flush()

    def attn_b(self):
        nc, P = self.nc, self.P
        NKB = NT // 128
        NQT = NL // 512
        VVv = self.VV.rearrange("(kb p) e -> p kb e", p=128)
        lam = self.lam
        with ExitStack() as st:
            kT = [self.sb(st, f"b_k{k}", [128, NT], BF16) for k in range(2)]
            qz = [[self.sb(st, f"b_q{c}{k}", [128, NL], BF16) for k in range(2)] for c in range(2)]
            rqz = Res("qz")
            for k in range(2):
                P.op("dve", lambda e, k=k: e.memset(qz[0][k][64:128, :], 0.0), writes=[rqz])
                P.op("dve", lambda e, k=k: e.memset(qz[1][k][0:64, :], 0.0), writes=[rqz])
            vh = [self.sb(st, f"b_v{k}", [128, NKB, 128], BF16) for k in range(2)]
            rkqv = mkres(2, "kqv")
            pt = [self.sb(st, f"b_p{k}", [128, 512], BF16) for k in range(4)]
            rpt = mkres(4, "p")
            ob = [self.sb(st, f"b_o{k}", [128, 512]) for k in range(2)]
            rob = mkres(2, "o")
            t2 = [self.sb(st, f"b_t{k}", [128, 512]) for k in range(3)]
            rt2 = mkres(3, "t")
            osb = [self.sb(st, f"b_ob{k}", [128, 512], BF16) for k in range(2)]
            rosb = mkres(2, "ob")
            zc = self.sb(st, "b_zc", [128, 512])
            rzc = Res("zc")
            zacc = [self.sb(st, f"b_zacc{k}", [128, 512]) for k in range(2)]
            rzacc = mkres(2, "zacc")
            onesb = self.sb(st, "b_ones", [128, 128], BF16)
            ronesb = Res("onesb")
            P.op("dve", lambda e: e.memset(onesb[:], 1.0), writes=[ronesb])
            ones = self.ones
            pS = [self.ps[0], self.ps[1]]
            rpS = [self.rps[0], self.rps[1]]
            pO = [self.ps[2], self.ps[3]]
            rpO = [self.rps[2], self.rps[3]]
            pZ = [self.ps[4], self.ps[5]]
            rpZ = [self.rps[4], self.rps[5]]
            pN = [self.ps[6], self.ps[7]]
            rpN = [self.rps[6], self.rps[7]]
            npt = 0
            tcount = 0
            pending = None

            def finalize(item):
                o, ro, hd, qt, k = item
                sq, rsq = t2[2], rt2[2]
                sb_, rsb_ = self.sqb[k], self.rsqb[k]
                P.op("dve", lambda e: e.tensor_tensor(out=sb_[:], in0=o[:], in1=o[:], op=ALU.mult), reads=[ro], writes=[rsb_])
                P.op("pe", lambda e: e.matmul(out=pN[k][:], lhsT=self.onesb[:], rhs=sb_[:], start=True, stop=True), reads=[rsb_, self.rconst], writes=[rpN[k]])
                P.op("act", lambda e: e.activation(out=sq[:], in_=pN[k][:], func=AF.Ln, bias=self.epsv(SUBLN_EPS), scale=1.0 / 128), reads=[rpN[k], self.rconst], writes=[rsq])
                P.op("act", lambda e: e.activation(out=sq[:], in_=sq[:], func=AF.Exp, scale=-0.5), reads=[rsq], writes=[rsq])
                P.op("dve", lambda e: e.tensor_tensor(out=o[:], in0=o[:], in1=sq[:], op=ALU.mult), reads=[ro, rsq], writes=[ro])
                ot, rot = osb[k], rosb[k]
                P.op("dve", lambda e: e.tensor_scalar(out=ot[:], in0=o[:], scalar1=lam[:, 6:7], scalar2=None, op0=ALU.mult), reads=[ro, self.rlam], writes=[rot])
                P.dma("sp", lambda e: e.dma_start(out=self.ON[hd * 128:(hd + 1) * 128, qt * 512:(qt + 1) * 512], in_=ot[:]), reads=[rot], writes=[self.rON])

            for hd in range(NHEAD):
                b = hd % 2
                self.run_jobs(self.bg_attnb, 10)
                P.dma("sp", lambda e, b=b, hd=hd: e.dma_start(out=kT[b][:], in_=self.KT[hd * 128:(hd + 1) * 128, :]), reads=[self.rKT], writes=[rkqv[b]])
                P.dma("sp", lambda e, b=b, hd=hd: e.dma_start(out=qz[0][b][0:64, :], in_=self.QT[hd * 128:hd * 128 + 64, :]), reads=[self.rQT], writes=[rkqv[b]])
                P.dma("sp", lambda e, b=b, hd=hd: e.dma_start(out=qz[1][b][64:128, :], in_=self.QT[hd * 128 + 64:(hd + 1) * 128, :]), reads=[self.rQT], writes=[rkqv[b]])
                P.dma("sp", lambda e, b=b, hd=hd: e.dma_start(out=vh[b][:], in_=VVv[:, :, hd * 128:(hd + 1) * 128]), reads=[self.rVV], writes=[rkqv[b]])
                for qt in range(NQT):
                    k = tcount % 2
                    tcount += 1
                    q0 = qt * 512
                    prev = None
                    for kb in range(NKB + 1):
                        if kb < NKB:
                            cur = []
                            for c in range(2):
                                p, rp_ = pt[npt % 4], rpt[npt % 4]
                                npt += 1
                                P.op("pe", lambda e, c=c, b=b, kb=kb, q0=q0: e.matmul(out=pS[c][:], lhsT=kT[b][:, kb * 128:(kb + 1) * 128],
                                                                                     rhs=qz[c][b][:, q0:q0 + 512], start=True, stop=True),
                                     reads=[rkqv[b], rqz], writes=[rpS[c]])
                                cur.append((p, rp_))
                            for c in range(2):
                                p, rp_ = cur[c]
                                P.op("act", lambda e, c=c, p=p: e.activation(out=p[:], in_=pS[c][:], func=AF.Exp), reads=[rpS[c]], writes=[rp_])
                        if prev is not None:
                            kp = kb - 1
                            for c in range(2):
                                p, rp_ = prev[c]
                                P.op("pe", lambda e, c=c, p=p, b=b, kp=kp: e.matmul(out=pO[c][:], lhsT=vh[b][:, kp, :], rhs=p[:], start=(kp == 0), stop=(kp == NKB - 1)),
                                     reads=[rkqv[b], rp_], writes=[rpO[c]])
                            p, rp_ = prev[1]
                            P.op("pe", lambda e, p=p, kp=kp: e.matmul(out=pZ[1][:], lhsT=onesb[:], rhs=p[:], start=(kp == 0), stop=(kp == NKB - 1)),
                                 reads=[ronesb, rp_], writes=[rpZ[1]])
                            p, rp_ = prev[0]
                            za, rza = zacc[kp % 2], rzacc[kp % 2]
                            if kp < 2:
                                P.op("dve", lambda e, p=p, za=za: e.tensor_copy(out=za[:], in_=p[:]), reads=[rp_], writes=[rza])
                            else:
                                P.op("dve", lambda e, p=p, za=za: e.tensor_tensor(out=za[:], in0=za[:], in1=p[:], op=ALU.add), reads=[rza, rp_], writes=[rza])
                        prev = cur if kb < NKB else None
                        if kb == 12 and pending is not None:
                            finalize(pending)
                            pending = None
                    o, ro = ob[k], rob[k]
                    P.op("dve", lambda e, o=o: e.tensor_copy(out=o[:], in_=pO[0][:]), reads=[rpO[0]], writes=[ro])
                    P.op("dve", lambda e: e.tensor_copy(out=t2[1][:], in_=pO[1][:]), reads=[rpO[1]], writes=[rt2[1]])
                    P.op("dve", lambda e: e.tensor_tensor(out=zacc[0][:], in0=zacc[0][:], in1=zacc[1][:], op=ALU.add), reads=[rzacc[0], rzacc[1]], writes=[rzacc[0]])
                    P.op("pe", lambda e: e.matmul(out=pZ[0][:], lhsT=ones[:], rhs=zacc[0][:], start=True, stop=True), reads=[rzacc[0], self.rconst], writes=[rpZ[0]])
                    P.op("dve", lambda e: e.tensor_copy(out=t2[0][:], in_=pZ[0][:]), reads=[rpZ[0]], writes=[rt2[0]])
                    P.op("dve", lambda e: e.tensor_copy(out=zc[:], in_=pZ[1][:]), reads=[rpZ[1]], writes=[rzc])
                    P.op("dve", lambda e: e.reciprocal(out=t2[0][:], in_=t2[0][:]), reads=[rt2[0]], writes=[rt2[0]])
                    P.op("dve", lambda e: e.reciprocal(out=zc[:], in_=zc[:]), reads=[rzc], writes=[rzc])
                    P.op("dve", lambda e, o=o: e.tensor_tensor(out=o[:], in0=o[:], in1=t2[0][:], op=ALU.mult), reads=[ro, rt2[0]], writes=[ro])
                    P.op("dve", lambda e: e.tensor_tensor(out=t2[1][:], in0=t2[1][:], in1=zc[:], op=ALU.mult), reads=[rt2[1], rzc], writes=[rt2[1]])
                    P.op("dve", lambda e, o=o: e.scalar_tensor_tensor(out=o[:], in0=t2[1][:], scalar=lam[:, 5:6], in1=o[:], op0=ALU.mult, op1=ALU.add),
                         reads=[rt2[1], ro, self.rlam], writes=[ro])
                    pending = (o, ro, hd, qt, k)
            finalize(pending)
            self.run_jobs(self.bg_attnb, len(self.bg_attnb))

    def attn_c(self, XI, rXI, XO, rXO):
        nc, P = self.nc, self.P
        i, j = 1, 1
        XIv = XI.rearrange("(c p) t -> p c t", p=128)
        XOv = XO.rearrange("(c p) t -> p c t", p=128)
        ONv = self.ON.rearrange("(c p) t -> p c t", p=128)
        wov = self.wo.rearrange("(o p) k -> o p k", p=128)
        ones = self.ones
        with ExitStack() as st:
            woa = self.sb(st, "c3_wo", [128, KC, KC * 128], BF16)
            rwoa = Res("wo")
            xt = self.sb(st, "c3_x", [128, KC, 512])
            rx = mkres(KC, "x")
            on = [self.sb(st, f"c3_on{k}", [128, KC, 512], BF16) for k in range(2)]
            ron = mkres(2, "on")
            y = self.sb(st, "c3_y", [128, KC, 512])
            ry = mkres(KC, "y")
            tmp = [self.sb(st, f"c3_t{k}", [128, 512]) for k in range(4)]
            rtmp = mkres(4, "tmp")
            rstd2 = self.sb(st, "c3_rstd2", [128, 512])
            rrstd2 = Res("rstd2")
            rwo_l = mkres(KC, "wo")
            for dc in range(KC):
                self.load_w_cast(woa[:, dc, :], rwo_l[dc], wov[dc])
            s = 0
            for g in range(8):
                t0, n = g * 512, 512
                ont, ront = on[g % 2], ron[g % 2]
                P.dma("sp", lambda e, ont=ont, t0=t0: e.dma_start(out=ont[:], in_=ONv[:, :, t0:t0 + 512]), reads=[self.rON], writes=[ront])
                P.dma("sp", lambda e, t0=t0: e.dma_start(out=xt[:], in_=XIv[:, :, t0:t0 + 512]), reads=[rXI], writes=rx)
                pst2, rpst2 = self.ps[6], self.rps[6]
                pend = []
                for dc in range(KC):
                    py, rpy = self.ps[4 + dc % 2], self.rps[4 + dc % 2]
                    for kc in range(KC):
                        P.op("pe", lambda e, py=py, dc=dc, kc=kc, ont=ont: e.matmul(out=py[:], lhsT=woa[:, dc, kc * 128:(kc + 1) * 128], rhs=ont[:, kc, :],
                                                                                   start=(kc == 0), stop=(kc == KC - 1)), reads=[rwo_l[dc], ront], writes=[rpy])
                    tq, rtq = self.sqb[dc % 2], self.rsqb[dc % 2]
                    P.op("dve", lambda e, py=py, dc=dc: e.tensor_copy(out=y[:, dc, :], in_=py[:]), reads=[rpy], writes=[ry[dc]])
                    P.op("act", lambda e, tq=tq, dc=dc: e.activation(out=tq[:], in_=y[:, dc, :], func=AF.Square), reads=[ry[dc]], writes=[rtq])
                    pend.append((tq, rtq, dc))
                    if len(pend) > 1:
                        self._stat_mm(pend.pop(0), pst2, rpst2, n)
                self._stat_mm(pend.pop(0), pst2, rpst2, n)
                self.rstd_from_sumsq(pst2, rpst2, n, rstd2, rrstd2, tmp[0], rtmp[0], NORM_EPS)
                for c in range(KC):
                    tq, rtq = tmp[c % 2], rtmp[c % 2]
                    P.op("dve", lambda e, tq=tq, c=c: e.tensor_tensor(out=tq[:], in0=y[:, c, :], in1=rstd2[:], op=ALU.mult), reads=[ry[c], rrstd2], writes=[rtq])
                    P.op("dve", lambda e, tq=tq, c=c: e.scalar_tensor_tensor(out=y[:, c, :], in0=tq[:], scalar=self.mv(i, s, j, 2, c), in1=xt[:, c, :],
                                                                              op0=ALU.mult, op1=ALU.add), reads=[rtq, rx[c], self.rmodv], writes=[ry[c]])
                P.dma("sp", lambda e, t0=t0: e.dma_start(out=XOv[:, :, t0:t0 + 512], in_=y[:]), reads=ry, writes=[rXO])


def _tile_rows(w, ncol_chunk=128):
    K, N = w.shape
    kc, oc = K // 128, N // 128
    return np.ascontiguousarray(w.reshape(kc, 128, oc, 128).transpose(2, 1, 0, 3).reshape(oc * 128, kc * 128))


def _pvec(v):
    return np.ascontiguousarray(v.reshape(-1, 128).T)


def prep_shared(inp):
    sh = {}
    sh["ident"] = np.eye(128, dtype=np.float32)
    aw = inp["ada_w"]
    sh["adaw"] = np.concatenate([_tile_rows(aw[i]) for i in range(2)], axis=0)
    for i in range(2):
        for w, (ki, ko) in ((1, ("ffn1_w_in", "ffn1_w_out")), (2, ("ffn2_w_in", "ffn2_w_out"))):
            wi = inp[ki][i]
            a = _tile_rows(wi[:, :FF]).reshape(FC, 128, KC * 128)
            u = _tile_rows(wi[:, FF:]).reshape(FC, 128, KC * 128)
            sh[f"win{i}{w}"] = np.ascontiguousarray(np.concatenate([a, u], axis=2).reshape(FC * 128, 2 * KC * 128))
            sh[f"wout{i}{w}"] = _tile_rows(inp[ko][i])
    pw1 = inp["conv_w_pw1"][0]
    a = _tile_rows(pw1[:, :D]).reshape(KC, 128, KC * 128)
    g = _tile_rows(pw1[:, D:]).reshape(KC, 128, KC * 128)
    sh["pw1"] = np.ascontiguousarray(np.concatenate([a, g], axis=2).reshape(KC * 128, 2 * KC * 128))
    sh["pw2"] = _tile_rows(inp["conv_w_pw2"][0])
    wqkv = inp["attn_w_qkv"][0]
    wq, wk, wv = wqkv[:, :D], wqkv[:, D:2 * D], wqkv[:, 2 * D:]
    r = np.arange(D)
    i32 = r % 32
    partner = np.where(i32 < 16, r + 16, r - 16)
    sh["wq"] = _tile_rows(wq)
    sh["wk"] = _tile_rows(wk)
    pm = np.zeros((128, 128), np.float32)
    m = np.arange(128)
    pm[np.where((m % 32) < 16, m + 16, m - 16), m] = 1.0
    sh["perm"] = pm
    sh["wv"] = np.ascontiguousarray(wv.reshape(KC, 128, 4, 512).transpose(2, 1, 0, 3).reshape(4 * 128, KC * 512))
    sh["wo"] = _tile_rows(inp["attn_w_o"][0])
    t = np.arange(NL)
    inv_freq = (10000.0 ** (-np.arange(0, 32, 2, dtype=np.float32) / 32)).astype(np.float32)
    rr = np.arange(128)
    jj = rr % 64
    pos = np.where((jj < 32)[:, None], (t // 64)[None, :], (t % 64)[None, :]).astype(np.float32)
    fr = inv_freq[(jj % 32) % 16][:, None]
    ang = (pos * fr).astype(np.float32)
    cs = np.cos(ang).astype(np.float32)
    sn = np.sin(ang).astype(np.float32)
    sgn = np.where(((jj % 32) < 16)[:, None], -1.0, 1.0).astype(np.float32)
    sn = sn * sgn
    sh["rope"] = np.ascontiguousarray(np.concatenate([cs * 0.125, sn * 0.125, cs, sn], axis=0).astype(np.float32))
    return sh


def prep_vecs(inp, b):
    v = np.zeros((128, NV), np.float32)
    cc = np.stack([inp["c"][b], inp["c_ctx"]], axis=0)
    v[:, V_CC:V_CC + 32] = cc.reshape(2, KC, 128).transpose(2, 1, 0).reshape(128, 32)
    v[:, V_ADAB:V_ADAB + 288] = inp["ada_b"].reshape(2, 144, 128).transpose(2, 0, 1).reshape(128, 288)
    v[:, V_NPRE:V_NPRE + 96] = inp["norm_pre"].reshape(2, 3, KC, 128).transpose(3, 0, 1, 2).reshape(128, 96)
    v[:, V_NPOST:V_NPOST + 96] = inp["norm_post"].reshape(2, 3, KC, 128).transpose(3, 0, 1, 2).reshape(128, 96)
    v[:, V_BPW1:V_BPW1 + 32] = inp["conv_b_pw1"][0].reshape(2, KC, 128).transpose(2, 0, 1).reshape(128, 32)
    v[:, V_WDW:V_WDW + 496] = inp["conv_w_dw"][0].reshape(CW, KC, 128).transpose(2, 1, 0).reshape(128, 496)
    v[:, V_BDW:V_BDW + 16] = _pvec(inp["conv_b_dw"][0])
    v[:, V_LNG:V_LNG + 16] = _pvec(inp["conv_ln_g"][0])
    v[:, V_LNB:V_LNB + 16] = _pvec(inp["conv_ln_b"][0])
    v[:, V_BPW2:V_BPW2 + 16] = _pvec(inp["conv_b_pw2"][0])
    lam = np.concatenate([inp["attn_lambda_q1"][0], inp["attn_lambda_k1"][0], inp["attn_lambda_q2"][0], inp["attn_lambda_k2"][0]])
    v[:, V_LAM:V_LAM + 256] = lam[None, :]
    v[:, V_SUBLN] = inp["attn_subln_g"][0]
    return v


_CACHE = {}


def run(inp, phases=None, dbg=None, cores=NCORES, trace=False):
    key = (None if phases is None else tuple(phases), dbg)
    if key not in _CACHE:
        b = Builder(phases, dbg)
        nc = b.build()
        _CACHE[key] = (nc, b.in_names)
    nc, in_names = _CACHE[key]
    sh = prep_shared(inp)
    in_maps = []
    for b in range(cores):
        m = dict(sh)
        m["x"] = np.ascontiguousarray(inp["x"][b])
        m["ctx"] = np.ascontiguousarray(inp["ctx"][b])
        m["vecs"] = prep_vecs(inp, b)
        in_maps.append({k: m[k] for k in in_names})
    return run_bass_kernel_spmd(nc, in_maps, core_ids=list(range(cores)), trace=trace)


def kernel(**inputs):
    inp = {k: np.asarray(v) for k, v in inputs.items()}
    res = run(inp)
    return np.stack([res.results[b]["out"] for b in range(NCORES)], axis=0)
```

```python
import math
from contextlib import ExitStack

import numpy as np
import concourse.bass as bass
import concourse.mybir as mybir
from concourse.bass_utils import run_bass_kernel_spmd

F32 = mybir.dt.float32
BF16 = mybir.dt.bfloat16
AF = mybir.ActivationFunctionType
ALU = mybir.AluOpType

NCORES = 8
D = 2048
NL = 4096
NCTX = 256
NT = NL + NCTX
FF = 5632
KC = D // 128
FC = FF // 128
CW = 31
PAD = 15
NHEAD = 16
NORM_EPS = 1e-6
SUBLN_EPS = 1e-5
LN_EPS = 1e-5

V_CC = 0
V_ADAB = V_CC + 32
V_NPRE = V_ADAB + 288
V_NPOST = V_NPRE + 96
V_BPW1 = V_NPOST + 96
V_WDW = V_BPW1 + 32
V_BDW = V_WDW + 496
V_LNG = V_BDW + 16
V_LNB = V_LNG + 16
V_BPW2 = V_LNB + 16
V_LAM = V_BPW2 + 16
V_SUBLN = V_LAM + 256
NV = V_SUBLN + 1


class Res:
    __slots__ = ("name", "w", "rs", "excl")

    def __init__(self, name="", excl=False):
        self.name = name
        self.excl = excl
        self.w = None
        self.rs = {}


def mkres(n, name=""):
    return [Res(f"{name}{i}") for i in range(n)]


class Eng:
    def __init__(self, name, sem):
        self.name = name
        self.sem = sem
        self.cnt = 0
        self.known = {}
        self.ops = []
        self.ring = []
        self.ring_i = 0


class Prog:
    def __init__(self, nc, stack):
        self.nc = nc
        self.sems = {}
        self.E = {}
        for name in ("pe", "act", "dve", "pool", "sp"):
            s = stack.enter_context(nc.semaphore(f"prog_{name}"))
            self.sems[name] = s
            self.E[name] = Eng(name, s)
        for q, n in (("sp", 24), ("pool", 24), ("act", 8)):
            for i in range(n):
                key = f"dma_{q}_{i}"
                s = stack.enter_context(nc.semaphore(key))
                self.sems[key] = s
                self.E[q].ring.append([key, 0])

    def _need(self, eng, toks):
        for key, val in toks:
            if key == eng.name and eng.name == "pe":
                continue
            if eng.known.get(key, 0) < val:
                eng.known[key] = val
                eng.ops.append(("wait", key, val))

    def _deps(self, eng, reads, writes, same_eng_war=False):
        toks = []
        for r in reads:
            if r.w is not None:
                toks.append(r.w)
        for r in writes:
            if r.w is not None and r.w[0] != eng.name:
                toks.append(r.w)
            for k, v in r.rs.items():
                if k != eng.name:
                    toks.append((k, v))
        self._need(eng, toks)

    def _commit(self, tok, reads, writes):
        for r in reads:
            if r.rs.get(tok[0], 0) < tok[1]:
                r.rs[tok[0]] = tok[1]
        for r in writes:
            r.w = tok
            r.rs = {}

    def op(self, eng, fn, reads=(), writes=()):
        e = self.E[eng]
        if any(r.excl for r in reads):
            writes = list(writes) + [r for r in reads if r.excl]
        self._deps(e, reads, writes)
        e.cnt += 1
        tok = (eng, e.cnt)
        e.ops.append(("op", fn, eng, 1))
        self._commit(tok, reads, writes)

    def dma(self, q, fn, reads=(), writes=()):
        e = self.E[q]
        toks = []
        for r in reads:
            if r.w is not None:
                toks.append(r.w)
        for r in writes:
            if r.w is not None:
                toks.append(r.w)
            toks.extend(r.rs.items())
        slot = e.ring[e.ring_i]
        e.ring_i = (e.ring_i + 1) % len(e.ring)
        if slot[1] > 0:
            toks.append((slot[0], slot[1]))
        self._need(e, toks)
        slot[1] += 16
        tok = (slot[0], slot[1])
        e.ops.append(("op", fn, slot[0], 16))
        self._commit(tok, reads, writes)

    def drain(self, q="sp"):
        e = self.E[q]
        toks = []
        for name, o in self.E.items():
            if o.cnt > 0 and name != q:
                toks.append((name, o.cnt))
            for key, val in o.ring:
                if val > 0:
                    toks.append((key, val))
        self._need(e, toks)

    def flush(self, name="blk"):
        nc = self.nc
        sems = self.sems
        with nc.Block() as block:
            def replay(handle, ops):
                for o in ops:
                    if o[0] == "wait":
                        handle.wait_ge(sems[o[1]], o[2])
                    else:
                        o[1](handle).then_inc(sems[o[2]], o[3])

            @block.tensor
            def _(h):
                replay(h, self.E["pe"].ops)

            @block.scalar
            def _(h):
                replay(h, self.E["act"].ops)

            @block.vector
            def _(h):
                replay(h, self.E["dve"].ops)

            @block.gpsimd
            def _(h):
                replay(h, self.E["pool"].ops)

            @block.sync
            def _(h):
                replay(h, self.E["sp"].ops)
        for e in self.E.values():
            e.ops = []


class Builder:
    def __init__(self, phases=None, dbg=None):
        self.phases = phases
        self.dbg = dbg
        self.nc = bass.Bass("TRN2", target_bir_lowering=False)
        self.stack = ExitStack()
        self.in_names = []

    def din(self, name, shape, dt=F32):
        self.in_names.append(name)
        return self.nc.dram_tensor(name, list(shape), dt, kind="ExternalInput").ap()

    def dint(self, name, shape, dt=F32):
        return self.nc.dram_tensor(name, list(shape), dt, kind="Internal").ap()

    def sb(self, st, name, shape, dt=F32):
        self._uid = getattr(self, "_uid", 0) + 1
        return st.enter_context(self.nc.sbuf_tensor(f"sb{self._uid}_{name}", list(shape), dt))

    def build(self):
        nc = self.nc
        st = self.stack
        with st:
            self.P = Prog(nc, st)
            self._declare()
            self._globals(st)
            self._run_phases()
        return nc

    def _declare(self):
        self.x_in = self.din("x", [NL, D])
        self.ctx_in = self.din("ctx", [NCTX, D])
        self.vecs_in = self.din("vecs", [128, NV])
        self.ident_in = self.din("ident", [128, 128])
        self.adaw = self.din("adaw", [2 * 144 * 128, 2048])
        self.win = {}
        self.wout = {}
        for i in range(2):
            for w in (1, 2):
                self.win[(i, w)] = self.din(f"win{i}{w}", [FC * 128, 2 * KC * 128])
                self.wout[(i, w)] = self.din(f"wout{i}{w}", [KC * 128, FC * 128])
        self.pw1 = self.din("pw1", [KC * 128, 2 * KC * 128])
        self.pw2 = self.din("pw2", [KC * 128, KC * 128])
        self.wq = self.din("wq", [KC * 128, KC * 128])
        self.wk = self.din("wk", [KC * 128, KC * 128])
        self.perm_in = self.din("perm", [128, 128])
        self.wv = self.din("wv", [4 * 128, KC * 512])
        self.wo = self.din("wo", [KC * 128, KC * 128])
        self.rope = self.din("rope", [4 * 128, NL])
        self.out = self.nc.dram_tensor("out", [NL, D], F32, kind="ExternalOutput").ap()
        self.XA = self.dint("XA", [D, NT])
        self.XB = self.dint("XB", [D, NT])
        self.QT = self.dint("QT", [D, NL], BF16)
        self.KT = self.dint("KT", [D, NT], BF16)
        self.VV = self.dint("VV", [NT, D], BF16)
        self.ON = self.dint("ON", [D, NL], BF16)
        self.rXA = Res("XA")
        self.rXB = Res("XB")
        self.rQT = Res("QT")
        self.rKT = Res("KT")
        self.rVV = Res("VV")
        self.rON = Res("ON")
        if self.dbg is not None:
            self.dbg_out = self.nc.dram_tensor("dbg", [D, NT], F32, kind="ExternalOutput").ap()
            self.dbgv_out = self.nc.dram_tensor("dbgv", [128, 576], F32, kind="ExternalOutput").ap()

    def _globals(self, st):
        nc, P = self.nc, self.P
        self.ps = [st.enter_context(nc.psum_tensor(f"ps{i}", [128, 512], F32)) for i in range(8)]
        self.rps = [Res(f"ps{i}", excl=True) for i in range(8)]
        self.vecs = self.sb(st, "vecs", [128, NV])
        self.rvecs = Res("vecs")
        self.ident = self.sb(st, "ident", [128, 128])
        self.ones = self.sb(st, "ones", [128, 128])
        self.rconst = Res("const")
        self.modv = self.sb(st, "modv", [128, 2 * 2 * 3 * 3 * 16])
        self.rmodv = Res("modv")
        self.scb = self.sb(st, "scb", [128, 32], BF16)
        self.rscb = Res("scb")
        self.lam = self.sb(st, "lamv", [128, 8])
        self.rlam = Res("lam")
        vecs, ident, ones = self.vecs, self.ident, self.ones
        P.dma("sp", lambda e: e.dma_start(out=vecs[:], in_=self.vecs_in[:, :]), writes=[self.rvecs])
        P.dma("sp", lambda e: e.dma_start(out=ident[:], in_=self.ident_in[:, :]), writes=[self.rconst])
        P.op("dve", lambda e: e.memset(ones[:], 1.0), writes=[self.rconst])
        self.onesb = self.sb(st, "onesb", [128, 128], BF16)
        P.op("dve", lambda e: e.memset(self.onesb[:], 1.0), writes=[self.rconst])
        self.sqb = [self.sb(st, f"sqb{k}", [128, 512], BF16) for k in range(2)]
        self.rsqb = mkres(2, "sqb")
        self.setup_eps(st)

    def mv(self, i, s, j, kind, c):
        off = ((((i * 2 + s) * 3 + j) * 3 + kind) * 16) + c
        return self.modv[:, off:off + 1]

    def vcol(self, off, n=1):
        return self.vecs[:, off:off + n]

    def _run_phases(self):
        P = self.P
        ph = self.phases
        def want(name):
            return ph is None or name in ph
        if want("tin"):
            self.phase_transpose_in(self.XA, self.rXA)
            P.drain(); P.flush()
        self.mod1_in_conv = want("conv") and want("mod")
        if want("mod"):
            self.phase_mod((0,) if self.mod1_in_conv else (0, 1))
            P.drain(); P.flush()
        full = ph is None
        self.bg_conv = []
        self.bg_attnb = []
        bg_f20 = []
        if full:
            p = self.ffn_prep(0, 2); p["pre"] = True; self.bg_conv = p["jobs"]
            p = self.ffn_prep(1, 1); p["pre"] = True; bg_f20 = p["jobs"]
            p = self.ffn_prep(1, 2); p["pre"] = True; self.bg_attnb = p["jobs"]
        if want("ffn1_0"):
            self.phase_ffn(0, 1, 0, self.XA, self.rXA, self.XB, self.rXB, True)
            P.drain(); P.flush()
        if want("conv"):
            self.phase_conv(self.XB, self.rXB, self.XA, self.rXA)
            P.drain(); P.flush()
        if want("ffn2_0"):
            self.phase_ffn(0, 2, 2, self.XA, self.rXA, self.XB, self.rXB, True, bg=bg_f20)
            P.drain(); P.flush()
        if want("ffn1_1"):
            self.phase_ffn(1, 1, 0, self.XB, self.rXB, self.XA, self.rXA, True)
            P.drain(); P.flush()
        if want("attn"):
            self.phase_attn(self.XA, self.rXA, self.XB, self.rXB)
            P.drain(); P.flush()
        if want("ffn2_1"):
            self.phase_ffn(1, 2, 2, self.XB, self.rXB, self.XA, self.rXA, False)
            P.drain(); P.flush()
        if self.dbg is not None:
            src, rsrc = (self.XA, self.rXA) if self.dbg == "A" else (self.XB, self.rXB)
            rdbg = Res("dbg")
            P.dma("sp", lambda e: e.dma_start(out=self.dbgv_out[:, :], in_=self.modv[:]), reads=[self.rmodv], writes=[Res()])
            for c in range(KC):
                P.dma("sp", lambda e, c=c: e.dma_start(out=self.dbg_out[c * 128:(c + 1) * 128, :], in_=src[c * 128:(c + 1) * 128, :]), reads=[rsrc], writes=[rdbg])
        if want("tout"):
            self.phase_transpose_out(self.XA, self.rXA)
        P.drain(); P.flush()

    def phase_transpose_in(self, XO, rXO):
        nc, P = self.nc, self.P
        with ExitStack() as st:
            xin = [self.sb(st, f"ti_x{i}", [128, D]) for i in range(2)]
            rxin = mkres(2)
            stg = [self.sb(st, f"ti_s{i}", [128, KC, 512]) for i in range(2)]
            rstg = mkres(2)
            ident = self.ident
            XOv = XO.rearrange("(c p) t -> p c t", p=128)
            groups = [(self.x_in, g * 512, 4, g * 512) for g in range(8)] + [(self.ctx_in, 0, 2, NL)]
            blk = 0
            for gi, (src, r0, nb, c0) in enumerate(groups):
                sg, rsg = stg[gi % 2], rstg[gi % 2]
                for tb in range(nb):
                    xb, rxb = xin[blk % 2], rxin[blk % 2]
                    blk += 1
                    rr = r0 + tb * 128
                    P.dma("sp", lambda e, xb=xb, src=src, rr=rr: e.dma_start(out=xb[:], in_=src[rr:rr + 128, :]), writes=[rxb])
                    for b4 in range(4):
                        pst, rpst = self.ps[b4 + 4 * (tb % 2)], self.rps[b4 + 4 * (tb % 2)]
                        for cc in range(4):
                            c = b4 * 4 + cc
                            P.op("pe", lambda e, pst=pst, cc=cc, xb=xb, c=c: e.transpose(
                                out=pst[:, cc * 128:(cc + 1) * 128], in_=xb[:, c * 128:(c + 1) * 128], identity=ident[:]),
                                reads=[rxb, self.rconst], writes=[rpst])
                        dst = sg[:, b4 * 4:(b4 + 1) * 4, tb * 128:(tb + 1) * 128]
                        srcp = pst[:].rearrange("p (c t) -> p c t", c=4)
                        if b4 % 2 == 0:
                            P.op("act", lambda e, dst=dst, srcp=srcp: e.activation(out=dst, in_=srcp, func=AF.Copy), reads=[rpst], writes=[rsg])
                        else:
                            P.op("dve", lambda e, dst=dst, srcp=srcp: e.tensor_copy(out=dst, in_=srcp), reads=[rpst], writes=[rsg])
                n = nb * 128
                P.dma("sp", lambda e, sg=sg, c0=c0, n=n: e.dma_start(out=XOv[:, :, c0:c0 + n], in_=sg[:, :, 0:n]), reads=[rsg], writes=[rXO])

    def phase_transpose_out(self, XI, rXI):
        nc, P = self.nc, self.P
        with ExitStack() as st:
            xt = [self.sb(st, f"to_x{i}", [128, KC, 512]) for i in range(2)]
            rxt = mkres(2)
            ot = [self.sb(st, f"to_o{i}", [128, D]) for i in range(2)]
            rot = mkres(2)
            rout = Res("out")
            ident = self.ident
            XIv = XI.rearrange("(c p) t -> p c t", p=128)
            blk = 0
            for g in range(8):
                xg, rxg = xt[g % 2], rxt[g % 2]
                P.dma("sp", lambda e, xg=xg, g=g: e.dma_start(out=xg[:], in_=XIv[:, :, g * 512:(g + 1) * 512]), reads=[rXI], writes=[rxg])
                for tb in range(4):
                    o, ro = ot[blk % 2], rot[blk % 2]
                    blk += 1
                    for b4 in range(4):
                        pst, rpst = self.ps[b4 + 4 * (tb % 2)], self.rps[b4 + 4 * (tb % 2)]
                        for cc in range(4):
                            c = b4 * 4 + cc
                            P.op("pe", lambda e, pst=pst, cc=cc, xg=xg, c=c, tb=tb: e.transpose(
                                out=pst[:, cc * 128:(cc + 1) * 128], in_=xg[:, c, tb * 128:(tb + 1) * 128], identity=ident[:]),
                                reads=[rxg, self.rconst], writes=[rpst])
                        dst = o[:, b4 * 512:(b4 + 1) * 512]
                        if b4 % 2 == 0:
                            P.op("act", lambda e, dst=dst, pst=pst: e.activation(out=dst, in_=pst[:], func=AF.Copy), reads=[rpst], writes=[ro])
                        else:
                            P.op("dve", lambda e, dst=dst, pst=pst: e.tensor_copy(out=dst, in_=pst[:]), reads=[rpst], writes=[ro])
                    r0 = g * 512 + tb * 128
                    P.dma("sp", lambda e, o=o, r0=r0: e.dma_start(out=self.out[r0:r0 + 128, :], in_=o[:]), reads=[ro], writes=[rout])

    def mod_pieces(self, layer, wb, rwb, raw, rraw, banks):
        P = self.P
        OCB = 2
        vecs = self.vecs
        scb = self.scb
        adv = self.adaw.rearrange("(g o p) k -> g p o k", p=128, o=OCB)
        ng = 144 // OCB
        pieces = []
        for g in range(ng):
            def dma_part(g=g):
                w, rw = wb[g % len(wb)], rwb[g % len(wb)]
                gg = layer * ng + g
                for o in range(OCB):
                    P.dma("pool", lambda e, o=o: e.dma_start(out=w[:, o, :], in_=adv[gg][:, o, :], max_dma_last_dim=2048), writes=[rw[o]])

            def compute_part(g=g):
                w, rw = wb[g % len(wb)], rwb[g % len(wb)]
                pst, rpst = self.ps[banks[g % 2]], self.rps[banks[g % 2]]
                for o in range(OCB):
                    for kc in range(KC):
                        P.op("pe", lambda e, o=o, kc=kc: e.matmul(
                            out=pst[:, o * 2:o * 2 + 2], lhsT=w[:, o, kc * 128:(kc + 1) * 128], rhs=scb[:, kc * 2:kc * 2 + 2],
                            start=(kc == 0), stop=(kc == KC - 1)), reads=[rw[o], self.rscb], writes=[rpst])
                for o in range(OCB):
                    oc = g * OCB + o
                    ioc = layer * 144 + oc
                    P.op("dve", lambda e, o=o, oc=oc, ioc=ioc: e.tensor_scalar(
                        out=raw[:, oc * 2:oc * 2 + 2], in0=pst[:, o * 2:o * 2 + 2], scalar1=vecs[:, V_ADAB + ioc:V_ADAB + ioc + 1],
                        scalar2=None, op0=ALU.add), reads=[rpst, self.rvecs], writes=[rraw])
            pieces.append((dma_part, compute_part))
        return pieces

    def mod_derive(self, i, raw, rraw):
        P = self.P
        vecs = self.vecs
        modv = self.modv
        for s in range(2):
            for j in range(3):
                wgt = 0.5 if j != 1 else 1.0

                def rawv(r):
                    b0 = (((3 * j + r) * 16) * 2) + s
                    return raw[:, b0:b0 + 31:2]
                offA = (((i * 2 + s) * 3 + j) * 3 + 0) * 16
                offB = offA + 16
                offG = offA + 32
                npre = vecs[:, V_NPRE + (i * 3 + j) * 16:V_NPRE + (i * 3 + j) * 16 + 16]
                npost = vecs[:, V_NPOST + (i * 3 + j) * 16:V_NPOST + (i * 3 + j) * 16 + 16]
                P.op("dve", lambda e, o=offA, a=rawv(1), b=npre: e.scalar_tensor_tensor(
                    out=modv[:, o:o + 16], in0=a, scalar=1.0, in1=b, op0=ALU.add, op1=ALU.mult),
                    reads=[rraw, self.rvecs], writes=[self.rmodv])
                P.op("dve", lambda e, o=offB, a=rawv(0): e.tensor_copy(out=modv[:, o:o + 16], in_=a),
                     reads=[rraw], writes=[self.rmodv])
                P.op("dve", lambda e, o=offG, a=rawv(2), b=npost, wgt=wgt: e.scalar_tensor_tensor(
                    out=modv[:, o:o + 16], in0=a, scalar=wgt, in1=b, op0=ALU.mult, op1=ALU.mult),
                    reads=[rraw, self.rvecs], writes=[self.rmodv])

    def phase_mod(self, layers=(0,)):
        nc, P = self.nc, self.P
        with ExitStack() as st:
            sc = self.sb(st, "mod_sc", [128, 32])
            rsc = Res("sc")
            vecs = self.vecs
            P.op("act", lambda e: e.activation(out=sc[:], in_=vecs[:, V_CC:V_CC + 32], func=AF.Silu), reads=[self.rvecs], writes=[rsc])
            P.op("dve", lambda e: e.tensor_copy(out=self.scb[:], in_=sc[:]), reads=[rsc], writes=[self.rscb])
            wb = [self.sb(st, f"mod_w{k}", [128, 2, 2048], BF16) for k in range(6)]
            rwb = [mkres(2) for _ in range(6)]
            for layer in layers:
                raw = self.sb(st, f"mod_raw{layer}", [128, 144 * 2])
                rraw = Res("raw")
                for dma_part, compute_part in self.mod_pieces(layer, wb, rwb, raw, rraw, (0, 1)):
                    dma_part()
                    compute_part()
                self.mod_derive(layer, raw, rraw)

    def rstd_from_sumsq(self, pst, rpst, n, out, rout, tmp, rtmp, eps):
        P = self.P
        P.op("act", lambda e: e.activation(out=tmp[:, 0:n], in_=pst[:, 0:n], func=AF.Sqrt, bias=self.epsv(eps), scale=1.0 / D),
             reads=[rpst, self.rconst], writes=[rtmp])
        P.op("dve", lambda e: e.reciprocal(out=out[:, 0:n], in_=tmp[:, 0:n]), reads=[rtmp], writes=[rout])

    def _stat_mm(self, item, pst, rpst, n, last=KC - 1):
        tq, rtq, dc = item
        ones = self.onesb
        self.P.op("pe", lambda e: e.matmul(out=pst[:, 0:n], lhsT=ones[:], rhs=tq[:, 0:n], start=(dc == 0), stop=(dc == last)),
                  reads=[rtq, self.rconst], writes=[rpst])

    def epsv(self, eps):
        return self.epst[:, self.eps_idx[eps]:self.eps_idx[eps] + 1]

    def setup_eps(self, st):
        self.epst = self.sb(st, "epst", [128, 4])
        self.eps_idx = {}
        for k, v in enumerate(sorted({NORM_EPS, SUBLN_EPS, LN_EPS})):
            self.eps_idx[v] = k
            self.P.op("dve", lambda e, k=k, v=v: e.memset(self.epst[:, k:k + 1], v), writes=[self.rconst])

    def load_w_cast(self, dst, rdst, src_ap):
        self.P.dma("pool", lambda e: e.dma_start(out=dst, in_=src_ap, max_dma_last_dim=2048), writes=[rdst])

    def ffn_prep(self, i, w):
        if not hasattr(self, "_ffn_prep"):
            self._ffn_prep = {}
        if (i, w) in self._ffn_prep:
            return self._ffn_prep[(i, w)]
        P = self.P
        NQ, QF = 4, FC // 4
        win, wout = self.win[(i, w)], self.wout[(i, w)]
        winv = win.rearrange("(f p) k -> f p k", p=128)
        woutv = wout.rearrange("(d p) k -> d p k", p=128)
        WinS = self.dint(f"wins{i}{w}", [FC * 2 * 128, KC * 128], BF16).rearrange("(f h p) k -> f h p k", h=2, p=128)
        WoutS = self.dint(f"wouts{i}{w}", [KC * NQ * 128, QF * 128], BF16).rearrange("(d q p) k -> d q p k", q=NQ, p=128)
        rWinS = [[Res("wins") for _ in range(2)] for _ in range(FC)]
        rWoutS = [[Res("wouts") for _ in range(NQ)] for _ in range(KC)]
        jobs = []
        for fc in range(FC):
            for half in range(2):
                jobs.append(lambda fc=fc, half=half: P.dma("pool", lambda e: e.dma_start(
                    out=WinS[fc, half], in_=winv[fc][:, half * KC * 128:(half + 1) * KC * 128], max_dma_last_dim=2048), writes=[rWinS[fc][half]]))
        for dc in range(KC):
            for q in range(NQ):
                jobs.append(lambda dc=dc, q=q: P.dma("pool", lambda e: e.dma_start(
                    out=WoutS[dc, q], in_=woutv[dc][:, q * QF * 128:(q + 1) * QF * 128], max_dma_last_dim=2048), writes=[rWoutS[dc][q]]))
        d = {"WinS": WinS, "WoutS": WoutS, "rWinS": rWinS, "rWoutS": rWoutS, "jobs": jobs, "pre": False}
        self._ffn_prep[(i, w)] = d
        return d

    def run_jobs(self, jobs, n):
        for _ in range(n):
            if jobs:
                jobs.pop(0)()

    def phase_ffn(self, i, w, j, XI, rXI, XO, rXO, with_ctx, bg=None):
        nc, P = self.nc, self.P
        prep = self.ffn_prep(i, w)
        pre = prep["pre"]
        bg = bg if bg is not None else []
        win, wout = self.win[(i, w)], self.wout[(i, w)]
        winv = win.rearrange("(f p) k -> f p k", p=128)
        woutv = wout.rearrange("(d p) k -> d p k", p=128)
        XIv = XI.rearrange("(c p) t -> p c t", p=128)
        XOv = XO.rearrange("(c p) t -> p c t", p=128)
        tiles = [(g * 512, 512, 0) for g in range(8)] + ([(NL, 256, 1)] if with_ctx else [])
        NQ = 4
        QF = FC // NQ
        with ExitStack() as st:
            xts = [self.sb(st, f"f_x{k}", [128, KC, 512]) for k in range(2)]
            rxs = [mkres(KC, "x") for _ in range(2)]
            h = self.sb(st, "f_h", [128, KC, 512], BF16)
            rh = mkres(KC, "h")
            act = self.sb(st, "f_act", [128, FC, 512], BF16)
            ract = mkres(FC, "act")
            y = self.sb(st, "f_y", [128, KC, 512])
            ry = mkres(KC, "y")
            wi = [self.sb(st, f"f_wi{k}", [128, KC * 128], BF16) for k in range(4)]
            rwi = mkres(4, "wi")
            wo = [self.sb(st, f"f_wo{k}", [128, QF * 128], BF16) for k in range(4)]
            rwo = mkres(4, "wo")
            tmp = [self.sb(st, f"f_t{k}", [128, 512]) for k in range(4)]
            rtmp = mkres(4, "tmp")
            rstd = self.sb(st, "f_rstd", [128, 512])
            rrstd = Res("rstd")
            rstd2 = self.sb(st, "f_rstd2", [128, 512])
            rrstd2 = Res("rstd2")
            ones = self.ones
            cnt = {"wi": 0, "wo": 0}
            WinS, WoutS, rWinS, rWoutS = prep["WinS"], prep["WoutS"], prep["rWinS"], prep["rWoutS"]

            def load_x(k):
                t0, n, s = tiles[k]
                xt, rx = xts[k % 2], rxs[k % 2]
                P.dma("pool", lambda e: e.dma_start(out=xt[:, :, 0:n], in_=XIv[:, :, t0:t0 + n]), reads=[rXI], writes=rx)

            def prenorm(k):
                t0, n, s = tiles[k]
                xt, rx = xts[k % 2], rxs[k % 2]
                pst, rpst = self.ps[6], self.rps[6]
                for c in range(KC):
                    tq, rtq = self.sqb[c % 2], self.rsqb[c % 2]
                    P.op("act", lambda e, tq=tq, c=c: e.activation(out=tq[:, 0:n], in_=xt[:, c, 0:n], func=AF.Square), reads=[rx[c]], writes=[rtq])
                    P.op("pe", lambda e, tq=tq, c=c: e.matmul(out=pst[:, 0:n], lhsT=self.onesb[:], rhs=tq[:, 0:n], start=(c == 0), stop=(c == KC - 1)),
                         reads=[rtq, self.rconst], writes=[rpst])
                self.rstd_from_sumsq(pst, rpst, n, rstd, rrstd, tmp[2], rtmp[2], NORM_EPS)
                for c in range(KC):
                    tq, rtq = tmp[2 + c % 2], rtmp[2 + c % 2]
                    P.op("dve", lambda e, tq=tq, c=c: e.tensor_tensor(out=tq[:, 0:n], in0=xt[:, c, 0:n], in1=rstd[:, 0:n], op=ALU.mult),
                         reads=[rx[c], rrstd], writes=[rtq])
                    P.op("act", lambda e, tq=tq, c=c: e.activation(out=h[:, c, 0:n], in_=tq[:, 0:n], func=AF.Identity,
                                                                   scale=self.mv(i, s, j, 0, c), bias=self.mv(i, s, j, 1, c)),
                         reads=[rtq, self.rmodv], writes=[rh[c]])

            def instage(k):
                t0, n, s = tiles[k]
                for fc in range(FC):
                    pa, rpa = self.ps[fc % 2], self.rps[fc % 2]
                    pu, rpu = self.ps[2 + fc % 2], self.rps[2 + fc % 2]
                    for half, (pp, rpp) in enumerate(((pa, rpa), (pu, rpu))):
                        wt, rwt = wi[cnt["wi"] % 4], rwi[cnt["wi"] % 4]
                        cnt["wi"] += 1
                        if k == 0 and not pre:
                            self.load_w_cast(wt[:], rwt, winv[fc][:, half * KC * 128:(half + 1) * KC * 128])
                            if len(tiles) > 1:
                                P.dma("sp", lambda e, wt=wt, fc=fc, half=half: e.dma_start(out=WinS[fc, half], in_=wt[:]), reads=[rwt], writes=[rWinS[fc][half]])
                        else:
                            P.dma("sp", lambda e, wt=wt, fc=fc, half=half: e.dma_start(out=wt[:], in_=WinS[fc, half]), reads=[rWinS[fc][half]], writes=[rwt])
                        for kc in range(KC):
                            P.op("pe", lambda e, pp=pp, wt=wt, kc=kc: e.matmul(out=pp[:, 0:n], lhsT=wt[:, kc * 128:(kc + 1) * 128], rhs=h[:, kc, 0:n],
                                                                                start=(kc == 0), stop=(kc == KC - 1)), reads=[rwt, rh[kc]], writes=[rpp])
                    tq, rtq = tmp[fc % 2], rtmp[fc % 2]
                    P.op("act", lambda e, tq=tq, pa=pa: e.activation(out=tq[:, 0:n], in_=pa[:, 0:n], func=AF.Silu), reads=[rpa], writes=[rtq])
                    P.op("dve", lambda e, tq=tq, pu=pu, fc=fc: e.tensor_tensor(out=act[:, fc, 0:n], in0=pu[:, 0:n], in1=tq[:, 0:n], op=ALU.mult),
                         reads=[rpu, rtq], writes=[ract[fc]])

            def outstage(k):
                t0, n, s = tiles[k]
                pst2, rpst2 = self.ps[7], self.rps[7]
                pend = []
                for dc in range(KC):
                    py, rpy = self.ps[4 + dc % 2], self.rps[4 + dc % 2]
                    for q in range(NQ):
                        wt, rwt = wo[cnt["wo"] % 4], rwo[cnt["wo"] % 4]
                        cnt["wo"] += 1
                        if k == 0 and not pre:
                            self.load_w_cast(wt[:], rwt, woutv[dc][:, q * QF * 128:(q + 1) * QF * 128])
                            if len(tiles) > 1:
                                P.dma("sp", lambda e, wt=wt, dc=dc, q=q: e.dma_start(out=WoutS[dc, q], in_=wt[:]), reads=[rwt], writes=[rWoutS[dc][q]])
                        else:
                            P.dma("sp", lambda e, wt=wt, dc=dc, q=q: e.dma_start(out=wt[:], in_=WoutS[dc, q]), reads=[rWoutS[dc][q]], writes=[rwt])
                        for f in range(QF):
                            fc = q * QF + f
                            P.op("pe", lambda e, py=py, wt=wt, f=f, fc=fc: e.matmul(out=py[:, 0:n], lhsT=wt[:, f * 128:(f + 1) * 128], rhs=act[:, fc, 0:n],
                                                                                     start=(fc == 0), stop=(fc == FC - 1)), reads=[rwt, ract[fc]], writes=[rpy])
                    tq, rtq = self.sqb[dc % 2], self.rsqb[dc % 2]
                    P.op("dve", lambda e, py=py, dc=dc: e.tensor_copy(out=y[:, dc, 0:n], in_=py[:, 0:n]), reads=[rpy], writes=[ry[dc]])
                    P.op("act", lambda e, tq=tq, dc=dc: e.activation(out=tq[:, 0:n], in_=y[:, dc, 0:n], func=AF.Square), reads=[ry[dc]], writes=[rtq])
                    pend.append((tq, rtq, dc))
                    if len(pend) > 1:
                        self._stat_mm(pend.pop(0), pst2, rpst2, n)
                self._stat_mm(pend.pop(0), pst2, rpst2, n)
                self.rstd_from_sumsq(pst2, rpst2, n, rstd2, rrstd2, tmp[0], rtmp[0], NORM_EPS)

            def post(k):
                t0, n, s = tiles[k]
                xt, rx = xts[k % 2], rxs[k % 2]
                for c in range(KC):
                    tq, rtq = tmp[c % 2], rtmp[c % 2]
                    P.op("dve", lambda e, tq=tq, c=c: e.tensor_tensor(out=tq[:, 0:n], in0=y[:, c, 0:n], in1=rstd2[:, 0:n], op=ALU.mult),
                         reads=[ry[c], rrstd2], writes=[rtq])
                    P.op("dve", lambda e, tq=tq, c=c: e.scalar_tensor_tensor(out=y[:, c, 0:n], in0=tq[:, 0:n], scalar=self.mv(i, s, j, 2, c), in1=xt[:, c, 0:n],
                                                                              op0=ALU.mult, op1=ALU.add),
                         reads=[rtq, rx[c], self.rmodv], writes=[ry[c]])
                P.dma("pool", lambda e: e.dma_start(out=XOv[:, :, t0:t0 + n], in_=y[:, :, 0:n]), reads=ry, writes=[rXO])

            NTI = len(tiles)
            load_x(0)
            if NTI > 1:
                load_x(1)
            prenorm(0)
            bper = (len(bg) + NTI - 1) // NTI
            for k in range(NTI):
                instage(k)
                self.run_jobs(bg, bper)
                if k + 1 < NTI:
                    prenorm(k + 1)
                outstage(k)
                post(k)
                if k + 2 < NTI:
                    load_x(k + 2)
            self.run_jobs(bg, len(bg))

    def phase_conv(self, XI, rXI, XO, rXO):
        nc, P = self.nc, self.P
        i, j = 0, 1
        E = 512 + 2 * PAD
        pw1v = self.pw1.rearrange("(o p) k -> o p k", p=128)
        pw2v = self.pw2.rearrange("(o p) k -> o p k", p=128)
        XIv = XI.rearrange("(c p) t -> p c t", p=128)
        XOv = XO.rearrange("(c p) t -> p c t", p=128)
        tiles = [(g * 512, 512, 0, 0, NL) for g in range(8)] + [(NL, 256, 1, NL, NT)]
        vecs = self.vecs
        ones = self.ones
        with ExitStack() as st:
            xe = self.sb(st, "c_x", [128, KC, E])
            rx = mkres(KC, "x")
            h = self.sb(st, "c_h", [128, KC, E], BF16)
            rh = mkres(KC, "h")
            u = self.sb(st, "c_u", [128, KC, E], BF16)
            ru = mkres(KC, "u")
            dg = [self.sb(st, f"c_dg{k}", [128, CW, 128], BF16) for k in range(2)]
            rdg = [mkres(CW, "dg") for _ in range(2)]
            v = self.sb(st, "c_v", [128, KC, 512])
            rv = mkres(KC, "v")
            zt = self.sb(st, "c_z", [128, KC, 512], BF16)
            rz = mkres(KC, "z")
            w1 = [self.sb(st, f"c_w1{k}", [128, 2 * KC * 128], BF16) for k in range(2)]
            rw1 = mkres(2, "w1")
            w2 = [self.sb(st, f"c_w2{k}", [128, KC * 128], BF16) for k in range(2)]
            rw2 = mkres(2, "w2")
            tmp = [self.sb(st, f"c_t{k}", [128, E]) for k in range(4)]
            rtmp = mkres(4, "tmp")
            rstd_e = self.sb(st, "c_rstde", [128, E])
            rrstd_e = Res("rstde")
            mean = self.sb(st, "c_mean", [128, 512])
            rmean = Res("mean")
            rstd = self.sb(st, "c_rstd", [128, 512])
            rrstd = Res("rstd")
            rstd2 = self.sb(st, "c_rstd2", [128, 512])
            rrstd2 = Res("rstd2")
            nw1 = 0
            nw2 = 0
            mwb = [self.sb(st, f"c_mw{k}", [128, 2, 2048], BF16) for k in range(2)]
            rmwb = [mkres(2, "mw") for _ in range(2)]
            mraw = self.sb(st, "c_mraw", [128, 144 * 2])
            rmraw = Res("mraw")
            mpieces = self.mod_pieces(1, mwb, rmwb, mraw, rmraw, (6, 7)) if self.mod1_in_conv else []
            mq_dma = [p[0] for p in mpieces]
            mq_cmp = [p[1] for p in mpieces]
            mper = (len(mpieces) + len(tiles) - 1) // len(tiles)
            W1S = self.dint("pw1s", [KC * 128, 2 * KC * 128], BF16).rearrange("(o p) k -> o p k", p=128)
            W2S = self.dint("pw2s", [KC * 128, KC * 128], BF16).rearrange("(o p) k -> o p k", p=128)
            rW1S = mkres(KC, "w1s")
            rW2S = mkres(KC, "w2s")
            for tidx, (t0, n, s, s0, s1) in enumerate(tiles):
                lo = max(t0 - PAD, s0)
                hi = min(t0 + n + PAD, s1)
                ne = hi - lo
                eo = lo - (t0 - PAD)
                pieces = [(eo, eo + min(ne, 512))]
                if ne > 512:
                    pieces.append((eo + 512, eo + ne))
                P.dma("pool", lambda e, lo=lo, hi=hi, eo=eo, ne=ne: e.dma_start(out=xe[:, :, eo:eo + ne], in_=XIv[:, :, lo:hi]), reads=[rXI], writes=rx)
                for pi, (a, b) in enumerate(pieces):
                    pst, rpst = self.ps[6 + pi], self.rps[6 + pi]
                    m = b - a
                    for c in range(KC):
                        tq, rtq = self.sqb[c % 2], self.rsqb[c % 2]
                        P.op("act", lambda e, tq=tq, c=c, a=a, b=b, m=m: e.activation(out=tq[:, 0:m], in_=xe[:, c, a:b], func=AF.Square), reads=[rx[c]], writes=[rtq])
                        P.op("pe", lambda e, tq=tq, c=c, m=m, pst=pst: e.matmul(out=pst[:, 0:m], lhsT=self.onesb[:], rhs=tq[:, 0:m], start=(c == 0), stop=(c == KC - 1)),
                             reads=[rtq, self.rconst], writes=[rpst])
                    tq, rtq = tmp[2], rtmp[2]
                    P.op("act", lambda e, tq=tq, pst=pst, m=m: e.activation(out=tq[:, 0:m], in_=pst[:, 0:m], func=AF.Sqrt, bias=self.epsv(NORM_EPS), scale=1.0 / D),
                         reads=[rpst, self.rconst], writes=[rtq])
                    P.op("dve", lambda e, tq=tq, a=a, b=b, m=m: e.reciprocal(out=rstd_e[:, a:b], in_=tq[:, 0:m]), reads=[rtq], writes=[rrstd_e])
                for c in range(KC):
                    tq = tmp[2 + c % 2]
                    rtq = rtmp[2 + c % 2]
                    P.op("dve", lambda e, c=c, eo=eo, ne=ne, tq=tq: e.tensor_tensor(out=tq[:, 0:ne], in0=xe[:, c, eo:eo + ne], in1=rstd_e[:, eo:eo + ne], op=ALU.mult),
                         reads=[rx[c], rrstd_e], writes=[rtq])
                    P.op("act", lambda e, c=c, eo=eo, ne=ne, s=s, tq=tq: e.activation(out=h[:, c, eo:eo + ne], in_=tq[:, 0:ne], func=AF.Identity,
                                                                                      scale=self.mv(i, s, j, 0, c), bias=self.mv(i, s, j, 1, c)),
                         reads=[rtq, self.rmodv], writes=[rh[c]])
                if eo > 0:
                    P.op("dve", lambda e, eo=eo: e.memset(u[:, :, 0:eo], 0.0), reads=rh, writes=ru)
                if eo + ne < n + 2 * PAD:
                    P.op("dve", lambda e, eo=eo, ne=ne, n=n: e.memset(u[:, :, eo + ne:n + 2 * PAD], 0.0), reads=rh, writes=ru)
                k2 = 0
                issued = 0
                for oc in range(KC):
                    outstanding = len(mq_cmp) - len(mq_dma)
                    if outstanding >= 2 or (outstanding > 0 and (issued >= mper or not mq_dma)):
                        mq_cmp.pop(0)()
                    if issued < mper and mq_dma and (len(mq_cmp) - len(mq_dma)) < 2:
                        mq_dma.pop(0)()
                        issued += 1
                    wt, rwt = w1[nw1 % 2], rw1[nw1 % 2]
                    nw1 += 1
                    if tidx == 0:
                        self.load_w_cast(wt[:], rwt, pw1v[oc])
                        P.dma("sp", lambda e, wt=wt, oc=oc: e.dma_start(out=W1S[oc], in_=wt[:]), reads=[rwt], writes=[rW1S[oc]])
                    else:
                        P.dma("sp", lambda e, wt=wt, oc=oc: e.dma_start(out=wt[:], in_=W1S[oc]), reads=[rW1S[oc]], writes=[rwt])
                    for (a, b) in pieces:
                        m = b - a
                        pa, rpa = self.ps[k2 % 2], self.rps[k2 % 2]
                        pg, rpg = self.ps[2 + k2 % 2], self.rps[2 + k2 % 2]
                        tq, rtq = tmp[k2 % 2], rtmp[k2 % 2]
                        k2 += 1
                        for kc in range(KC):
                            P.op("pe", lambda e, pa=pa, wt=wt, kc=kc, a=a, b=b, m=m: e.matmul(out=pa[:, 0:m], lhsT=wt[:, kc * 128:(kc + 1) * 128], rhs=h[:, kc, a:b],
                                                                                              start=(kc == 0), stop=(kc == KC - 1)), reads=[rwt, rh[kc]], writes=[rpa])
                        for kc in range(KC):
                            P.op("pe", lambda e, pg=pg, wt=wt, kc=kc, a=a, b=b, m=m: e.matmul(out=pg[:, 0:m], lhsT=wt[:, (KC + kc) * 128:(KC + kc + 1) * 128], rhs=h[:, kc, a:b],
                                                                                              start=(kc == 0), stop=(kc == KC - 1)), reads=[rwt, rh[kc]], writes=[rpg])
                        P.op("act", lambda e, tq=tq, pg=pg, m=m, oc=oc: e.activation(out=tq[:, 0:m], in_=pg[:, 0:m], func=AF.Sigmoid,
                                                                                      bias=vecs[:, V_BPW1 + 16 + oc:V_BPW1 + 16 + oc + 1]),
                             reads=[rpg, self.rvecs], writes=[rtq])
                        P.op("dve", lambda e, tq=tq, pa=pa, m=m, oc=oc, a=a, b=b: e.scalar_tensor_tensor(
                            out=u[:, oc, a:b], in0=pa[:, 0:m], scalar=vecs[:, V_BPW1 + oc:V_BPW1 + oc + 1], in1=tq[:, 0:m], op0=ALU.add, op1=ALU.mult),
                            reads=[rpa, rtq, self.rvecs], writes=[ru[oc]])
                while len(mq_cmp) > len(mq_dma):
                    mq_cmp.pop(0)()
                self.run_jobs(self.bg_conv, (152 + len(tiles) - 1) // len(tiles))
                ps1, rps1 = self.ps[6], self.rps[6]
                ps2, rps2 = self.ps[7], self.rps[7]

                def ln_stats(c, n=n):
                    tq, rtq = self.sqb[c % 2], self.rsqb[c % 2]
                    P.op("pe", lambda e: e.matmul(out=ps1[:, 0:n], lhsT=ones[:], rhs=v[:, c, 0:n], start=(c == 0), stop=(c == KC - 1)),
                         reads=[rv[c], self.rconst], writes=[rps1])
                    P.op("act", lambda e: e.activation(out=tq[:, 0:n], in_=v[:, c, 0:n], func=AF.Square), reads=[rv[c]], writes=[rtq])
                    P.op("pe", lambda e: e.matmul(out=ps2[:, 0:n], lhsT=self.onesb[:], rhs=tq[:, 0:n], start=(c == 0), stop=(c == KC - 1)),
                         reads=[rtq, self.rconst], writes=[rps2])
                for c in range(KC):
                    dgt, rdgt = dg[c % 2], rdg[c % 2]
                    for k in range(CW):
                        wk = vecs[:, V_WDW + c * CW + k:V_WDW + c * CW + k + 1]
                        if k % 2 == 0:
                            P.op("act", lambda e, dgt=dgt, k=k, wk=wk: e.activation(out=dgt[:, k, :], in_=self.ident[:], func=AF.Identity, scale=wk),
                                 reads=[self.rconst, self.rvecs], writes=[rdgt[k]])
                        else:
                            P.op("dve", lambda e, dgt=dgt, k=k, wk=wk: e.tensor_scalar(out=dgt[:, k, :], in0=self.ident[:], scalar1=wk, scalar2=None, op0=ALU.mult),
                                 reads=[self.rconst, self.rvecs], writes=[rdgt[k]])
                    pv, rpv = self.ps[4 + c % 2], self.rps[4 + c % 2]
                    for k in range(CW):
                        P.op("pe", lambda e, pv=pv, dgt=dgt, k=k, c=c, n=n: e.matmul(out=pv[:, 0:n], lhsT=dgt[:, k, :], rhs=u[:, c, k:k + n], start=(k == 0), stop=(k == CW - 1)),
                             reads=[rdgt[k], ru[c]], writes=[rpv])
                    P.op("dve", lambda e, pv=pv, c=c, n=n: e.tensor_scalar(out=v[:, c, 0:n], in0=pv[:, 0:n], scalar1=vecs[:, V_BDW + c:V_BDW + c + 1], scalar2=None, op0=ALU.add),
                         reads=[rpv, self.rvecs], writes=[rv[c]])
                    if c >= 1:
                        ln_stats(c - 1)
                ln_stats(KC - 1)
                P.op("dve", lambda e, n=n: e.tensor_scalar(out=mean[:, 0:n], in0=ps1[:, 0:n], scalar1=1.0 / D, scalar2=None, op0=ALU.mult), reads=[rps1], writes=[rmean])
                P.op("act", lambda e, n=n: e.activation(out=tmp[2][:, 0:n], in_=mean[:, 0:n], func=AF.Square), reads=[rmean], writes=[rtmp[2]])
                P.op("dve", lambda e, n=n: e.scalar_tensor_tensor(out=tmp[3][:, 0:n], in0=ps2[:, 0:n], scalar=1.0 / D, in1=tmp[2][:, 0:n], op0=ALU.mult, op1=ALU.subtract),
                     reads=[rps2, rtmp[2]], writes=[rtmp[3]])
                P.op("act", lambda e, n=n: e.activation(out=tmp[2][:, 0:n], in_=tmp[3][:, 0:n], func=AF.Sqrt, bias=self.epsv(LN_EPS), scale=1.0),
                     reads=[rtmp[3], self.rconst], writes=[rtmp[2]])
                P.op("dve", lambda e, n=n: e.reciprocal(out=rstd[:, 0:n], in_=tmp[2][:, 0:n]), reads=[rtmp[2]], writes=[rrstd])
                for c in range(KC):
                    tq, rtq = tmp[c % 2], rtmp[c % 2]
                    P.op("dve", lambda e, tq=tq, c=c, n=n: e.tensor_tensor(out=tq[:, 0:n], in0=v[:, c, 0:n], in1=mean[:, 0:n], op=ALU.subtract), reads=[rv[c], rmean], writes=[rtq])
                    P.op("dve", lambda e, c=c, n=n, tq=tq: e.tensor_tensor(out=v[:, c, 0:n], in0=tq[:, 0:n], in1=rstd[:, 0:n], op=ALU.mult), reads=[rtq, rrstd], writes=[rv[c]])
                    P.op("act", lambda e, c=c, n=n: e.activation(out=zt[:, c, 0:n], in_=v[:, c, 0:n], func=AF.Silu,
                                                                 scale=vecs[:, V_LNG + c:V_LNG + c + 1], bias=vecs[:, V_LNB + c:V_LNB + c + 1]),
                         reads=[rv[c], self.rvecs], writes=[rz[c]])
                y, ry = v, rv
                pst2, rpst2 = self.ps[6], self.rps[6]
                pend = []
                for dc in range(KC):
                    wt, rwt = w2[nw2 % 2], rw2[nw2 % 2]
                    nw2 += 1
                    if tidx == 0:
                        self.load_w_cast(wt[:], rwt, pw2v[dc])
                        P.dma("sp", lambda e, wt=wt, dc=dc: e.dma_start(out=W2S[dc], in_=wt[:]), reads=[rwt], writes=[rW2S[dc]])
                    else:
                        P.dma("sp", lambda e, wt=wt, dc=dc: e.dma_start(out=wt[:], in_=W2S[dc]), reads=[rW2S[dc]], writes=[rwt])
                    py, rpy = self.ps[4 + dc % 2], self.rps[4 + dc % 2]
                    for kc in range(KC):
                        P.op("pe", lambda e, py=py, wt=wt, kc=kc, n=n: e.matmul(out=py[:, 0:n], lhsT=wt[:, kc * 128:(kc + 1) * 128], rhs=zt[:, kc, 0:n],
                                                                                 start=(kc == 0), stop=(kc == KC - 1)), reads=[rwt, rz[kc]], writes=[rpy])
                    tq, rtq = self.sqb[dc % 2], self.rsqb[dc % 2]
                    P.op("dve", lambda e, py=py, dc=dc, n=n: e.tensor_scalar(out=y[:, dc, 0:n], in0=py[:, 0:n], scalar1=vecs[:, V_BPW2 + dc:V_BPW2 + dc + 1],
                                                                              scalar2=None, op0=ALU.add), reads=[rpy, self.rvecs], writes=[ry[dc]])
                    P.op("act", lambda e, tq=tq, dc=dc, n=n: e.activation(out=tq[:, 0:n], in_=y[:, dc, 0:n], func=AF.Square), reads=[ry[dc]], writes=[rtq])
                    pend.append((tq, rtq, dc))
                    if len(pend) > 1:
                        self._stat_mm(pend.pop(0), pst2, rpst2, n)
                self._stat_mm(pend.pop(0), pst2, rpst2, n)
                self.rstd_from_sumsq(pst2, rpst2, n, rstd2, rrstd2, tmp[0], rtmp[0], NORM_EPS)
                for c in range(KC):
                    tq, rtq = tmp[c % 2], rtmp[c % 2]
                    P.op("dve", lambda e, tq=tq, c=c, n=n: e.tensor_tensor(out=tq[:, 0:n], in0=y[:, c, 0:n], in1=rstd2[:, 0:n], op=ALU.mult),
                         reads=[ry[c], rrstd2], writes=[rtq])
                    P.op("dve", lambda e, tq=tq, c=c, n=n, s=s: e.scalar_tensor_tensor(out=y[:, c, 0:n], in0=tq[:, 0:n], scalar=self.mv(i, s, j, 2, c), in1=xe[:, c, PAD:PAD + n],
                                                                                        op0=ALU.mult, op1=ALU.add),
                         reads=[rtq, rx[c], self.rmodv], writes=[ry[c]])
                P.dma("sp", lambda e, t0=t0, n=n: e.dma_start(out=XOv[:, :, t0:t0 + n], in_=y[:, :, 0:n]), reads=ry, writes=[rXO])
            self.run_jobs(self.bg_conv, len(self.bg_conv))
            if self.mod1_in_conv:
                while mq_cmp:
                    if len(mq_dma) == len(mq_cmp):
                        mq_dma.pop(0)()
                    mq_cmp.pop(0)()
                self.mod_derive(1, mraw, rmraw)

    def phase_attn(self, XI, rXI, XO, rXO):
        self.attn_a(XI, rXI)
        self.P.drain(); self.P.flush()
        self.attn_b()
        self.P.drain(); self.P.flush()
        self.attn_c(XI, rXI, XO, rXO)

    def attn_a(self, XI, rXI):
        nc, P = self.nc, self.P
        i, j = 1, 1
        XIv = XI.rearrange("(c p) t -> p c t", p=128)
        ropev = self.rope.rearrange("(k p) t -> p k t", p=128)
        wv_v = self.wv.rearrange("(b p) k -> b p k", p=128)
        ones = self.ones
        vecs = self.vecs
        lam_init = 0.8 - 0.6 * math.exp(-0.3 * 1)
        lam = self.lam
        with ExitStack() as st0:
            lt = self.sb(st0, "a_lt", [128, 128])
            rlt = Res("lt")
            for q in range(2):
                P.op("dve", lambda e, q=q: e.tensor_tensor(out=lt[:, q * 64:(q + 1) * 64], in0=vecs[:, V_LAM + q * 128:V_LAM + q * 128 + 64],
                                                            in1=vecs[:, V_LAM + q * 128 + 64:V_LAM + q * 128 + 128], op=ALU.mult),
                     reads=[self.rvecs], writes=[rlt])
                P.op("dve", lambda e, q=q: e.tensor_reduce(out=lam[:, q:q + 1], in_=lt[:, q * 64:(q + 1) * 64], axis=mybir.AxisListType.X, op=ALU.add),
                     reads=[rlt], writes=[self.rlam])
            P.op("act", lambda e: e.activation(out=lam[:, 2:4], in_=lam[:, 0:2], func=AF.Exp), reads=[self.rlam], writes=[self.rlam])
            P.op("dve", lambda e: e.tensor_tensor(out=lam[:, 4:5], in0=lam[:, 3:4], in1=lam[:, 2:3], op=ALU.subtract), reads=[self.rlam], writes=[self.rlam])
            P.op("dve", lambda e: e.tensor_scalar(out=lam[:, 5:6], in0=lam[:, 4:5], scalar1=-lam_init, scalar2=None, op0=ALU.add), reads=[self.rlam], writes=[self.rlam])
            P.op("dve", lambda e: e.tensor_scalar(out=lam[:, 6:7], in0=vecs[:, V_SUBLN:V_SUBLN + 1], scalar1=1.0 - lam_init, scalar2=None, op0=ALU.mult),
                 reads=[self.rvecs], writes=[self.rlam])
            P.drain(); P.flush()
        groups = [[(g * 512, 512, 0) for g in range(4)], [(g * 512, 512, 0) for g in range(4, 8)] + [(NL, 256, 1)]]
        for grp in groups:
            g0 = grp[0][0]
            gn = sum(t[1] for t in grp)
            gl = sum(t[1] for t in grp if t[2] == 0)
            with ExitStack() as st:
                hg = self.sb(st, "a_h", [128, KC, 2304], BF16)
                rhg = [mkres(KC, "h") for _ in grp]
                xt = self.sb(st, "a_x", [128, KC, 512])
                rx = mkres(KC, "x")
                rp = self.sb(st, "a_rope", [128, 4, 2048])
                rrp = Res("rope")
                tmp = [self.sb(st, f"a_t{k}", [128, 512]) for k in range(4)]
                rtmp = mkres(4, "tmp")
                rstd = self.sb(st, "a_rstd", [128, 512])
                rrstd = Res("rstd")
                wsl = [self.sb(st, f"a_w{k}", [128, 1, KC * 128], BF16) for k in range(2)]
                abf = [self.sb(st, f"a_ab{k}", [128, 512], BF16) for k in range(2)]
                rabf = mkres(2, "abf")
                permf = self.sb(st, "a_permf", [128, 128])
                permb = self.sb(st, "a_permb", [128, 128], BF16)
                rperm = Res("perm")
                P.dma("sp", lambda e: e.dma_start(out=permf[:], in_=self.perm_in[:, :]), writes=[rperm])
                P.op("dve", lambda e: e.tensor_copy(out=permb[:], in_=permf[:]), reads=[rperm], writes=[rperm])
                rwsl2 = [mkres(2, "w") for _ in range(2)]
                wvs = [self.sb(st, f"a_wv{k}", [128, KC * 512], BF16) for k in range(1)]
                rwvs = mkres(1, "wv")
                ost = [self.sb(st, f"a_o{k}", [128, 512], BF16) for k in range(4)]
                rost = mkres(4, "ost")
                P.dma("sp", lambda e, g0=g0, gl=gl: e.dma_start(out=rp[:, :, 0:gl], in_=ropev[:, :, g0:g0 + gl]), writes=[rrp])
                for ti, (t0, n, s) in enumerate(grp):
                    off = t0 - g0
                    P.dma("sp", lambda e, t0=t0, n=n: e.dma_start(out=xt[:, :, 0:n], in_=XIv[:, :, t0:t0 + n]), reads=[rXI], writes=rx)
                    pst, rpst = self.ps[6 + ti % 2], self.rps[6 + ti % 2]
                    for c in range(KC):
                        tq, rtq = self.sqb[c % 2], self.rsqb[c % 2]
                        P.op("act", lambda e, tq=tq, c=c, n=n: e.activation(out=tq[:, 0:n], in_=xt[:, c, 0:n], func=AF.Square), reads=[rx[c]], writes=[rtq])
                        P.op("pe", lambda e, tq=tq, c=c, n=n, pst=pst: e.matmul(out=pst[:, 0:n], lhsT=self.onesb[:], rhs=tq[:, 0:n], start=(c == 0), stop=(c == KC - 1)),
                             reads=[rtq, self.rconst], writes=[rpst])
                    self.rstd_from_sumsq(pst, rpst, n, rstd, rrstd, tmp[2], rtmp[2], NORM_EPS)
                    for c in range(KC):
                        tq, rtq = tmp[2 + c % 2], rtmp[2 + c % 2]
                        P.op("dve", lambda e, tq=tq, c=c, n=n: e.tensor_tensor(out=tq[:, 0:n], in0=xt[:, c, 0:n], in1=rstd[:, 0:n], op=ALU.mult),
                             reads=[rx[c], rrstd], writes=[rtq])
                        P.op("act", lambda e, tq=tq, c=c, n=n, s=s, off=off: e.activation(out=hg[:, c, off:off + n], in_=tq[:, 0:n], func=AF.Identity,
                                                                                           scale=self.mv(i, s, j, 0, c), bias=self.mv(i, s, j, 1, c)),
                             reads=[rtq, self.rmodv], writes=[rhg[ti][c]])
                nw = 0
                no = 0
                k2 = 0
                pendq = None
                for which, (wa, dst, rdst, tc, tsn) in enumerate(((self.wk, self.KT, self.rKT, 2, 3), (self.wq, self.QT, self.rQT, 0, 1))):
                    wav = wa.rearrange("(o p) k -> o p k", p=128)
                    def issue_w(oc_, slot):
                        self.load_w_cast(wsl[slot][:, 0, :], rwsl2[slot][0], wav[oc_])
                    issue_w(0, nw % 2)
                    for oc in range(NHEAD):
                        wt, rwt2 = wsl[nw % 2], rwsl2[nw % 2]
                        nw += 1
                        if oc + 1 < NHEAD:
                            issue_w(oc + 1, nw % 2)
                        for ti, (t0, n, s) in enumerate(grp):
                            if which == 1 and s == 1:
                                continue
                            off = t0 - g0
                            pa, rpa = self.ps[k2 % 2], self.rps[k2 % 2]
                            pb, rpb = self.ps[2 + k2 % 2], self.rps[2 + k2 % 2]
                            ta, rta = tmp[k2 % 2], rtmp[k2 % 2]
                            tb, rtb = tmp[2 + k2 % 2], rtmp[2 + k2 % 2]
                            k2 += 1
                            o, ro = ost[no % 4], rost[no % 4]
                            no += 1
                            for kc in range(KC):
                                P.op("pe", lambda e, pa=pa, wt=wt, kc=kc, n=n, off=off: e.matmul(out=pa[:, 0:n], lhsT=wt[:, 0, kc * 128:(kc + 1) * 128], rhs=hg[:, kc, off:off + n],
                                                                                                  start=(kc == 0), stop=(kc == KC - 1)), reads=[rwt2[0], rhg[ti][kc]], writes=[rpa])
                            if s == 0:
                                ab, rab = abf[k2 % 2], rabf[k2 % 2]
                                P.op("act", lambda e, ab=ab, pa=pa, n=n: e.activation(out=ab[:, 0:n], in_=pa[:, 0:n], func=AF.Copy), reads=[rpa], writes=[rab])
                            else:
                                ab, rab = None, None
                            if pendq is not None:
                                pendq()

                            def post(s=s, ab=ab, rab=rab, pa=pa, rpa=rpa, pb=pb, rpb=rpb, ta=ta, rta=rta, tb=tb, rtb=rtb, o=o, ro=ro, n=n, off=off,
                                     tc=tc, tsn=tsn, dst=dst, rdst=rdst, oc=oc, t0=t0):
                                if s == 0:
                                    P.op("pe", lambda e: e.matmul(out=pb[:, 0:n], lhsT=permb[:], rhs=ab[:, 0:n], start=True, stop=True),
                                         reads=[rperm, rab], writes=[rpb])
                                    P.op("dve", lambda e: e.tensor_tensor(out=ta[:, 0:n], in0=pa[:, 0:n], in1=rp[:, tc, off:off + n], op=ALU.mult),
                                         reads=[rpa, rrp], writes=[rta])
                                    P.op("dve", lambda e: e.tensor_tensor(out=tb[:, 0:n], in0=pb[:, 0:n], in1=rp[:, tsn, off:off + n], op=ALU.mult),
                                         reads=[rpb, rrp], writes=[rtb])
                                    P.op("pool", lambda e: e.tensor_tensor(out=o[:, 0:n], in0=ta[:, 0:n], in1=tb[:, 0:n], op=ALU.add),
                                         reads=[rta, rtb], writes=[ro])
                                else:
                                    P.op("act", lambda e: e.activation(out=o[:, 0:n], in_=pa[:, 0:n], func=AF.Copy), reads=[rpa], writes=[ro])
                                P.dma("sp", lambda e: e.dma_start(out=dst[oc * 128:(oc + 1) * 128, t0:t0 + n], in_=o[:, 0:n]), reads=[ro], writes=[rdst])
                            pendq = post
                    if pendq is not None:
                        pendq()
                        pendq = None
                nblk = gn // 128
                for nb in range(4):
                    wt, rwt = wvs[0], rwvs[0]
                    self.load_w_cast(wt[:], rwt, wv_v[nb])
                    for tb in range(nblk):
                        ti = min(tb // 4, len(grp) - 1)
                        pv, rpv = self.ps[4 + tb % 2], self.rps[4 + tb % 2]
                        o, ro = ost[no % 4], rost[no % 4]
                        no += 1
                        for kc in range(KC):
                            P.op("pe", lambda e, pv=pv, wt=wt, kc=kc, tb=tb: e.matmul(out=pv[:, :], lhsT=hg[:, kc, tb * 128:(tb + 1) * 128], rhs=wt[:, kc * 512:(kc + 1) * 512],
                                                                                      start=(kc == 0), stop=(kc == KC - 1)), reads=[rwt, rhg[ti][kc]], writes=[rpv])
                        if tb % 2 == 0:
                            P.op("act", lambda e, o=o, pv=pv: e.activation(out=o[:, :], in_=pv[:, :], func=AF.Copy), reads=[rpv], writes=[ro])
                        else:
                            P.op("dve", lambda e, o=o, pv=pv: e.tensor_copy(out=o[:, :], in_=pv[:, :]), reads=[rpv], writes=[ro])
                        r0 = g0 + tb * 128
                        P.dma("sp", lambda e, o=o, r0=r0, nb=nb: e.dma_start(out=self.VV[r0:r0 + 128, nb * 512:(nb + 1) * 512], in_=o[:, :]), reads=[ro], writes=[self.rVV])
                P.drain(); P.flush()

    def attn_b(self):
        nc, P = self.nc, self.P
        NKB = NT // 128
        NQT = NL // 512
        VVv = self.VV.rearrange("(kb p) e -> p kb e", p=128)
        lam = self.lam
        with ExitStack() as st:
            kT = [self.sb(st, f"b_k{k}", [128, NT], BF16) for k in range(2)]
            qz = [[self.sb(st, f"b_q{c}{k}", [128, NL], BF16) for k in range(2)] for c in range(2)]
            rqz = Res("qz")
            for k in range(2):
                P.op("dve", lambda e, k=k: e.memset(qz[0][k][64:128, :], 0.0), writes=[rqz])
                P.op("dve", lambda e, k=k: e.memset(qz[1][k][0:64, :], 0.0), writes=[rqz])
            vh = [self.sb(st, f"b_v{k}", [128, NKB, 128], BF16) for k in range(2)]
            rkqv = mkres(2, "kqv")
            pt = [self.sb(st, f"b_p{k}", [128, 512], BF16) for k in range(8)]
            rpt = mkres(8, "p")
            ob = [self.sb(st, f"b_o{k}", [128, 512]) for k in range(2)]
            rob = mkres(2, "o")
            t2 = [self.sb(st, f"b_t{k}", [128, 512]) for k in range(3)]
            rt2 = mkres(3, "t")
            osb = [self.sb(st, f"b_ob{k}", [128, 512], BF16) for k in range(2)]
            rosb = mkres(2, "ob")
            zc = self.sb(st, "b_zc", [128, 512])
            rzc = Res("zc")
            zacc = [self.sb(st, f"b_zacc{k}", [128, 512]) for k in range(2)]
            rzacc = mkres(2, "zacc")
            onesb = self.sb(st, "b_ones", [128, 128], BF16)
            ronesb = Res("onesb")
            P.op("dve", lambda e: e.memset(onesb[:], 1.0), writes=[ronesb])
            ones = self.ones
            pS = [self.ps[0], self.ps[1]]
            rpS = [self.rps[0], self.rps[1]]
            pO = [self.ps[2], self.ps[3]]
            rpO = [self.rps[2], self.rps[3]]
            pZ = [self.ps[4], self.ps[5]]
            rpZ = [self.rps[4], self.rps[5]]
            pN = [self.ps[6], self.ps[7]]
            rpN = [self.rps[6], self.rps[7]]
            npt = 0
            tcount = 0
            pending = None

            def finalize(item):
                o, ro, hd, qt, k = item
                sq, rsq = t2[2], rt2[2]
                sb_, rsb_ = self.sqb[k], self.rsqb[k]
                P.op("dve", lambda e: e.tensor_tensor(out=sb_[:], in0=o[:], in1=o[:], op=ALU.mult), reads=[ro], writes=[rsb_])
                P.op("pe", lambda e: e.matmul(out=pN[k][:], lhsT=self.onesb[:], rhs=sb_[:], start=True, stop=True), reads=[rsb_, self.rconst], writes=[rpN[k]])
                P.op("act", lambda e: e.activation(out=sq[:], in_=pN[k][:], func=AF.Ln, bias=self.epsv(SUBLN_EPS), scale=1.0 / 128), reads=[rpN[k], self.rconst], writes=[rsq])
                P.op("act", lambda e: e.activation(out=sq[:], in_=sq[:], func=AF.Exp, scale=-0.5), reads=[rsq], writes=[rsq])
                P.op("dve", lambda e: e.tensor_tensor(out=o[:], in0=o[:], in1=sq[:], op=ALU.mult), reads=[ro, rsq], writes=[ro])
                ot, rot = osb[k], rosb[k]
                P.op("dve", lambda e: e.tensor_scalar(out=ot[:], in0=o[:], scalar1=lam[:, 6:7], scalar2=None, op0=ALU.mult), reads=[ro, self.rlam], writes=[rot])
                P.dma("sp", lambda e: e.dma_start(out=self.ON[hd * 128:(hd + 1) * 128, qt * 512:(qt + 1) * 512], in_=ot[:]), reads=[rot], writes=[self.rON])

            for hd in range(NHEAD):
                b = hd % 2
                self.run_jobs(self.bg_attnb, 10)
                P.dma("sp", lambda e, b=b, hd=hd: e.dma_start(out=kT[b][:], in_=self.KT[hd * 128:(hd + 1) * 128, :]), reads=[self.rKT], writes=[rkqv[b]])
                P.dma("sp", lambda e, b=b, hd=hd: e.dma_start(out=qz[0][b][0:64, :], in_=self.QT[hd * 128:hd * 128 + 64, :]), reads=[self.rQT], writes=[rkqv[b]])
                P.dma("sp", lambda e, b=b, hd=hd: e.dma_start(out=qz[1][b][64:128, :], in_=self.QT[hd * 128 + 64:(hd + 1) * 128, :]), reads=[self.rQT], writes=[rkqv[b]])
                P.dma("sp", lambda e, b=b, hd=hd: e.dma_start(out=vh[b][:], in_=VVv[:, :, hd * 128:(hd + 1) * 128]), reads=[self.rVV], writes=[rkqv[b]])
                for qt in range(NQT):
                    k = tcount % 2
                    tcount += 1
                    q0 = qt * 512
                    prev = None
                    for kb in range(NKB + 1):
                        if kb < NKB:
                            cur = []
                            for c in range(2):
                                p, rp_ = pt[npt % 8], rpt[npt % 8]
                                npt += 1
                                P.op("pe", lambda e, c=c, b=b, kb=kb, q0=q0: e.matmul(out=pS[c][:], lhsT=kT[b][:, kb * 128:(kb + 1) * 128],
                                                                                     rhs=qz[c][b][:, q0:q0 + 512], start=True, stop=True),
                                     reads=[rkqv[b], rqz], writes=[rpS[c]])
                                cur.append((p, rp_))
                            for c in range(2):
                                p, rp_ = cur[c]
                                P.op("act", lambda e, c=c, p=p: e.activation(out=p[:], in_=pS[c][:], func=AF.Exp), reads=[rpS[c]], writes=[rp_])
                        if prev is not None:
                            kp = kb - 1
                            for c in range(2):
                                p, rp_ = prev[c]
                                P.op("pe", lambda e, c=c, p=p, b=b, kp=kp: e.matmul(out=pO[c][:], lhsT=vh[b][:, kp, :], rhs=p[:], start=(kp == 0), stop=(kp == NKB - 1)),
                                     reads=[rkqv[b], rp_], writes=[rpO[c]])
                            p, rp_ = prev[1]
                            P.op("pe", lambda e, p=p, kp=kp: e.matmul(out=pZ[1][:], lhsT=onesb[:], rhs=p[:], start=(kp == 0), stop=(kp == NKB - 1)),
                                 reads=[ronesb, rp_], writes=[rpZ[1]])
                            p, rp_ = prev[0]
                            za, rza = zacc[kp % 2], rzacc[kp % 2]
                            if kp < 2:
                                P.op("dve", lambda e, p=p, za=za: e.tensor_copy(out=za[:], in_=p[:]), reads=[rp_], writes=[rza])
                            else:
                                P.op("dve", lambda e, p=p, za=za: e.tensor_tensor(out=za[:], in0=za[:], in1=p[:], op=ALU.add), reads=[rza, rp_], writes=[rza])
                        prev = cur if kb < NKB else None
                        if kb == 12 and pending is not None:
                            finalize(pending)
                            pending = None
                    o, ro = ob[k], rob[k]
                    P.op("dve", lambda e, o=o: e.tensor_copy(out=o[:], in_=pO[0][:]), reads=[rpO[0]], writes=[ro])
                    P.op("dve", lambda e: e.tensor_copy(out=t2[1][:], in_=pO[1][:]), reads=[rpO[1]], writes=[rt2[1]])
                    P.op("dve", lambda e: e.tensor_tensor(out=zacc[0][:], in0=zacc[0][:], in1=zacc[1][:], op=ALU.add), reads=[rzacc[0], rzacc[1]], writes=[rzacc[0]])
                    P.op("pe", lambda e: e.matmul(out=pZ[0][:], lhsT=ones[:], rhs=zacc[0][:], start=True, stop=True), reads=[rzacc[0], self.rconst], writes=[rpZ[0]])
                    P.op("dve", lambda e: e.tensor_copy(out=t2[0][:], in_=pZ[0][:]), reads=[rpZ[0]], writes=[rt2[0]])
                    P.op("dve", lambda e: e.tensor_copy(out=zc[:], in_=pZ[1][:]), reads=[rpZ[1]], writes=[rzc])
                    P.op("dve", lambda e: e.reciprocal(out=t2[0][:], in_=t2[0][:]), reads=[rt2[0]], writes=[rt2[0]])
                    P.op("dve", lambda e: e.reciprocal(out=zc[:], in_=zc[:]), reads=[rzc], writes=[rzc])
                    P.op("dve", lambda e, o=o: e.tensor_tensor(out=o[:], in0=o[:], in1=t2[0][:], op=ALU.mult), reads=[ro, rt2[0]], writes=[ro])
                    P.op("dve", lambda e: e.tensor_tensor(out=t2[1][:], in0=t2[1][:], in1=zc[:], op=ALU.mult), reads=[rt2[1], rzc], writes=[rt2[1]])
                    P.op("dve", lambda e, o=o: e.scalar_tensor_tensor(out=o[:], in0=t2[1][:], scalar=lam[:, 5:6], in1=o[:], op0=ALU.mult, op1=ALU.add),
                         reads=[rt2[1], ro, self.rlam], writes=[ro])
                    pending = (o, ro, hd, qt, k)
            finalize(pending)
            self.run_jobs(self.bg_attnb, len(self.bg_attnb))

    def attn_c(self, XI, rXI, XO, rXO):
        nc, P = self.nc, self.P
        i, j = 1, 1
        XIv = XI.rearrange("(c p) t -> p c t", p=128)
        XOv = XO.rearrange("(c p) t -> p c t", p=128)
        ONv = self.ON.rearrange("(c p) t -> p c t", p=128)
        wov = self.wo.rearrange("(o p) k -> o p k", p=128)
        ones = self.ones
        with ExitStack() as st:
            woa = self.sb(st, "c3_wo", [128, KC, KC * 128], BF16)
            rwoa = Res("wo")
            xt = self.sb(st, "c3_x", [128, KC, 512])
            rx = mkres(KC, "x")
            on = [self.sb(st, f"c3_on{k}", [128, KC, 512], BF16) for k in range(2)]
            ron = mkres(2, "on")
            y = self.sb(st, "c3_y", [128, KC, 512])
            ry = mkres(KC, "y")
            tmp = [self.sb(st, f"c3_t{k}", [128, 512]) for k in range(4)]
            rtmp = mkres(4, "tmp")
            rstd2 = self.sb(st, "c3_rstd2", [128, 512])
            rrstd2 = Res("rstd2")
            rwo_l = mkres(KC, "wo")
            for dc in range(KC):
                self.load_w_cast(woa[:, dc, :], rwo_l[dc], wov[dc])
            s = 0
            for g in range(8):
                t0, n = g * 512, 512
                ont, ront = on[g % 2], ron[g % 2]
                P.dma("sp", lambda e, ont=ont, t0=t0: e.dma_start(out=ont[:], in_=ONv[:, :, t0:t0 + 512]), reads=[self.rON], writes=[ront])
                P.dma("sp", lambda e, t0=t0: e.dma_start(out=xt[:], in_=XIv[:, :, t0:t0 + 512]), reads=[rXI], writes=rx)
                pst2, rpst2 = self.ps[6], self.rps[6]
                pend = []
                for dc in range(KC):
                    py, rpy = self.ps[4 + dc % 2], self.rps[4 + dc % 2]
                    for kc in range(KC):
                        P.op("pe", lambda e, py=py, dc=dc, kc=kc, ont=ont: e.matmul(out=py[:], lhsT=woa[:, dc, kc * 128:(kc + 1) * 128], rhs=ont[:, kc, :],
                                                                                   start=(kc == 0), stop=(kc == KC - 1)), reads=[rwo_l[dc], ront], writes=[rpy])
                    tq, rtq = self.sqb[dc % 2], self.rsqb[dc % 2]
                    P.op("dve", lambda e, py=py, dc=dc: e.tensor_copy(out=y[:, dc, :], in_=py[:]), reads=[rpy], writes=[ry[dc]])
                    P.op("act", lambda e, tq=tq, dc=dc: e.activation(out=tq[:], in_=y[:, dc, :], func=AF.Square), reads=[ry[dc]], writes=[rtq])
                    pend.append((tq, rtq, dc))
                    if len(pend) > 1:
                        self._stat_mm(pend.pop(0), pst2, rpst2, n)
                self._stat_mm(pend.pop(0), pst2, rpst2, n)
                self.rstd_from_sumsq(pst2, rpst2, n, rstd2, rrstd2, tmp[0], rtmp[0], NORM_EPS)
                for c in range(KC):
                    tq, rtq = tmp[c % 2], rtmp[c % 2]
                    P.op("dve", lambda e, tq=tq, c=c: e.tensor_tensor(out=tq[:], in0=y[:, c, :], in1=rstd2[:], op=ALU.mult), reads=[ry[c], rrstd2], writes=[rtq])
                    P.op("dve", lambda e, tq=tq, c=c: e.scalar_tensor_tensor(out=y[:, c, :], in0=tq[:], scalar=self.mv(i, s, j, 2, c), in1=xt[:, c, :],
                                                                              op0=ALU.mult, op1=ALU.add), reads=[rtq, rx[c], self.rmodv], writes=[ry[c]])
                P.dma("sp", lambda e, t0=t0: e.dma_start(out=XOv[:, :, t0:t0 + 512], in_=y[:]), reads=ry, writes=[rXO])


def _tile_rows(w, ncol_chunk=128):
    K, N = w.shape
    kc, oc = K // 128, N // 128
    return np.ascontiguousarray(w.reshape(kc, 128, oc, 128).transpose(2, 1, 0, 3).reshape(oc * 128, kc * 128))


def _pvec(v):
    return np.ascontiguousarray(v.reshape(-1, 128).T)


def prep_shared(inp):
    sh = {}
    sh["ident"] = np.eye(128, dtype=np.float32)
    aw = inp["ada_w"]
    sh["adaw"] = np.concatenate([_tile_rows(aw[i]) for i in range(2)], axis=0)
    for i in range(2):
        for w, (ki, ko) in ((1, ("ffn1_w_in", "ffn1_w_out")), (2, ("ffn2_w_in", "ffn2_w_out"))):
            wi = inp[ki][i]
            a = _tile_rows(wi[:, :FF]).reshape(FC, 128, KC * 128)
            u = _tile_rows(wi[:, FF:]).reshape(FC, 128, KC * 128)
            sh[f"win{i}{w}"] = np.ascontiguousarray(np.concatenate([a, u], axis=2).reshape(FC * 128, 2 * KC * 128))
            sh[f"wout{i}{w}"] = _tile_rows(inp[ko][i])
    pw1 = inp["conv_w_pw1"][0]
    a = _tile_rows(pw1[:, :D]).reshape(KC, 128, KC * 128)
    g = _tile_rows(pw1[:, D:]).reshape(KC, 128, KC * 128)
    sh["pw1"] = np.ascontiguousarray(np.concatenate([a, g], axis=2).reshape(KC * 128, 2 * KC * 128))
    sh["pw2"] = _tile_rows(inp["conv_w_pw2"][0])
    wqkv = inp["attn_w_qkv"][0]
    wq, wk, wv = wqkv[:, :D], wqkv[:, D:2 * D], wqkv[:, 2 * D:]
    r = np.arange(D)
    i32 = r % 32
    partner = np.where(i32 < 16, r + 16, r - 16)
    sh["wq"] = _tile_rows(wq)
    sh["wk"] = _tile_rows(wk)
    pm = np.zeros((128, 128), np.float32)
    m = np.arange(128)
    pm[np.where((m % 32) < 16, m + 16, m - 16), m] = 1.0
    sh["perm"] = pm
    sh["wv"] = np.ascontiguousarray(wv.reshape(KC, 128, 4, 512).transpose(2, 1, 0, 3).reshape(4 * 128, KC * 512))
    sh["wo"] = _tile_rows(inp["attn_w_o"][0])
    t = np.arange(NL)
    inv_freq = (10000.0 ** (-np.arange(0, 32, 2, dtype=np.float32) / 32)).astype(np.float32)
    rr = np.arange(128)
    jj = rr % 64
    pos = np.where((jj < 32)[:, None], (t // 64)[None, :], (t % 64)[None, :]).astype(np.float32)
    fr = inv_freq[(jj % 32) % 16][:, None]
    ang = (pos * fr).astype(np.float32)
    cs = np.cos(ang).astype(np.float32)
    sn = np.sin(ang).astype(np.float32)
    sgn = np.where(((jj % 32) < 16)[:, None], -1.0, 1.0).astype(np.float32)
    sn = sn * sgn
    sh["rope"] = np.ascontiguousarray(np.concatenate([cs * 0.125, sn * 0.125, cs, sn], axis=0).astype(np.float32))
    return sh


def prep_vecs(inp, b):
    v = np.zeros((128, NV), np.float32)
    cc = np.stack([inp["c"][b], inp["c_ctx"]], axis=0)
    v[:, V_CC:V_CC + 32] = cc.reshape(2, KC, 128).transpose(2, 1, 0).reshape(128, 32)
    v[:, V_ADAB:V_ADAB + 288] = inp["ada_b"].reshape(2, 144, 128).transpose(2, 0, 1).reshape(128, 288)
    v[:, V_NPRE:V_NPRE + 96] = inp["norm_pre"].reshape(2, 3, KC, 128).transpose(3, 0, 1, 2).reshape(128, 96)
    v[:, V_NPOST:V_NPOST + 96] = inp["norm_post"].reshape(2, 3, KC, 128).transpose(3, 0, 1, 2).reshape(128, 96)
    v[:, V_BPW1:V_BPW1 + 32] = inp["conv_b_pw1"][0].reshape(2, KC, 128).transpose(2, 0, 1).reshape(128, 32)
    v[:, V_WDW:V_WDW + 496] = inp["conv_w_dw"][0].reshape(CW, KC, 128).transpose(2, 1, 0).reshape(128, 496)
    v[:, V_BDW:V_BDW + 16] = _pvec(inp["conv_b_dw"][0])
    v[:, V_LNG:V_LNG + 16] = _pvec(inp["conv_ln_g"][0])
    v[:, V_LNB:V_LNB + 16] = _pvec(inp["conv_ln_b"][0])
    v[:, V_BPW2:V_BPW2 + 16] = _pvec(inp["conv_b_pw2"][0])
    lam = np.concatenate([inp["attn_lambda_q1"][0], inp["attn_lambda_k1"][0], inp["attn_lambda_q2"][0], inp["attn_lambda_k2"][0]])
    v[:, V_LAM:V_LAM + 256] = lam[None, :]
    v[:, V_SUBLN] = inp["attn_subln_g"][0]
    return v


_CACHE = {}


def run(inp, phases=None, dbg=None, cores=NCORES, trace=False):
    key = (None if phases is None else tuple(phases), dbg)
    if key not in _CACHE:
        b = Builder(phases, dbg)
        nc = b.build()
        _CACHE[key] = (nc, b.in_names)
    nc, in_names = _CACHE[key]
    sh = prep_shared(inp)
    in_maps = []
    for b in range(cores):
        m = dict(sh)
        m["x"] = np.ascontiguousarray(inp["x"][b])
        m["ctx"] = np.ascontiguousarray(inp["ctx"][b])
        m["vecs"] = prep_vecs(inp, b)
        in_maps.append({k: m[k] for k in in_names})
    return run_bass_kernel_spmd(nc, in_maps, core_ids=list(range(cores)), trace=trace)


def kernel(**inputs):
    inp = {k: np.asarray(v) for k, v in inputs.items()}
    res = run(inp)
    return np.stack([res.results[b]["out"] for b in range(NCORES)], axis=0)
```
